# Optimizing a Trainium2 kernel written in Bass

```python
import math
import jax, jax.numpy as jnp
from jax import lax
import numpy as np

D_MODEL = 1024
BATCH = 16
SEQ = 4096
DEPTH = 4

GRID_W = 64
CTX_LEN = 256
EPS = 1e-6
CONV_CH = 512
CONV_WIDTH = 31
N_HEADS = 8
HEAD_DIM = 64
V_DIM = 2 * HEAD_DIM
QK_W = N_HEADS * 2 * HEAD_DIM
ATTN_W = N_HEADS * V_DIM
Q_BLOCK = 128
ROPE_BASE = 10000.0
POOL_CH = 512
POOL_WINDOWS = (2, 4, 8, 16)
POOL_GROUP = POOL_CH // len(POOL_WINDOWS)
N_BRANCH = 3
D_FF = -(-(8 * D_MODEL) // (3 * 256)) * 256
N_MOD = 6
COL_A = 0
COL_Q = COL_A + 2 * CONV_CH
COL_K = COL_Q + QK_W
COL_V = COL_K + QK_W
COL_P = COL_V + ATTN_W
COL_G = COL_P + POOL_CH
IN_W = COL_G + N_BRANCH * D_MODEL

kernel_name = "hybrid_conv_diffattn_pool_dit_block"


def rms_norm(x, g):
    x32 = x.astype(jnp.float32)
    y = x32 * lax.rsqrt(jnp.mean(x32 * x32, axis=-1, keepdims=True) + EPS)
    return (y * g.astype(jnp.float32)).astype(x.dtype)


def layer_norm(x, g, b):
    x32 = x.astype(jnp.float32)
    mu = jnp.mean(x32, axis=-1, keepdims=True)
    var = jnp.mean(jnp.square(x32 - mu), axis=-1, keepdims=True)
    y = (x32 - mu) * lax.rsqrt(var + EPS) * g.astype(jnp.float32) + b.astype(jnp.float32)
    return y.astype(x.dtype)


def axial_rope_tables(n_tokens):
    rows = n_tokens // GRID_W
    row = jnp.repeat(jnp.arange(rows), GRID_W).astype(jnp.float32)
    col = jnp.tile(jnp.arange(GRID_W), rows).astype(jnp.float32)
    half = HEAD_DIM // 2
    inv = ROPE_BASE ** (-jnp.arange(0, half, 2, dtype=jnp.float32) / half)
    ang_r = row[:, None] * inv[None, :]
    ang_c = col[:, None] * inv[None, :]
    return (jnp.cos(ang_r), jnp.sin(ang_r), jnp.cos(ang_c), jnp.sin(ang_c))


def _rotate(x, cos, sin):
    n = x.shape[-1] // 2
    x1, x2 = x[..., :n], x[..., n:]
    cos = cos.astype(x.dtype)
    sin = sin.astype(x.dtype)
    return jnp.concatenate([x1 * cos - x2 * sin, x2 * cos + x1 * sin], axis=-1)


def apply_axial_rope(x, rope):
    cos_r, sin_r, cos_c, sin_c = rope
    half = HEAD_DIM // 2
    return jnp.concatenate([_rotate(x[..., :half], cos_r, sin_r),
                            _rotate(x[..., half:], cos_c, sin_c)], axis=-1)


def depthwise_conv(u, w, b):
    k, ch = w.shape
    out = lax.conv_general_dilated(u, w[:, None, :].astype(u.dtype), window_strides=(1,),
                                   padding=[(k // 2, k // 2)],
                                   dimension_numbers=('NWC', 'WIO', 'NWC'),
                                   feature_group_count=ch)
    return out + b.astype(u.dtype)


def conformer_conv(h, w_glu, conv_w, conv_b, ln_g, ln_b, w_proj):
    u = h @ w_glu
    u = u[..., :CONV_CH] * jax.nn.sigmoid(u[..., CONV_CH:])
    u = depthwise_conv(u, conv_w, conv_b)
    u = layer_norm(u, ln_g, ln_b)
    return jax.nn.silu(u) @ w_proj


def multiscale_pool(h, w_p, w_group, scale, w_proj):
    p = h @ w_p
    b_, l_ = p.shape[0], p.shape[1]
    cs = jnp.cumsum(p.astype(jnp.float32), axis=1)
    cs = jnp.pad(cs, ((0, 0), (1, 0), (0, 0)))
    t = jnp.arange(l_)
    outs = []
    for gi, w in enumerate(POOL_WINDOWS):
        lo = jnp.clip(t - w // 2, 0, l_)
        hi = jnp.clip(t + w // 2, 0, l_)
        csg = cs[..., gi * POOL_GROUP:(gi + 1) * POOL_GROUP]
        s = jnp.take(csg, hi, axis=1) - jnp.take(csg, lo, axis=1)
        outs.append(s / (hi - lo).astype(jnp.float32)[None, :, None])
    pooled = jnp.concatenate(outs, axis=-1).astype(p.dtype) - p
    pooled = pooled.reshape(b_, l_, len(POOL_WINDOWS), POOL_GROUP)
    mixed = jnp.einsum('blgc,gcd->blgd', pooled, w_group).reshape(b_, l_, POOL_CH)
    return (mixed * scale) @ w_proj


def diff_qkv(h, w_q, w_k, w_v):
    b_, l_ = h.shape[0], h.shape[1]
    q = (h @ w_q).reshape(b_, l_, N_HEADS, 2, HEAD_DIM).transpose(0, 2, 3, 1, 4)
    k = (h @ w_k).reshape(b_, l_, N_HEADS, 2, HEAD_DIM).transpose(0, 2, 3, 1, 4)
    v = (h @ w_v).reshape(b_, l_, N_HEADS, V_DIM).transpose(0, 2, 1, 3)
    return q, k, v


def diff_attend(q, k, v, lam):
    s = jnp.einsum('bhiqd,bhikd->bhiqk', q, k).astype(jnp.float32) * (HEAD_DIM ** -0.5)
    p = jax.nn.softmax(s, axis=-1)
    a = (p[:, :, 0] - lam * p[:, :, 1]).astype(v.dtype)
    return jnp.einsum('bhqk,bhkd->bhqd', a, v)


def diff_finish(o, subln_g, lam_init, w_proj):
    o = rms_norm(o, subln_g) * (1.0 - lam_init)
    b_, h_, l_, _ = o.shape
    return o.transpose(0, 2, 1, 3).reshape(b_, l_, ATTN_W) @ w_proj


def gated_merge(h, y_a, y_b, y_c, w_g, b_gate, w_out):
    g = jax.nn.sigmoid(h @ w_g + b_gate)
    m = (g[..., :D_MODEL] * y_a + g[..., D_MODEL:2 * D_MODEL] * y_b
         + g[..., 2 * D_MODEL:] * y_c)
    return m @ w_out


def token_mixer(h, hc, rope, need_ctx, lam_init, w_in, b_gate, conv_w, conv_b, ln_g, ln_b,
                w_conv_out, lam_q1, lam_k1, lam_q2, lam_k2, subln_g, w_attn_out,
                w_pool_group, pool_scale, w_pool_out, w_out):
    w_glu = w_in[:, COL_A:COL_Q]
    w_q = w_in[:, COL_Q:COL_K]
    w_k = w_in[:, COL_K:COL_V]
    w_v = w_in[:, COL_V:COL_P]
    w_p = w_in[:, COL_P:COL_G]
    w_g = w_in[:, COL_G:IN_W]
    lam = (jnp.exp(jnp.sum(lam_q1.astype(jnp.float32) * lam_k1.astype(jnp.float32)))
           - jnp.exp(jnp.sum(lam_q2.astype(jnp.float32) * lam_k2.astype(jnp.float32)))
           + lam_init)

    qc, kc, vc = diff_qkv(hc, w_q, w_k, w_v)
    q, k, v = diff_qkv(h, w_q, w_k, w_v)
    q = apply_axial_rope(q, rope)
    k = apply_axial_rope(k, rope)
    k_all = jnp.concatenate([kc, k], axis=3)
    v_all = jnp.concatenate([vc, v], axis=2)
    b_, h_, _, l_, _ = q.shape
    nb = l_ // Q_BLOCK
    qb = jnp.moveaxis(q.reshape(b_, h_, 2, nb, Q_BLOCK, HEAD_DIM), 3, 0)
    ob = lax.map(lambda qq: diff_attend(qq, k_all, v_all, lam), qb)
    o = jnp.moveaxis(ob, 0, 2).reshape(b_, h_, l_, V_DIM)

    y_a = conformer_conv(h, w_glu, conv_w, conv_b, ln_g, ln_b, w_conv_out)
    y_b = diff_finish(o, subln_g, lam_init, w_attn_out)
    y_c = multiscale_pool(h, w_p, w_pool_group, pool_scale, w_pool_out)
    y = gated_merge(h, y_a, y_b, y_c, w_g, b_gate, w_out)
    if not need_ctx:
        return y, None
    oc = diff_attend(qc, kc, vc, lam)
    yc_a = conformer_conv(hc, w_glu, conv_w, conv_b, ln_g, ln_b, w_conv_out)
    yc_b = diff_finish(oc, subln_g, lam_init, w_attn_out)
    yc_c = multiscale_pool(hc, w_p, w_pool_group, pool_scale, w_pool_out)
    yc = gated_merge(hc, yc_a, yc_b, yc_c, w_g, b_gate, w_out)
    return y, yc


def swiglu(h, w_ffn_in, w_ffn_out):
    u = h @ w_ffn_in
    return (jax.nn.silu(u[..., :D_FF]) * u[..., D_FF:]) @ w_ffn_out


def setup_inputs(seed: int = 0) -> dict:
    key = jax.random.key(seed)
    ks = jax.random.split(key, 32)
    f32 = jnp.float32
    nrm = lambda k, shape, s: jax.random.normal(k, shape, f32) * s
    gain = lambda k, shape: 1.0 + 0.02 * jax.random.normal(k, shape, f32)
    L = DEPTH
    return {
        "x": nrm(ks[0], (BATCH, SEQ, D_MODEL), 1.0),
        "c": nrm(ks[1], (BATCH, D_MODEL), 1.0),
        "ctx": nrm(ks[2], (BATCH, CTX_LEN, D_MODEL), 1.0),
        "c_ctx": nrm(ks[3], (D_MODEL,), 1.0),
        "w_mod": nrm(ks[4], (L, D_MODEL, N_MOD * D_MODEL), 0.5 * D_MODEL ** -0.5),
        "b_mod": nrm(ks[5], (L, N_MOD * D_MODEL), 0.02),
        "g_pre_mix": gain(ks[6], (L, D_MODEL)),
        "g_post_mix": gain(ks[7], (L, D_MODEL)),
        "w_in": nrm(ks[8], (L, D_MODEL, IN_W), D_MODEL ** -0.5),
        "b_gate": nrm(ks[9], (L, N_BRANCH * D_MODEL), 0.02),
        "conv_w": nrm(ks[10], (L, CONV_WIDTH, CONV_CH), CONV_WIDTH ** -0.5),
        "conv_b": nrm(ks[11], (L, CONV_CH), 0.02),
        "conv_ln_g": gain(ks[12], (L, CONV_CH)),
        "conv_ln_b": nrm(ks[13], (L, CONV_CH), 0.02),
        "w_conv_out": nrm(ks[14], (L, CONV_CH, D_MODEL), CONV_CH ** -0.5),
        "lam_q1": nrm(ks[15], (L, HEAD_DIM), 0.1),
        "lam_k1": nrm(ks[16], (L, HEAD_DIM), 0.1),
        "lam_q2": nrm(ks[17], (L, HEAD_DIM), 0.1),
        "lam_k2": nrm(ks[18], (L, HEAD_DIM), 0.1),
        "subln_g": gain(ks[19], (L, V_DIM)),
        "w_attn_out": nrm(ks[20], (L, ATTN_W, D_MODEL), ATTN_W ** -0.5),
        "w_pool_group": nrm(ks[21], (L, len(POOL_WINDOWS), POOL_GROUP, POOL_GROUP), POOL_GROUP ** -0.5),
        "pool_scale": 1.0 + 0.1 * jax.random.normal(ks[22], (L, POOL_CH), f32),
        "w_pool_out": nrm(ks[23], (L, POOL_CH, D_MODEL), POOL_CH ** -0.5),
        "w_out": nrm(ks[24], (L, D_MODEL, D_MODEL), D_MODEL ** -0.5),
        "g_pre_ffn": gain(ks[25], (L, D_MODEL)),
        "g_post_ffn": gain(ks[26], (L, D_MODEL)),
        "w_ffn_in": nrm(ks[27], (L, D_MODEL, 2 * D_FF), D_MODEL ** -0.5),
        "w_ffn_out": nrm(ks[28], (L, D_FF, D_MODEL), D_FF ** -0.5),
    }


def reference(x, c, ctx, c_ctx, w_mod, b_mod, g_pre_mix, g_post_mix, w_in, b_gate, conv_w,
              conv_b, conv_ln_g, conv_ln_b, w_conv_out, lam_q1, lam_k1, lam_q2, lam_k2,
              subln_g, w_attn_out, w_pool_group, pool_scale, w_pool_out, w_out, g_pre_ffn,
              g_post_ffn, w_ffn_in, w_ffn_out):
    rope = axial_rope_tables(x.shape[1])
    xl, xc = x, ctx
    for l in range(DEPTH):
        need_ctx = l < DEPTH - 1
        lam_init = 0.8 - 0.6 * math.exp(-0.3 * l)
        mod = jax.nn.silu(c) @ w_mod[l] + b_mod[l]
        modc = jax.nn.silu(c_ctx) @ w_mod[l] + b_mod[l]
        sh1, sc1, gt1, sh2, sc2, gt2 = jnp.split(mod[:, None, :], N_MOD, axis=-1)
        csh1, csc1, cgt1, csh2, csc2, cgt2 = jnp.split(modc, N_MOD, axis=-1)

        h = rms_norm(xl, g_pre_mix[l]) * (1.0 + sc1) + sh1
        hc = rms_norm(xc, g_pre_mix[l]) * (1.0 + csc1) + csh1
        y, yc = token_mixer(h, hc, rope, need_ctx, lam_init, w_in[l], b_gate[l], conv_w[l],
                            conv_b[l], conv_ln_g[l], conv_ln_b[l], w_conv_out[l], lam_q1[l],
                            lam_k1[l], lam_q2[l], lam_k2[l], subln_g[l], w_attn_out[l],
                            w_pool_group[l], pool_scale[l], w_pool_out[l], w_out[l])
        xl = xl + gt1 * rms_norm(y, g_post_mix[l])

        h = rms_norm(xl, g_pre_ffn[l]) * (1.0 + sc2) + sh2
        xl = xl + gt2 * rms_norm(swiglu(h, w_ffn_in[l], w_ffn_out[l]), g_post_ffn[l])

        if need_ctx:
            xc = xc + cgt1 * rms_norm(yc, g_post_mix[l])
            hc = rms_norm(xc, g_pre_ffn[l]) * (1.0 + csc2) + csh2
            xc = xc + cgt2 * rms_norm(swiglu(hc, w_ffn_in[l], w_ffn_out[l]), g_post_ffn[l])
    return xl
```

```python
import math
from contextlib import ExitStack

import numpy as np
import concourse.bass as bass
import concourse.mybir as mybir
from concourse.bass_utils import run_bass_kernel_spmd

F32 = mybir.dt.float32
BF16 = mybir.dt.bfloat16
AF = mybir.ActivationFunctionType
ALU = mybir.AluOpType

D = 1024
KC = 8
GRID_W = 64
EPS = 1e-6
CONV_CH = 512
CONV_W = 31
NH = 8
HD = 64
D_FF = 2816
NJ = D_FF // 128
COL_A = 0
COL_Q = 1024
COL_K = 2048
COL_V = 3072
COL_P = 4096
COL_G = 4608
IN_W = 7680
TT = 512

NA = 28 * 1024 + 2 * 4096
NBK = 3 * 1024 + 512 + 1024 + 512
NBW = 512 + 8 * NBK + 8 * 1024 + 44 * 1024 + 8 * D_FF
NMW = 48 * 1024
WBUF = 5120

PV_G = 0
PV_BMOD = 32
PV_BGATE = 80
PV_CONVW = 104
PV_CONVB = 228
PV_LNG = 232
PV_LNB = 236
PV_PSC = 240
PV_SUBG = 244
PV_LAM = 245
PV_L = 501
PV_CF = 0
PV_CL = 32
PV_GLOB = 64


class Op:
    __slots__ = ("eng", "fn", "dma", "deps", "sig", "sem", "val", "waits", "pre", "tag")

    def __init__(self, eng, fn, dma):
        self.eng = eng
        self.fn = fn
        self.dma = dma
        self.deps = ()
        self.sig = dma
        self.sem = None
        self.val = 0
        self.waits = ()
        self.pre = None


ENGS = ("pe", "act", "dve", "pool", "sp")
import os as _os
EPOCH = 20000
DMA_NS = 8
DMA_MAXV = 30000
TRACE_TAGS = False
ZSPLIT = int(_os.environ.get('KZSPLIT', '512'))
USE_SILU = _os.environ.get('KSILU', '1') == '1'
CONV_DVE_TAPS = int(_os.environ.get('KCTAPS', '31'))
SAME_ENGINE_SYNC = _os.environ.get('KSES', '1') == '1'


class Cx:
    def __init__(self, nc, stack):
        self.nc = nc
        self.stack = stack
        self.ops = []
        self.sec = ""
        self.waitinfo = {}
        self.lastw = {}
        self.readers = {}
        self.nsem = 0
        self.finals = []

    def new_sem(self):
        self.nsem += 1
        return self.stack.enter_context(self.nc.semaphore("s%d" % self.nsem))

    def add(self, eng, fn, r=(), w=(), dma=False):
        op = Op(eng, fn, dma)
        op.tag = self.sec
        deps = {}
        lastw = self.lastw
        readers = self.readers
        for k in r:
            d = lastw.get(k)
            if d is not None:
                deps[id(d)] = d
            readers.setdefault(k, []).append(op)
        for k in w:
            d = lastw.get(k)
            if d is not None:
                deps[id(d)] = d
            rl = readers.get(k)
            if rl:
                for d in rl:
                    if d is not op:
                        deps[id(d)] = d
            lastw[k] = op
            readers[k] = []
        op.deps = tuple(deps.values())
        self.ops.append(op)
        return op

    def mm(self, out, lhsT, rhs, start, stop, r, w, tp=None):
        if tp is None:
            fn = lambda e: e.matmul(out, lhsT, rhs, start=start, stop=stop)
        else:
            fn = lambda e: e.matmul(out, lhsT, rhs, start=start, stop=stop, tile_position=tp)
        return self.add("pe", fn, r, w)

    def act(self, out, in_, func, r, w, bias=None, scale=None):
        kw = {}
        if bias is not None:
            kw["bias"] = bias
        if scale is not None:
            kw["scale"] = scale
        return self.add("act", lambda e: e.activation(out=out, in_=in_, func=func, **kw), r, w)

    def tt(self, out, in0, in1, op, r, w, eng="dve"):
        return self.add(eng, lambda e: e.tensor_tensor(out=out, in0=in0, in1=in1, op=op), r, w)

    def ts(self, out, in0, s1, s2, op0, op1, r, w, eng="dve"):
        if s2 is None:
            fn = lambda e: e.tensor_scalar(out=out, in0=in0, scalar1=s1, scalar2=None, op0=op0)
        else:
            fn = lambda e: e.tensor_scalar(out=out, in0=in0, scalar1=s1, scalar2=s2, op0=op0, op1=op1)
        return self.add(eng, fn, r, w)

    def stt(self, out, in0, scalar, in1, op0, op1, r, w, eng="dve"):
        return self.add(eng, lambda e: e.scalar_tensor_tensor(out=out, in0=in0, scalar=scalar, in1=in1,
                                                              op0=op0, op1=op1), r, w)

    def recip(self, out, in_, r, w):
        return self.add("dve", lambda e: e.reciprocal(out=out, in_=in_), r, w)

    def copy(self, out, in_, r, w, eng="dve"):
        return self.add(eng, lambda e: e.tensor_copy(out=out, in_=in_), r, w)

    def memset(self, ap, val, w, eng="dve"):
        return self.add(eng, lambda e: e.memset(ap, val), (), w)

    def load(self, out, in_, r, w):
        return self.add("sp", lambda e: e.dma_start(out=out, in_=in_), r, w, dma=True)

    def store(self, out, in_, r, w):
        return self.add("pool", lambda e: e.dma_start(out=out, in_=in_), r, w, dma=True)

    def finalize(self):
        ops = self.ops
        def needs(op, d):
            if d.dma or op.dma or d.eng != op.eng:
                return True
            return SAME_ENGINE_SYNC and op.eng != "pe"

        for op in ops:
            for d in op.deps:
                if needs(op, d):
                    d.sig = True
        cnt = {e: 0 for e in ENGS}
        csem = {e: None for e in ENGS}
        dpool = {e: [[self.new_sem(), 0] for _ in range(DMA_NS)] for e in ("sp", "pool")}
        dn = {"sp": 0, "pool": 0}
        for op in ops:
            if op.dma:
                pool = dpool[op.eng]
                j = dn[op.eng] % DMA_NS
                dn[op.eng] += 1
                ent = pool[j]
                if ent[1] > 0:
                    op.pre = (ent[0], ent[1])
                op.sem = ent[0]
                op.val = ent[1] + 16
                ent[1] = op.val
                if ent[1] > DMA_MAXV:
                    pool[j] = [self.new_sem(), 0]
            elif op.sig:
                e = op.eng
                if csem[e] is None or cnt[e] >= EPOCH:
                    csem[e] = self.new_sem()
                    cnt[e] = 0
                cnt[e] += 1
                op.sem = csem[e]
                op.val = cnt[e]
        waited = {e: {} for e in ENGS}
        nw = 0
        for op in ops:
            need = {}
            if op.pre is not None:
                need[id(op.pre[0])] = [op.pre[0], op.pre[1], None]
            for d in op.deps:
                if needs(op, d):
                    ent = need.get(id(d.sem))
                    if ent is None:
                        need[id(d.sem)] = [d.sem, d.val, d]
                    elif d.val > ent[1]:
                        ent[1] = d.val
                        ent[2] = d
            wl = []
            wd = waited[op.eng]
            for k, (s, v, dsrc) in need.items():
                if wd.get(k, 0) < v:
                    wd[k] = v
                    wl.append((s, v, dsrc))
            op.waits = wl
            nw += len(wl)
        self.n_waits = nw

    def emit(self, block):
        per = {e: [] for e in ENGS}
        for op in self.ops:
            per[op.eng].append(op)
        finals = self.finals

        def run(e, name):
            for op in per[name]:
                for (s, v, dsrc) in op.waits:
                    wi = e.wait_ge(s, v)
                    if TRACE_TAGS:
                        try:
                            self.waitinfo[wi.ins.name] = (op.tag, dsrc.tag + "@" + dsrc.eng if dsrc is not None else "dma-sem")
                        except Exception:
                            pass
                ins = op.fn(e)
                if TRACE_TAGS:
                    try:
                        self.waitinfo[ins.ins.name] = (op.tag, "op")
                    except Exception:
                        pass
                if op.sig:
                    ins.then_inc(op.sem, 16 if op.dma else 1)
            if name == "pool":
                for op in finals:
                    e.wait_ge(op.sem, op.val)

        @block.sync
        def _(e):
            run(e, "sp")

        @block.gpsimd
        def _(e):
            run(e, "pool")

        @block.vector
        def _(e):
            run(e, "dve")

        @block.scalar
        def _(e):
            run(e, "act")

        @block.tensor
        def _(e):
            run(e, "pe")


class Buf:
    __slots__ = ("t", "key")

    def __init__(self, t, key):
        self.t = t
        self.key = key


class Ring:
    def __init__(self, bufs):
        self.bufs = bufs
        self.i = 0

    def next(self):
        b = self.bufs[self.i % len(self.bufs)]
        self.i += 1
        return b


def build_program(L, NB, S, CTX):
    assert S % TT == 0 and CTX % 128 == 0 and CTX <= TT
    NC3 = NB + 1
    NLT = S // TT
    TOT = CTX + S
    NKT = TOT // 128
    NKC = CTX // 128
    NPV = L * PV_L + PV_GLOB
    UPAD = 15
    PPAD = 8

    nc = bass.Bass("TRN2", target_bir_lowering=False)
    dt_ = nc.dram_tensor
    xT_in = dt_("xT", [NB, D, S], F32, kind="ExternalInput").ap()
    cxT_in = dt_("cxT", [NB, D, CTX], F32, kind="ExternalInput").ap()
    cT_in = dt_("cT", [128, KC * NC3], F32, kind="ExternalInput").ap()
    pvec_in = dt_("pvec", [128, NPV], F32, kind="ExternalInput").ap()
    wA_in = dt_("wA", [L, 128, NA], F32, kind="ExternalInput").ap()
    wB_in = dt_("wB", [L, 128, NBW], F32, kind="ExternalInput").ap()
    wM_in = dt_("wM", [L, 128, NMW], F32, kind="ExternalInput").ap()
    rope_in = dt_("rope", [2, 128, S], F32, kind="ExternalInput").ap()
    perm_in = dt_("perm", [128, 128], F32, kind="ExternalInput").ap()
    outT = dt_("outT", [NB, D, S], F32, kind="ExternalOutput").ap()

    wA_bf = dt_("wA_bf", [L, 128, NA], BF16).ap()
    wB_bf = dt_("wB_bf", [L, 128, NBW], BF16).ap()
    wM_bf = dt_("wM_bf", [L, 128, NMW], BF16).ap()
    xs = dt_("xs", [NB, D, S], F32).ap()
    xcs = dt_("xcs", [NB, D, CTX], F32).ap()
    hT_d = dt_("hT_d", [NB, D, TOT], BF16).ap()
    ul_d = dt_("ul_d", [NB, CONV_CH, S + 2 * UPAD], F32).ap()
    uc_d = dt_("uc_d", [NB, CONV_CH, CTX + 2 * UPAD], F32).ap()
    pl_d = dt_("pl_d", [NB, CONV_CH, S + 2 * PPAD], F32).ap()
    pc_d = dt_("pc_d", [NB, CONV_CH, CTX + 2 * PPAD], F32).ap()
    QT_d = dt_("QT_d", [NB, NH, 128, TOT], BF16).ap()
    KT_d = dt_("KT_d", [NB, NH, 128, TOT], BF16).ap()
    V_d = dt_("V_d", [NB, NH, 128, NKT, 128], BF16).ap()
    OT_d = dt_("OT_d", [NB, D, TOT], BF16).ap()

    stack = ExitStack()
    cx = Cx(nc, stack)

    off = [(nc.sbuf_base + 63) // 64 * 64]
    top = nc.sbuf_top
    nbuf = [0]

    def alloc(shape, dtype, at=None):
        n = 1
        for s_ in shape[1:]:
            n *= s_
        nbytes = n * (4 if dtype == F32 else 2)
        nbytes = (nbytes + 63) // 64 * 64
        if at is None:
            o = off[0]
            off[0] += nbytes
            assert off[0] <= top, ("SBUF overflow", off[0], top)
        else:
            o = at
        nbuf[0] += 1
        t = nc.alloc_sbuf_tensor_at("b%d" % nbuf[0], list(shape), dtype, offset=o)
        return Buf(t, ("sb", nbuf[0])), o, nbytes

    def A(shape, dtype):
        return alloc(shape, dtype)[0]

    ones_f = A([128, 128], F32)
    ones_b = A([128, 128], BF16)
    perm_f = A([128, 128], F32)
    perm_b = A([128, 128], BF16)
    epsb = A([128, 1], F32)
    zerob = A([128, 1], F32)
    pvec = A([128, NPV], F32)
    cT = A([128, KC * NC3], F32)
    cs_b = A([128, KC * NC3], BF16)
    modb = A([128, 48 * NC3], F32)
    A1 = A([128, KC * NC3], F32)
    G1 = A([128, KC * NC3], F32)
    A2 = A([128, KC * NC3], F32)
    G2 = A([128, KC * NC3], F32)
    lamt = A([128, 8], F32)

    xt_ring = Ring([A([128, KC, TT], F32) for _ in range(2)])
    hT_ring = Ring([A([128, KC, TT], BF16) for _ in range(2)])
    w_ring = Ring([A([128, WBUF], BF16) for _ in range(3)])
    TFW = TT + 16
    tf_ring = Ring([A([128, TFW], F32) for _ in range(8)])
    lamtmp = tf_ring.bufs[0]
    rs_ring = Ring([A([128, TT], F32) for _ in range(2)])
    tb_ring = Ring([A([128, TT], BF16) for _ in range(4)])
    tf2_ring = Ring([A([128, TFW], F32) for _ in range(3)])
    bg_rs = A([128, TT], F32)
    wgb = A([128, 512], BF16)

    arena0 = off[0]
    o = arena0
    att = []
    for i in range(2):
        kt_, _, n1 = alloc([128, TOT], BF16, at=o); o += n1
        vt_, _, n2 = alloc([128, NKT, 128], BF16, at=o); o += n2
        qt_, _, n3 = alloc([128, TOT], BF16, at=o); o += n3
        att.append((kt_, vt_, qt_))
    pt_list = []
    for i in range(6):
        b_, _, n1 = alloc([128, 2, TT], BF16, at=o); o += n1
        pt_list.append(b_)
    pt_ring = Ring(pt_list)
    paccA, _, n1 = alloc([128, 2, TT], F32, at=o); o += n1
    paccB, _, n1 = alloc([128, 2, TT], F32, at=o); o += n1
    zsum, _, n1 = alloc([128, 2, TT], F32, at=o); o += n1
    rzb, _, n1 = alloc([128, 2, TT], F32, at=o); o += n1
    osb, _, n1 = alloc([128, 2, TT], F32, at=o); o += n1
    att_extra = [paccA, paccB, zsum, rzb, osb]
    att_extra_keys = [(paccA.key, 'd'), (paccA.key, 'p'), (paccB.key, 'd'), (paccB.key, 'p')]
    att_end = o
    o = arena0
    vtok_l = []
    for i in range(2):
        b_, _, n1 = alloc([128, D], BF16, at=o); o += n1
        vtok_l.append(b_)
    vtok_ring = Ring(vtok_l)
    rope_l = []
    for i in range(2):
        b_, _, n1 = alloc([128, 2, TT], F32, at=o); o += n1
        rope_l.append(b_)
    rope_ring = Ring(rope_l)
    pa_end = o
    o = arena0
    uwin, _, n1 = alloc([128, 4, TT + 2 * UPAD], F32, at=o); o += n1
    pwin, _, n1 = alloc([128, 4, TT + 2 * PPAD], F32, at=o); o += n1
    cvb, _, n1 = alloc([128, 4, TT], F32, at=o); o += n1
    sbuf_, _, n1 = alloc([128, 4, TT], BF16, at=o); o += n1
    mxb, _, n1 = alloc([128, 4, TT], BF16, at=o); o += n1
    pdb, _, n1 = alloc([128, 4, TT], BF16, at=o); o += n1
    mb, _, n1 = alloc([128, KC, TT], BF16, at=o); o += n1
    yb, _, n1 = alloc([128, KC, TT], F32, at=o); o += n1
    actb, o_act, n1 = alloc([128, NJ, TT], BF16, at=o); o += n1
    OTt, _, _ = alloc([128, KC, TT], BF16, at=o_act)
    OTt.key = actb.key
    lnm, _, n1 = alloc([128, TT], F32, at=o); o += n1
    if CONV_DVE_TAPS < CONV_W:
        cv2, _, n1 = alloc([128, TT], F32, at=o); o += n1
    else:
        cv2 = lnm
    pb_end = o
    arena_end = max(att_end, pa_end, pb_end)
    assert arena_end <= top, ("SBUF overflow arena", arena_end, top)
    ARENA = ("arena",)
    VIEW_ATT, VIEW_A, VIEW_B = ("view", "att"), ("view", "a"), ("view", "b")

    class HalfView:
        def __init__(self, t3, h):
            self.t3, self.h = t3, h

        def __getitem__(self, idx):
            r_, c_ = idx
            return self.t3[r_, self.h, c_]

    pp = [Buf(stack.enter_context(nc.psum_tensor("pp%d" % i, [128, 2, TT], F32)), ("pp", i)) for i in range(4)]
    ps = [Buf(HalfView(pp[i // 2].t, i % 2), ("ps", i)) for i in range(8)]
    ps_ring = Ring(ps[0:7])

    cur_view = [None]
    view_keys = {
        VIEW_ATT: [b.key for trio in att for b in trio] + [b.key for b in pt_list] + [b.key for b in att_extra] + att_extra_keys,
        VIEW_A: [b.key for b in vtok_l] + [b.key for b in rope_l],
        VIEW_B: [b.key for b in (uwin, pwin, cvb, sbuf_, mxb, pdb, mb, yb, actb, lnm)],
    }

    def switch_view(v):
        if cur_view[0] == v:
            return
        old = cur_view[0]
        cur_view[0] = v
        if old is None:
            return
        keys = view_keys[old] + view_keys[v]
        cx.memset(lamt.t[:, 7:8], 0.0, w=keys + [("fence",)])

    cx.load(pvec.t[:, :], pvec_in[:, :], r=[], w=[pvec.key])
    cx.load(cT.t[:, :], cT_in[:, :], r=[], w=[cT.key])
    cx.load(perm_f.t[:, :], perm_in[:, :], r=[], w=[perm_f.key])
    cx.memset(ones_f.t[:, :], 1.0, w=[ones_f.key])
    cx.memset(ones_b.t[:, :], 1.0, w=[ones_b.key])
    cx.memset(epsb.t[:, :], EPS, w=[epsb.key])
    cx.memset(zerob.t[:, :], 0.0, w=[zerob.key])
    cx.copy(perm_b.t[:, :], perm_f.t[:, :], r=[perm_f.key], w=[perm_b.key])
    tf = tf_ring.next()
    cx.act(tf.t[:, 0:KC * NC3], cT.t[:, :], AF.Sigmoid, r=[cT.key], w=[tf.key])
    cx.tt(cs_b.t[:, :], cT.t[:, :], tf.t[:, 0:KC * NC3], ALU.mult, r=[cT.key, tf.key], w=[cs_b.key])
    zt = tf_ring.next()
    cx.memset(zt.t[:, :], 0.0, w=[zt.key])
    for b in range(NB):
        for (dd, n_, pad, nm) in ((ul_d, S, UPAD, "ul"), (uc_d, CTX, UPAD, "uc"), (pl_d, S, PPAD, "pl"),
                                  (pc_d, CTX, PPAD, "pc")):
            v = dd[b].rearrange("(c p) t -> p c t", p=128)
            cx.store(v[:, :, 0:pad], zt.t[:, 0:4 * pad].rearrange("p (c t) -> p c t", c=4),
                     r=[zt.key], w=[(nm + "pad", b, 0)])
            cx.store(v[:, :, pad + n_:pad + n_ + pad], zt.t[:, 0:4 * pad].rearrange("p (c t) -> p c t", c=4),
                     r=[zt.key], w=[(nm + "pad", b, 1)])
    CW = 8192
    cast_q = []

    def queue_casts(l):
        for (src, dst, n_, nm) in ((wM_in, wM_bf, NMW, "wM"), (wA_in, wA_bf, NA, "wA"), (wB_in, wB_bf, NBW, "wB")):
            c0 = 0
            while c0 < n_:
                c1 = min(n_, c0 + CW)
                cast_q.append((dst[l][:, c0:c1], src[l][:, c0:c1], (nm, l, c0 // CW)))
                c0 = c1

    def drain_casts(n):
        while cast_q and n > 0:
            d_, s_, k_ = cast_q.pop(0)
            cx.store(d_, s_, r=[], w=[k_])
            n -= 1

    queue_casts(0)
    drain_casts(10 ** 9)

    def wkeys(nm, l, c0, c1):
        return [(nm, l, i) for i in range(c0 // CW, (c1 - 1) // CW + 1)]

    class WStream:
        def __init__(self, nm, dram, l):
            self.nm, self.dram, self.l, self.pos = nm, dram, l, 0

        def next(self, n):
            b = w_ring.next()
            c0, c1 = self.pos, self.pos + n
            cx.load(b.t[:, 0:n], self.dram[self.l][:, c0:c1], r=wkeys(self.nm, self.l, c0, c1), w=[b.key])
            self.pos = c1
            return b

    def pv(l, o_, n=1):
        return pvec.t[:, l * PV_L + o_: l * PV_L + o_ + n]

    def col(bufap, kc, j):
        return bufap.t[:, kc * NC3 + j: kc * NC3 + j + 1]

    def modcol(n, j):
        return modb.t[:, n * NC3 + j: n * NC3 + j + 1]

    def phase_mod(l):
        lam_init = 0.8 - 0.6 * math.exp(-0.3 * l)
        ws = WStream("wM", wM_bf, l)
        n = 0
        while n < 48:
            g = min(5, 48 - n)
            wb = ws.next(g * 1024)
            for i in range(g):
                p_ = ps_ring.next()
                for kc in range(KC):
                    cx.mm(p_.t[:, 0:NC3], wb.t[:, i * 1024 + kc * 128: i * 1024 + (kc + 1) * 128],
                          cs_b.t[:, kc * NC3:(kc + 1) * NC3], kc == 0, kc == KC - 1,
                          r=[wb.key, cs_b.key], w=[p_.key])
                cx.ts(modb.t[:, (n + i) * NC3:(n + i + 1) * NC3], p_.t[:, 0:NC3], pv(l, PV_BMOD + n + i), None,
                      ALU.add, None, r=[p_.key, pvec.key], w=[modb.key])
            n += g
        for kc in range(KC):
            sl = slice(kc * NC3, (kc + 1) * NC3)
            cx.ts(A1.t[:, sl], modb.t[:, (8 + kc) * NC3:(9 + kc) * NC3], pv(l, PV_G + 0 + kc), pv(l, PV_G + 0 + kc), ALU.mult, ALU.add,
                  r=[modb.key, pvec.key], w=[A1.key])
            cx.ts(G1.t[:, sl], modb.t[:, (16 + kc) * NC3:(17 + kc) * NC3], pv(l, PV_G + 8 + kc), None, ALU.mult, None,
                  r=[modb.key, pvec.key], w=[G1.key])
            cx.ts(A2.t[:, sl], modb.t[:, (32 + kc) * NC3:(33 + kc) * NC3], pv(l, PV_G + 16 + kc), pv(l, PV_G + 16 + kc), ALU.mult, ALU.add,
                  r=[modb.key, pvec.key], w=[A2.key])
            cx.ts(G2.t[:, sl], modb.t[:, (40 + kc) * NC3:(41 + kc) * NC3], pv(l, PV_G + 24 + kc), None, ALU.mult, None,
                  r=[modb.key, pvec.key], w=[G2.key])
        for i in range(2):
            cx.tt(lamtmp.t[:, 0:64], pv(l, PV_LAM + 128 * i, 64), pv(l, PV_LAM + 128 * i + 64, 64), ALU.mult,
                  r=[pvec.key], w=[lamtmp.key])
            cx.add("dve", (lambda i_: lambda e: e.reduce_sum(out=lamt.t[:, 5 + i_:6 + i_], in_=lamtmp.t[:, 0:64],
                                                             axis=mybir.AxisListType.X))(i),
                   r=[lamtmp.key], w=[lamt.key])
        cx.act(lamt.t[:, 0:2], lamt.t[:, 5:7], AF.Exp, r=[lamt.key], w=[lamt.key])
        cx.tt(lamt.t[:, 2:3], lamt.t[:, 0:1], lamt.t[:, 1:2], ALU.subtract, r=[lamt.key], w=[lamt.key])
        cx.ts(lamt.t[:, 3:4], lamt.t[:, 2:3], -1.0, -lam_init, ALU.mult, ALU.add, r=[lamt.key], w=[lamt.key])
        cx.ts(lamt.t[:, 4:5], pv(l, PV_SUBG), 1.0 - lam_init, None, ALU.mult, None, r=[pvec.key], w=[lamt.key])

    class Tile:
        pass

    def tiles_for(b):
        res = []
        t = Tile()
        t.kind, t.b, t.id, t.T, t.t0, t.c0, t.j = "c", b, "c", CTX, 0, 0, NB
        t.first, t.last = True, True
        res.append(t)
        for i in range(NLT):
            t = Tile()
            t.kind, t.b, t.id, t.T, t.t0, t.c0, t.j = "l", b, i, TT, i * TT, CTX + i * TT, b
            t.first, t.last = (i == 0), (i == NLT - 1)
            res.append(t)
        return res

    def xsrc(l, t):
        if t.kind == "c":
            return (cxT_in if l == 0 else xcs)[t.b]
        return (xT_in if l == 0 else xs)[t.b]

    def xdst(l, t):
        if t.kind == "c":
            return xcs[t.b]
        return (outT if l == L - 1 else xs)[t.b]

    def rstd_from_ps(p_, T, n):
        sd = tf_ring.next()
        cx.act(sd.t[:, 0:T], p_.t[:, 0:T], AF.Ln, r=[p_.key, epsb.key], w=[sd.key],
               bias=epsb.t[:, 0:1], scale=1.0 / n)
        rs = rs_ring.next()
        cx.act(rs.t[:, 0:T], sd.t[:, 0:T], AF.Exp, r=[sd.key], w=[rs.key], scale=-0.5)
        return rs

    def sumsq_stats(src, T, nchunks):
        p_ = ps_ring.next()
        for c in range(nchunks):
            sq = tf_ring.next()
            cx.act(sq.t[:, 0:T], src.t[:, c, 0:T], AF.Square, r=[src.key], w=[sq.key])
            cx.mm(p_.t[:, 0:T], ones_f.t[:, :], sq.t[:, 0:T], c == 0, c == nchunks - 1,
                  r=[ones_f.key, sq.key], w=[p_.key])
        return p_

    def adaln(xt, T, j, Asc, shift_n0, hT):
        p_ = sumsq_stats(xt, T, KC)
        rs = rstd_from_ps(p_, T, D)
        for kc in range(KC):
            t1 = tf_ring.next()
            cx.stt(t1.t[:, 0:T], xt.t[:, kc, 0:T], col(Asc, kc, j), rs.t[:, 0:T], ALU.mult, ALU.mult,
                   r=[xt.key, Asc.key, rs.key], w=[t1.key])
            cx.act(hT.t[:, kc, 0:T], t1.t[:, 0:T], AF.Identity, r=[t1.key, modb.key], w=[hT.key],
                   bias=modcol(shift_n0 + kc, j))

    def adaln_bg(xt, T, j, Asc, shift_n0, hT):
        p_ = ps[7]
        for c in range(KC):
            sq = tf2_ring.next()
            cx.act(sq.t[:, 0:T], xt.t[:, c, 0:T], AF.Square, r=[xt.key], w=[sq.key])
            cx.mm(p_.t[:, 0:T], ones_f.t[:, :], sq.t[:, 0:T], c == 0, c == KC - 1,
                  r=[ones_f.key, sq.key], w=[p_.key])
            yield
        sd = tf2_ring.next()
        cx.act(sd.t[:, 0:T], p_.t[:, 0:T], AF.Ln, r=[p_.key, epsb.key], w=[sd.key], bias=epsb.t[:, 0:1], scale=1.0 / D)
        rs = bg_rs
        cx.act(rs.t[:, 0:T], sd.t[:, 0:T], AF.Exp, r=[sd.key], w=[rs.key], scale=-0.5)
        yield
        for kc in range(KC):
            t1 = tf2_ring.next()
            cx.stt(t1.t[:, 0:T], xt.t[:, kc, 0:T], col(Asc, kc, j), rs.t[:, 0:T], ALU.mult, ALU.mult,
                   r=[xt.key, Asc.key, rs.key], w=[t1.key])
            cx.act(hT.t[:, kc, 0:T], t1.t[:, 0:T], AF.Identity, r=[t1.key, modb.key], w=[hT.key],
                   bias=modcol(shift_n0 + kc, j))
            yield

    def drain(g):
        if g is not None:
            for _ in g:
                pass

    def make_bg(g, n):
        def bg():
            if g is None:
                return
            for _ in range(n):
                try:
                    next(g)
                except StopIteration:
                    return
        return bg

    def proj(wb, woff, nk, rhs_buf, T, extra_r=()):
        p_ = ps_ring.next()
        for kc in range(nk):
            cx.mm(p_.t[:, 0:T], wb.t[:, woff + kc * 128: woff + (kc + 1) * 128], rhs_buf.t[:, kc, 0:T],
                  kc == 0, kc == nk - 1, r=[wb.key, rhs_buf.key] + list(extra_r), w=[p_.key])
        return p_

    def a1_gen(l, t, st):
        drain_casts(1)
        cx.sec = 'A1'
        b, T, j = t.b, t.T, t.j
        xt = xt_ring.next()
        cx.load(xt.t[:, :, 0:T], xsrc(l, t).rearrange("(kc p) t -> p kc t", p=128)[:, :, t.t0:t.t0 + T],
                r=[("x", t.kind, b, t.id)], w=[xt.key])
        hT = hT_ring.next()
        st["hT"] = hT
        yield
        for _ in adaln_bg(xt, T, j, A1, 0, hT):
            yield
            cx.sec = 'A1'
        cx.store(hT_d[b].rearrange("(kc p) t -> p kc t", p=128)[:, :, t.c0:t.c0 + T], hT.t[:, :, 0:T],
                 r=[hT.key], w=[("hT", b, t.id)])
        yield

    def phase_a2(l, t, st, bg_):
        def bg():
            bg_()
            cx.sec = 'A2'
        cx.sec = 'A2'
        b, T, j = t.b, t.T, t.j
        hT = st["hT"]
        if t.kind == "l":
            rp = rope_ring.next()
            cx.load(rp.t[:, :, 0:T], rope_in.rearrange("a p t -> p a t")[:, :, t.t0:t.t0 + T], r=[], w=[rp.key, VIEW_A])
        ws = WStream("wA", wA_bf, l)
        u_d = (uc_d if t.kind == "c" else ul_d)[b]
        unm = "uc" if t.kind == "c" else "ul"
        for half in range(2):
            wb = ws.next(4096)
            for ci in range(2):
                c = half * 2 + ci
                pa = proj(wb, (2 * ci) * 1024, KC, hT, T)
                pb = proj(wb, (2 * ci + 1) * 1024, KC, hT, T)
                sg = tf_ring.next()
                cx.act(sg.t[:, 0:T], pb.t[:, 0:T], AF.Sigmoid, r=[pb.key], w=[sg.key])
                u = tf_ring.next()
                cx.tt(u.t[:, 0:T], pa.t[:, 0:T], sg.t[:, 0:T], ALU.mult, r=[pa.key, sg.key], w=[u.key])
                cx.store(u_d[c * 128:(c + 1) * 128, UPAD + t.t0: UPAD + t.t0 + T], u.t[:, 0:T],
                         r=[u.key], w=[(unm, b, t.id)])
                bg()
        for (dst, nm) in ((QT_d, "Q"), (KT_d, "K")):
            for half in range(2):
                wb = ws.next(4096)
                for hi in range(4):
                    h = half * 4 + hi
                    pq = proj(wb, hi * 1024, KC, hT, T)
                    qb = tb_ring.next()
                    cx.act(qb.t[:, 0:T], pq.t[:, 0:T], AF.Identity, r=[pq.key, zerob.key],
                           w=[qb.key, ("lock", pq.key)], bias=zerob.t[:, 0:1])
                    if t.kind == "l":
                        psw = ps_ring.next()
                        cx.mm(psw.t[:, 0:T], perm_b.t[:, :], qb.t[:, 0:T], True, True,
                              r=[perm_b.key, qb.key], w=[psw.key])
                        t1 = tf_ring.next()
                        cx.tt(t1.t[:, 0:T], pq.t[:, 0:T], rp.t[:, 0, 0:T], ALU.mult,
                              r=[pq.key, rp.key, ("lock", pq.key)], w=[t1.key])
                        t2 = tf_ring.next()
                        cx.tt(t2.t[:, 0:T], psw.t[:, 0:T], rp.t[:, 1, 0:T], ALU.mult, r=[psw.key, rp.key], w=[t2.key])
                        qr = tb_ring.next()
                        cx.tt(qr.t[:, 0:T], t1.t[:, 0:T], t2.t[:, 0:T], ALU.add, r=[t1.key, t2.key], w=[qr.key], eng="pool")
                        qb = qr
                    cx.store(dst[b, h][:, t.c0:t.c0 + T], qb.t[:, 0:T], r=[qb.key], w=[(nm, b, h, t.id)])
                    bg()
        p_d = (pc_d if t.kind == "c" else pl_d)[b]
        pnm = "pc" if t.kind == "c" else "pl"
        wb = ws.next(4096)
        for c in range(4):
            pp = proj(wb, c * 1024, KC, hT, T)
            pf = tf_ring.next()
            cx.copy(pf.t[:, 0:T], pp.t[:, 0:T], r=[pp.key], w=[pf.key])
            cx.store(p_d[c * 128:(c + 1) * 128, PPAD + t.t0: PPAD + t.t0 + T], pf.t[:, 0:T],
                     r=[pf.key], w=[(pnm, b, t.id)])
            bg()
        wv = [ws.next(4096), ws.next(4096)]
        for tsi in range(T // 128):
            vt = vtok_ring.next()
            for nh in range(2):
                p_ = ps_ring.next()
                for kc in range(KC):
                    cx.mm(p_.t[:, :], hT.t[:, kc, tsi * 128:(tsi + 1) * 128], wv[nh].t[:, kc * 512:(kc + 1) * 512],
                          kc == 0, kc == KC - 1, r=[hT.key, wv[nh].key], w=[p_.key])
                if nh == 0:
                    cx.act(vt.t[:, 0:512], p_.t[:, :], AF.Identity, r=[p_.key, zerob.key], w=[vt.key, VIEW_A], bias=zerob.t[:, 0:1])
                else:
                    cx.copy(vt.t[:, 512:1024], p_.t[:, :], r=[p_.key], w=[vt.key, VIEW_A])
            kt = t.c0 // 128 + tsi
            cx.store(V_d[b].rearrange("h p k d -> p h k d")[:, :, kt, :], vt.t[:, :].rearrange("p (h d) -> p h d", h=NH),
                     r=[vt.key], w=[("V", b, kt)])

    def attention(l, b):
        switch_view(VIEW_ATT)
        cx.sec = 'ATT'
        need_ctx = l < L - 1
        tl = tiles_for(b)
        all_ids = [t.id for t in tl]
        spairs = Ring(pp[0:3])
        O0, O1 = ps[6], ps[7]
        deferred = []

        def hk(p_):
            i_ = int(p_.key[1])
            return [("ps", 2 * i_), ("ps", 2 * i_ + 1)]

        def tick(allow=True):
            for d_ in deferred:
                d_[0] -= 1
            if allow:
                for d_ in deferred:
                    if d_[0] <= 0:
                        deferred.remove(d_)
                        d_[1]()
                        break

        def flush():
            while deferred:
                deferred.pop(0)[1]()

        def flush_p2():
            pend = [d_ for d_ in deferred if d_[2] == 2]
            for d_ in pend:
                deferred.remove(d_)
                d_[1]()

        def part2(h, t):
            T = t.T
            zp = spairs.next()
            for i in range(2):
                cx.mm(zp.t[:, i, 0:T], ones_f.t[:, :], zsum.t[:, i, 0:T], True, True,
                      r=[ones_f.key, zsum.key], w=[hk(zp)[i]])
            cx.act(rzb.t[:, :, 0:T], zp.t[:, :, 0:T], AF.Ln, r=hk(zp), w=[rzb.key])
            cx.act(rzb.t[:, :, 0:T], rzb.t[:, :, 0:T], AF.Exp, r=[rzb.key], w=[rzb.key], scale=-1.0)
            t0_ = tf_ring.next()
            cx.tt(t0_.t[:, 0:T], osb.t[:, 0, 0:T], rzb.t[:, 0, 0:T], ALU.mult, r=[osb.key, rzb.key], w=[t0_.key])
            t1_ = tf_ring.next()
            cx.tt(t1_.t[:, 0:T], osb.t[:, 1, 0:T], rzb.t[:, 1, 0:T], ALU.mult, r=[osb.key, rzb.key], w=[t1_.key])
            o_ = tf_ring.next()
            cx.stt(o_.t[:, 0:T], t1_.t[:, 0:T], lamt.t[:, 3:4], t0_.t[:, 0:T], ALU.mult, ALU.add,
                   r=[t1_.key, lamt.key, t0_.key], w=[o_.key])
            osq = tf_ring.next()
            cx.tt(osq.t[:, 0:T], o_.t[:, 0:T], o_.t[:, 0:T], ALU.mult, r=[o_.key], w=[osq.key])
            deferred.append([int(_os.environ.get('KT3', '8')), lambda: part3(h, t, o_, osq), 3])

        def part3(h, t, o_, osq):
            T = t.T
            zp = spairs.next()
            cx.mm(zp.t[:, 0, 0:T], ones_f.t[:, :], osq.t[:, 0:T], True, True, r=[ones_f.key, osq.key],
                  w=[hk(zp)[0]])
            sd = tf_ring.next()
            cx.act(sd.t[:, 0:T], zp.t[:, 0, 0:T], AF.Ln, r=[hk(zp)[0], epsb.key], w=[sd.key],
                   bias=epsb.t[:, 0:1], scale=1.0 / 128)
            rs = rs_ring.next()
            cx.act(rs.t[:, 0:T], sd.t[:, 0:T], AF.Exp, r=[sd.key], w=[rs.key], scale=-0.5)
            ob = tb_ring.next()
            cx.stt(ob.t[:, 0:T], o_.t[:, 0:T], lamt.t[:, 4:5], rs.t[:, 0:T], ALU.mult, ALU.mult,
                   r=[o_.key, lamt.key, rs.key], w=[ob.key])
            cx.store(OT_d[b][h * 128:(h + 1) * 128, t.c0:t.c0 + T], ob.t[:, 0:T], r=[ob.key], w=[("OT", b, h, t.id)])

        for h in range(NH):
            KT, VT, QT = att[h % 2]
            cx.load(KT.t[:, :], KT_d[b, h][:, :], r=[("K", b, h, i) for i in all_ids], w=[KT.key, VIEW_ATT])
            cx.load(VT.t[:, :, :], V_d[b, h][:, :, :], r=[("V", b, k) for k in range(NKT)], w=[VT.key, VIEW_ATT])
            cx.load(QT.t[:, :], QT_d[b, h][:, :], r=[("Q", b, h, i) for i in all_ids], w=[QT.key, VIEW_ATT])
            tls = [t for t in tl if not (t.kind == "c" and not need_ctx)]
            steps = []
            for t in tls:
                nk = NKC if t.kind == "c" else NKT
                for kt in range(nk):
                    steps.append((t, kt, nk))

            def qk(si):
                t_, kt_, _ = steps[si]
                T_ = t_.T
                sp = spairs.next()
                ks = slice(kt_ * 128, (kt_ + 1) * 128)
                qs_ = slice(t_.c0, t_.c0 + T_)
                cx.mm(sp.t[:, 0, 0:T_], KT.t[0:64, ks], QT.t[0:64, qs_], True, True, r=[KT.key, QT.key],
                      w=[hk(sp)[0]], tp=(0, 0))
                cx.mm(sp.t[:, 1, 0:T_], KT.t[64:128, ks], QT.t[64:128, qs_], True, True, r=[KT.key, QT.key],
                      w=[hk(sp)[1]], tp=(64, 0))
                return sp

            XL = _os.environ.get('KXL', '0') == '1'
            sq_ = []
            if XL:
                sq_ = [qk(0)]
                if len(steps) > 1:
                    sq_.append(qk(1))
            for si, (t, kt, nk) in enumerate(steps):
                T = t.T
                if not XL and kt == 0:
                    sq_ = [qk(si)]
                    if nk > 1:
                        sq_.append(qk(si + 1))
                if TRACE_TAGS == 2:
                    cx.sec = 'ATT.l%d.b%d.h%d.%s.%d' % (l, b, h, t.id, kt)
                sp = sq_.pop(0)
                pt = pt_ring.next()
                cx.act(pt.t[:, :, 0:T], sp.t[:, :, 0:T], AF.Exp, r=hk(sp), w=[pt.key], scale=0.125)
                if (si + 2 < len(steps)) if XL else (kt + 2 < nk):
                    sq_.append(qk(si + 2))
                st_, sp_ = (kt == 0), (kt == nk - 1)
                cx.mm(O0.t[:, 0:T], VT.t[:, kt, :], pt.t[:, 0, 0:T], st_, sp_,
                      r=[VT.key, pt.key], w=[O0.key])
                cx.mm(O1.t[:, 0:T], VT.t[:, kt, :], pt.t[:, 1, 0:T], st_, sp_,
                      r=[VT.key, pt.key], w=[O1.key])
                acc = paccA if kt % 2 == 0 else paccB
                TD = T if T < TT else ZSPLIT
                for (eng_, c0_, c1_, kx) in (("dve", 0, TD, "d"), ("pool", TD, T, "p")):
                    if c1_ <= c0_:
                        continue
                    if kt < 2:
                        cx.copy(acc.t[:, :, c0_:c1_], pt.t[:, :, c0_:c1_], r=[pt.key], w=[(acc.key, kx)], eng=eng_)
                    else:
                        cx.tt(acc.t[:, :, c0_:c1_], acc.t[:, :, c0_:c1_], pt.t[:, :, c0_:c1_], ALU.add,
                              r=[(acc.key, kx), pt.key], w=[(acc.key, kx)], eng=eng_)
                tick(allow=(kt < nk - 1))
                if kt < nk - 1:
                    continue
                flush_p2()
                cx.act(osb.t[:, :, 0:T], pp[3].t[:, :, 0:T], AF.Identity, r=[O0.key, O1.key, zerob.key],
                       w=[osb.key], bias=zerob.t[:, 0:1])
                ak = [(paccA.key, "d"), (paccA.key, "p"), (paccB.key, "d"), (paccB.key, "p")]
                if nk > 1:
                    cx.tt(zsum.t[:, :, 0:T], paccA.t[:, :, 0:T], paccB.t[:, :, 0:T], ALU.add,
                          r=ak, w=[zsum.key])
                else:
                    cx.copy(zsum.t[:, :, 0:T], paccA.t[:, :, 0:T], r=ak, w=[zsum.key])
                deferred.append([int(_os.environ.get('KT2', '4')), (lambda h_, t_: lambda: part2(h_, t_))(h, t), 2])
        flush()

    def b1_gen(l, t, st):
        for _ in b1_gen_(l, t, st):
            yield
            cx.sec = 'B1'

    def b1_gen_(l, t, st):
        cx.sec = 'B1'
        drain_casts(1)
        b, T, j = t.b, t.T, t.j
        VB = VIEW_B
        hT = hT_ring.next()
        cx.load(hT.t[:, :, 0:T], hT_d[b].rearrange("(kc p) t -> p kc t", p=128)[:, :, t.c0:t.c0 + T],
                r=[("hT", b, t.id)], w=[hT.key])
        xt = xt_ring.next()
        cx.load(xt.t[:, :, 0:T], xsrc(l, t).rearrange("(kc p) t -> p kc t", p=128)[:, :, t.t0:t.t0 + T],
                r=[("x", t.kind, b, t.id)], w=[xt.key])
        st["hT"], st["xt"] = hT, xt
        if t.kind == "c":
            u_d, p_d, unm, pnm = uc_d[b], pc_d[b], "uc", "pc"
            nb_ids = ["c"]
        else:
            u_d, p_d, unm, pnm = ul_d[b], pl_d[b], "ul", "pl"
            nb_ids = [i for i in (t.id - 1, t.id, t.id + 1) if 0 <= i < NLT]
        cx.load(uwin.t[:, :, 0:T + 2 * UPAD], u_d.rearrange("(c p) t -> p c t", p=128)[:, :, t.t0:t.t0 + T + 2 * UPAD],
                r=[(unm, b, i) for i in nb_ids] + [(unm + "pad", b, 0), (unm + "pad", b, 1)], w=[uwin.key, VB])
        cx.load(pwin.t[:, :, 0:T + 2 * PPAD], p_d.rearrange("(c p) t -> p c t", p=128)[:, :, t.t0:t.t0 + T + 2 * PPAD],
                r=[(pnm, b, i) for i in nb_ids] + [(pnm + "pad", b, 0), (pnm + "pad", b, 1)], w=[pwin.key, VB])
        cx.load(wgb.t[:, :], wB_bf[l][:, 0:512], r=wkeys("wB", l, 0, 512), w=[wgb.key])
        yield
        for c in range(4):
            acc = cvb.t[:, c, 0:T]
            acc2 = cv2.t[:, 0:T]
            cw0 = PV_CONVW + c * CONV_W
            ND = CONV_DVE_TAPS
            cx.ts(acc, uwin.t[:, c, 0:T], pv(l, cw0), pv(l, PV_CONVB + c), ALU.mult, ALU.add,
                  r=[uwin.key, pvec.key], w=[(cvb.key, c), cvb.key, VB])
            if ND < CONV_W:
                cx.ts(acc2, uwin.t[:, c, ND:ND + T], pv(l, cw0 + ND), None, ALU.mult, None,
                      r=[uwin.key, pvec.key], w=[cv2.key], eng="pool")
            kd, kp = 1, ND + 1
            while kd < ND or kp < CONV_W:
                for _ in range(2):
                    if kd < ND:
                        cx.stt(acc, uwin.t[:, c, kd:kd + T], pv(l, cw0 + kd), acc, ALU.mult, ALU.add,
                               r=[uwin.key, pvec.key, (cvb.key, c)], w=[(cvb.key, c)])
                        kd += 1
                if kp < CONV_W:
                    cx.stt(acc2, uwin.t[:, c, kp:kp + T], pv(l, cw0 + kp), acc2, ALU.mult, ALU.add,
                           r=[uwin.key, pvec.key, cv2.key], w=[cv2.key], eng="pool")
                    kp += 1
                yield
            if ND < CONV_W:
                cx.tt(acc, acc, acc2, ALU.add, r=[(cvb.key, c), cv2.key], w=[(cvb.key, c)])
            cx.memset(lamt.t[:, 7:8], 0.0, w=[cvb.key] + [(cvb.key, c)])
            yield
        pm = ps_ring.next()
        for c in range(4):
            cx.mm(pm.t[:, 0:T], ones_f.t[:, :], cvb.t[:, c, 0:T], c == 0, c == 3, r=[ones_f.key, cvb.key], w=[pm.key])
        pq_ = ps_ring.next()
        for c in range(4):
            sq = tf2_ring.next()
            cx.act(sq.t[:, 0:T], cvb.t[:, c, 0:T], AF.Square, r=[cvb.key], w=[sq.key])
            cx.mm(pq_.t[:, 0:T], ones_f.t[:, :], sq.t[:, 0:T], c == 0, c == 3, r=[ones_f.key, sq.key], w=[pq_.key])
        yield
        mean = lnm
        cx.ts(mean.t[:, 0:T], pm.t[:, 0:T], 1.0 / CONV_CH, None, ALU.mult, None, r=[pm.key], w=[mean.key])
        msq = tf2_ring.next()
        cx.tt(msq.t[:, 0:T], mean.t[:, 0:T], mean.t[:, 0:T], ALU.mult, r=[mean.key], w=[msq.key])
        var = tf2_ring.next()
        cx.stt(var.t[:, 0:T], pq_.t[:, 0:T], 1.0 / CONV_CH, msq.t[:, 0:T], ALU.mult, ALU.subtract,
               r=[pq_.key, msq.key], w=[var.key])
        sd = tf2_ring.next()
        cx.act(sd.t[:, 0:T], var.t[:, 0:T], AF.Ln, r=[var.key, epsb.key], w=[sd.key], bias=epsb.t[:, 0:1], scale=1.0)
        rs = bg_rs
        cx.act(rs.t[:, 0:T], sd.t[:, 0:T], AF.Exp, r=[sd.key], w=[rs.key], scale=-0.5)
        yield
        for c in range(4):
            z = tf2_ring.next()
            cx.tt(z.t[:, 0:T], cvb.t[:, c, 0:T], mean.t[:, 0:T], ALU.subtract, r=[cvb.key, mean.key], w=[z.key])
            cx.tt(z.t[:, 0:T], z.t[:, 0:T], rs.t[:, 0:T], ALU.mult, r=[z.key, rs.key], w=[z.key])
            if USE_SILU:
                cx.act(sbuf_.t[:, c, 0:T], z.t[:, 0:T], AF.Silu, r=[z.key, pvec.key], w=[sbuf_.key, VB],
                       scale=pv(l, PV_LNG + c), bias=pv(l, PV_LNB + c))
            else:
                cx.ts(z.t[:, 0:T], z.t[:, 0:T], pv(l, PV_LNG + c), pv(l, PV_LNB + c), ALU.mult, ALU.add,
                      r=[z.key, pvec.key], w=[z.key])
                sg = tf2_ring.next()
                cx.act(sg.t[:, 0:T], z.t[:, 0:T], AF.Sigmoid, r=[z.key], w=[sg.key])
                cx.tt(sbuf_.t[:, c, 0:T], z.t[:, 0:T], sg.t[:, 0:T], ALU.mult, r=[z.key, sg.key], w=[sbuf_.key, VB])
            yield
        W2 = T + 2 * PPAD
        for g in range(4):
            wwin = 2 << g
            pw = pwin.t[:, g, :]
            cur = tf2_ring.next()
            cx.tt(cur.t[:, 1:W2], pw[:, 0:W2 - 1], pw[:, 1:W2], ALU.add, r=[pwin.key], w=[cur.key])
            lo, hi = 1, W2
            sh = 1
            for step in range(g):
                nx = tf2_ring.next()
                cx.tt(nx.t[:, lo + sh:hi - sh], cur.t[:, lo:hi - 2 * sh], cur.t[:, lo + 2 * sh:hi], ALU.add,
                      r=[cur.key], w=[nx.key])
                lo, hi = lo + sh, hi - sh
                cur = nx
                sh *= 2
            ctr = cur.t[:, PPAD:PPAD + T]
            gpv = pvec.t[:, L * PV_L + PV_CF + g * 8: L * PV_L + PV_CF + g * 8 + 8]
            gpl = pvec.t[:, L * PV_L + PV_CL + g * 8: L * PV_L + PV_CL + g * 8 + 8]
            if t.first:
                cx.tt(cur.t[:, PPAD:PPAD + 8], cur.t[:, PPAD:PPAD + 8], gpv, ALU.mult, r=[cur.key, pvec.key], w=[cur.key])
            if t.last:
                cx.tt(cur.t[:, PPAD + T - 8:PPAD + T], cur.t[:, PPAD + T - 8:PPAD + T], gpl, ALU.mult,
                      r=[cur.key, pvec.key], w=[cur.key])
            cx.stt(pdb.t[:, g, 0:T], ctr, 1.0 / wwin, pw[:, PPAD:PPAD + T], ALU.mult, ALU.subtract,
                   r=[cur.key, pwin.key], w=[pdb.key, VB])
            yield
        for g in range(4):
            p_ = ps_ring.next()
            cx.mm(p_.t[:, 0:T], wgb.t[:, g * 128:(g + 1) * 128], pdb.t[:, g, 0:T], True, True,
                  r=[wgb.key, pdb.key], w=[p_.key])
            cx.act(mxb.t[:, g, 0:T], p_.t[:, 0:T], AF.Identity, r=[p_.key, pvec.key, zerob.key], w=[mxb.key, VB],
                   scale=pv(l, PV_PSC + g), bias=zerob.t[:, 0:1])
            yield

    def out_and_residual(l, t, st, ws, nk_in, src, woffs, Gs, bg):
        T, j, xt = t.T, t.j, st["xt"]
        pst = ps[7]
        for k in range(KC):
            wb, wo = woffs(k)
            py = proj(wb, wo, nk_in, src, T)
            cx.act(yb.t[:, k, 0:T], py.t[:, 0:T], AF.Identity, r=[py.key, zerob.key], w=[yb.key, VIEW_B], bias=zerob.t[:, 0:1])
            sq = tf_ring.next()
            cx.act(sq.t[:, 0:T], yb.t[:, k, 0:T], AF.Square, r=[yb.key], w=[sq.key])
            cx.mm(pst.t[:, 0:T], ones_f.t[:, :], sq.t[:, 0:T], k == 0, k == KC - 1,
                  r=[ones_f.key, sq.key], w=[pst.key])
            bg()
        rs_ = rstd_from_ps(pst, T, D)
        for kc in range(KC):
            t1 = tf_ring.next()
            cx.stt(t1.t[:, 0:T], yb.t[:, kc, 0:T], col(Gs, kc, j), rs_.t[:, 0:T], ALU.mult, ALU.mult,
                   r=[yb.key, Gs.key, rs_.key], w=[t1.key])
            cx.tt(xt.t[:, kc, 0:T], xt.t[:, kc, 0:T], t1.t[:, 0:T], ALU.add, r=[xt.key, t1.key], w=[xt.key])

    def b_merge(l, t, st):
        drain_casts(1)
        cx.sec = 'Bmerge'
        b, T, j = t.b, t.T, t.j
        VB = VIEW_B
        hT = st["hT"]
        cx.load(OTt.t[:, :, 0:T], OT_d[b].rearrange("(kc p) t -> p kc t", p=128)[:, :, t.c0:t.c0 + T],
                r=[("OT", b, h, t.id) for h in range(NH)], w=[OTt.key, VB])
        ws = WStream("wB", wB_bf, l)
        ws.pos = 512
        st["ws"] = ws
        for k in range(KC):
            wb = ws.next(NBK)
            pg = [proj(wb, br * 1024, KC, hT, T) for br in range(3)]
            pya = proj(wb, 3072, 4, sbuf_, T)
            pyb = proj(wb, 3072 + 512, KC, OTt, T)
            pyc = proj(wb, 3072 + 512 + 1024, 4, mxb, T)
            ms = []
            for br, py in enumerate((pya, pyb, pyc)):
                gt = tf_ring.next()
                cx.act(gt.t[:, 0:T], pg[br].t[:, 0:T], AF.Sigmoid, r=[pg[br].key, pvec.key], w=[gt.key],
                       bias=pv(l, PV_BGATE + br * 8 + k))
                m_ = tf_ring.next()
                cx.tt(m_.t[:, 0:T], py.t[:, 0:T], gt.t[:, 0:T], ALU.mult, r=[py.key, gt.key], w=[m_.key])
                ms.append(m_)
            cx.tt(ms[0].t[:, 0:T], ms[0].t[:, 0:T], ms[1].t[:, 0:T], ALU.add, r=[ms[0].key, ms[1].key], w=[ms[0].key],
                  eng="pool")
            cx.tt(mb.t[:, k, 0:T], ms[0].t[:, 0:T], ms[2].t[:, 0:T], ALU.add, r=[ms[0].key, ms[2].key], w=[mb.key, VB],
                  eng="pool")
        cx.sec = 'Bout'
        wo_cache = {}

        def wo_mix(k):
            if k % 4 == 0:
                wo_cache["b"] = ws.next(4096)
            return wo_cache["b"], (k % 4) * 1024

        out_and_residual(l, t, st, ws, KC, mb, wo_mix, G1, lambda: None)

    def b_ffn(l, t, st, bg_):
        def bg():
            bg_()
            cx.sec = 'Bffn'
        cx.sec = 'Bffn'
        b, T, j = t.b, t.T, t.j
        xt, ws = st["xt"], st["ws"]
        h2 = hT_ring.next()
        adaln(xt, T, j, A2, 24, h2)
        for jj in range(NJ):
            if jj % 2 == 0:
                wb = ws.next(4096)
            o_ = (jj % 2) * 2048
            p1 = proj(wb, o_, KC, h2, T)
            p2 = proj(wb, o_ + 1024, KC, h2, T)
            if USE_SILU:
                t1 = tf_ring.next()
                cx.act(t1.t[:, 0:T], p1.t[:, 0:T], AF.Silu, r=[p1.key], w=[t1.key])
            else:
                sg = tf_ring.next()
                cx.act(sg.t[:, 0:T], p1.t[:, 0:T], AF.Sigmoid, r=[p1.key], w=[sg.key])
                t1 = tf_ring.next()
                cx.tt(t1.t[:, 0:T], p1.t[:, 0:T], sg.t[:, 0:T], ALU.mult, r=[p1.key, sg.key], w=[t1.key])
            cx.tt(actb.t[:, jj, 0:T], t1.t[:, 0:T], p2.t[:, 0:T], ALU.mult, r=[t1.key, p2.key], w=[actb.key, VIEW_B])
            bg()

        def wo_ffn(k):
            return ws.next(D_FF), 0

        cx.sec = 'Bffo'

        out_and_residual(l, t, st, ws, NJ, actb, wo_ffn, G2, bg)
        assert ws.pos == NBW, (ws.pos, NBW)
        so = cx.store(xdst(l, t).rearrange("(kc p) t -> p kc t", p=128)[:, :, t.t0:t.t0 + T], xt.t[:, :, 0:T],
                      r=[xt.key], w=[("x", t.kind, b, t.id)])
        if l == L - 1 and t.kind == "l":
            cx.finals.append(so)

    for l in range(L):
        drain_casts(10 ** 9)
        phase_mod(l)
        if l + 1 < L:
            queue_casts(l + 1)
        for b in range(NB):
            tl = tiles_for(b)
            switch_view(VIEW_A)
            sts = [dict() for _ in tl]
            drain(a1_gen(l, tl[0], sts[0]))
            for i, t in enumerate(tl):
                g = a1_gen(l, tl[i + 1], sts[i + 1]) if i + 1 < len(tl) else None
                phase_a2(l, t, sts[i], make_bg(g, 1))
                drain(g)
            attention(l, b)
            tlb = [t for t in tl if not (t.kind == "c" and l == L - 1)]
            switch_view(VIEW_B)
            sts = [dict() for _ in tlb]
            drain(b1_gen(l, tlb[0], sts[0]))
            for i, t in enumerate(tlb):
                b_merge(l, t, sts[i])
                g = b1_gen(l, tlb[i + 1], sts[i + 1]) if i + 1 < len(tlb) else None
                b_ffn(l, t, sts[i], make_bg(g, 2))
                drain(g)

    cx.finalize()
    with nc.Block() as block:
        cx.emit(block)
    stack.close()
    return nc, cx


def _cc(W, col0):
    K = W.shape[0]
    return np.ascontiguousarray(W[:, col0:col0 + 128].reshape(K // 128, 128, 128).transpose(1, 0, 2)).reshape(128, -1)


def _wide(W, col0, n):
    K = W.shape[0]
    return np.ascontiguousarray(W[:, col0:col0 + n].reshape(K // 128, 128, n).transpose(1, 0, 2)).reshape(128, -1)


def _vec(v):
    return np.ascontiguousarray(v.reshape(-1, 128).T)


def prep_shared(inp, L, S):
    f32 = np.float32
    wA = np.empty((L, 128, NA), f32)
    wB = np.empty((L, 128, NBW), f32)
    wM = np.empty((L, 128, NMW), f32)
    pvec = np.zeros((128, L * PV_L + PV_GLOB), f32)
    for l in range(L):
        w_in = inp["w_in"][l]
        parts = []
        for c in range(4):
            parts += [_cc(w_in, COL_A + c * 128), _cc(w_in, COL_A + CONV_CH + c * 128)]
        parts += [_cc(w_in, COL_Q + h * 128) for h in range(8)]
        parts += [_cc(w_in, COL_K + h * 128) for h in range(8)]
        parts += [_cc(w_in, COL_P + c * 128) for c in range(4)]
        parts += [_wide(w_in, COL_V, 512), _wide(w_in, COL_V + 512, 512)]
        wA[l] = np.concatenate(parts, axis=1)
        parts = [np.ascontiguousarray(inp["w_pool_group"][l].transpose(1, 0, 2)).reshape(128, 512)]
        for k in range(8):
            parts += [_cc(w_in, COL_G + br * 1024 + k * 128) for br in range(3)]
            parts += [_cc(inp["w_conv_out"][l], k * 128), _cc(inp["w_attn_out"][l], k * 128),
                      _cc(inp["w_pool_out"][l], k * 128)]
        parts += [_cc(inp["w_out"][l], k * 128) for k in range(8)]
        for jj in range(NJ):
            parts += [_cc(inp["w_ffn_in"][l], jj * 128), _cc(inp["w_ffn_in"][l], D_FF + jj * 128)]
        parts += [_cc(inp["w_ffn_out"][l], k * 128) for k in range(8)]
        wB[l] = np.concatenate(parts, axis=1)
        wM[l] = np.concatenate([_cc(inp["w_mod"][l], n * 128) for n in range(48)], axis=1)
        o = l * PV_L
        for i, nm in enumerate(("g_pre_mix", "g_post_mix", "g_pre_ffn", "g_post_ffn")):
            pvec[:, o + PV_G + 8 * i: o + PV_G + 8 * i + 8] = _vec(inp[nm][l])
        pvec[:, o + PV_BMOD:o + PV_BMOD + 48] = _vec(inp["b_mod"][l])
        pvec[:, o + PV_BGATE:o + PV_BGATE + 24] = _vec(inp["b_gate"][l])
        cw = inp["conv_w"][l]
        pvec[:, o + PV_CONVW:o + PV_CONVW + 124] = np.ascontiguousarray(
            cw.reshape(CONV_W, 4, 128).transpose(2, 1, 0)).reshape(128, 124)
        pvec[:, o + PV_CONVB:o + PV_CONVB + 4] = _vec(inp["conv_b"][l])
        pvec[:, o + PV_LNG:o + PV_LNG + 4] = _vec(inp["conv_ln_g"][l])
        pvec[:, o + PV_LNB:o + PV_LNB + 4] = _vec(inp["conv_ln_b"][l])
        pvec[:, o + PV_PSC:o + PV_PSC + 4] = _vec(inp["pool_scale"][l])
        pvec[:, o + PV_SUBG] = inp["subln_g"][l]
        for i, nm in enumerate(("lam_q1", "lam_k1", "lam_q2", "lam_k2")):
            pvec[:, o + PV_LAM + 64 * i: o + PV_LAM + 64 * (i + 1)] = inp[nm][l][None, :]
    og = L * PV_L
    for g in range(4):
        w = 2 << g
        half = w // 2
        for jx in range(8):
            cnt_f = min(jx + half, w)
            pvec[:, og + PV_CF + g * 8 + jx] = w / cnt_f
            dist = 8 - jx
            cnt_l = min(dist + half, w)
            pvec[:, og + PV_CL + g * 8 + jx] = w / cnt_l
    tpos = np.arange(S)
    row = (tpos // GRID_W).astype(f32)
    colp = (tpos % GRID_W).astype(f32)
    half = HD // 2
    inv = (np.float32(10000.0) ** (-np.arange(0, half, 2, dtype=f32) / np.float32(half))).astype(f32)
    rope = np.zeros((2, 128, S), f32)
    perm = np.zeros((128, 128), f32)
    for p in range(128):
        d = p % 64
        pos = row if d < 32 else colp
        dd = d % 32
        ang = (pos * inv[dd % 16]).astype(f32)
        rope[0, p] = np.cos(ang)
        if dd < 16:
            rope[1, p] = -np.sin(ang)
            partner = p + 16
        else:
            rope[1, p] = np.sin(ang)
            partner = p - 16
        perm[partner, p] = 1.0
    return dict(wA=wA, wB=wB, wM=wM, pvec=pvec, rope=rope, perm=perm)


def prep_core(inp, bs):
    NB = len(bs)
    xT = np.ascontiguousarray(np.stack([inp["x"][b].T for b in bs]))
    cxT = np.ascontiguousarray(np.stack([inp["ctx"][b].T for b in bs]))
    cv = np.stack([inp["c"][b] for b in bs] + [inp["c_ctx"]])
    cT = np.ascontiguousarray(cv.T.reshape(KC, 128, NB + 1).transpose(1, 0, 2)).reshape(128, KC * (NB + 1))
    return dict(xT=xT, cxT=cxT, cT=cT.astype(np.float32))


_CACHE = {}


def run(inp, n_cores, NB, L=None):
    inp = {k: np.asarray(v) for k, v in inp.items()}
    B, S, _ = inp["x"].shape
    CTX = inp["ctx"].shape[1]
    if L is None:
        L = inp["w_in"].shape[0]
    key = (L, NB, S, CTX)
    if key not in _CACHE:
        _CACHE[key] = build_program(L, NB, S, CTX)
    nc, cx = _CACHE[key]
    shared = prep_shared(inp, L, S)
    in_maps = []
    for i in range(n_cores):
        m = dict(shared)
        m.update(prep_core(inp, list(range(i * NB, (i + 1) * NB))))
        in_maps.append(m)
    res = run_bass_kernel_spmd(nc, in_maps, core_ids=list(range(n_cores)))
    out = np.empty((n_cores * NB, S, D), np.float32)
    for i in range(n_cores):
        o = res.results[i]["outT"]
        for jb in range(NB):
            out[i * NB + jb] = o[jb].T
    return out


def kernel(**inputs):
    return run(inputs, 8, 2)
```

```python
import math
from contextlib import ExitStack

import numpy as np
import concourse.bass as bass
import concourse.mybir as mybir
from concourse.bass_utils import run_bass_kernel_spmd

F32 = mybir.dt.float32
BF16 = mybir.dt.bfloat16
AF = mybir.ActivationFunctionType
ALU = mybir.AluOpType

D = 1024
KC = 8
GRID_W = 64
EPS = 1e-6
CONV_CH = 512
CONV_W = 31
NH = 8
HD = 64
D_FF = 2816
NJ = D_FF // 128
COL_A = 0
COL_Q = 1024
COL_K = 2048
COL_V = 3072
COL_P = 4096
COL_G = 4608
IN_W = 7680
TT = 512

NA = 28 * 1024 + 2 * 4096
NBK = 3 * 1024 + 512 + 1024 + 512
NBW = 512 + 8 * NBK + 8 * 1024 + 44 * 1024 + 8 * D_FF
NMW = 48 * 1024
WBUF = 5120

PV_G = 0
PV_BMOD = 32
PV_BGATE = 80
PV_CONVW = 104
PV_CONVB = 228
PV_LNG = 232
PV_LNB = 236
PV_PSC = 240
PV_SUBG = 244
PV_LAM = 245
PV_L = 501
PV_CF = 0
PV_CL = 32
PV_GLOB = 64


class Op:
    __slots__ = ("eng", "fn", "dma", "deps", "sig", "sem", "val", "waits", "pre", "tag")

    def __init__(self, eng, fn, dma):
        self.eng = eng
        self.fn = fn
        self.dma = dma
        self.deps = ()
        self.sig = dma
        self.sem = None
        self.val = 0
        self.waits = ()
        self.pre = None


ENGS = ("pe", "act", "dve", "pool", "sp")
import os as _os
EPOCH = 20000
DMA_NS = 8
DMA_MAXV = 30000
TRACE_TAGS = False
ZSPLIT = int(_os.environ.get('KZSPLIT', '512'))
USE_SILU = _os.environ.get('KSILU', '1') == '1'
CONV_DVE_TAPS = int(_os.environ.get('KCTAPS', '31'))
SAME_ENGINE_SYNC = _os.environ.get('KSES', '1') == '1'


class Cx:
    def __init__(self, nc, stack):
        self.nc = nc
        self.stack = stack
        self.ops = []
        self.sec = ""
        self.waitinfo = {}
        self.lastw = {}
        self.readers = {}
        self.nsem = 0
        self.finals = []

    def new_sem(self):
        self.nsem += 1
        return self.stack.enter_context(self.nc.semaphore("s%d" % self.nsem))

    def add(self, eng, fn, r=(), w=(), dma=False):
        op = Op(eng, fn, dma)
        op.tag = self.sec
        deps = {}
        lastw = self.lastw
        readers = self.readers
        for k in r:
            d = lastw.get(k)
            if d is not None:
                deps[id(d)] = d
            readers.setdefault(k, []).append(op)
        for k in w:
            d = lastw.get(k)
            if d is not None:
                deps[id(d)] = d
            rl = readers.get(k)
            if rl:
                for d in rl:
                    if d is not op:
                        deps[id(d)] = d
            lastw[k] = op
            readers[k] = []
        op.deps = tuple(deps.values())
        self.ops.append(op)
        return op

    def mm(self, out, lhsT, rhs, start, stop, r, w, tp=None):
        if tp is None:
            fn = lambda e: e.matmul(out, lhsT, rhs, start=start, stop=stop)
        else:
            fn = lambda e: e.matmul(out, lhsT, rhs, start=start, stop=stop, tile_position=tp)
        return self.add("pe", fn, r, w)

    def act(self, out, in_, func, r, w, bias=None, scale=None):
        kw = {}
        if bias is not None:
            kw["bias"] = bias
        if scale is not None:
            kw["scale"] = scale
        return self.add("act", lambda e: e.activation(out=out, in_=in_, func=func, **kw), r, w)

    def tt(self, out, in0, in1, op, r, w, eng="dve"):
        return self.add(eng, lambda e: e.tensor_tensor(out=out, in0=in0, in1=in1, op=op), r, w)

    def ts(self, out, in0, s1, s2, op0, op1, r, w, eng="dve"):
        if s2 is None:
            fn = lambda e: e.tensor_scalar(out=out, in0=in0, scalar1=s1, scalar2=None, op0=op0)
        else:
            fn = lambda e: e.tensor_scalar(out=out, in0=in0, scalar1=s1, scalar2=s2, op0=op0, op1=op1)
        return self.add(eng, fn, r, w)

    def stt(self, out, in0, scalar, in1, op0, op1, r, w, eng="dve"):
        return self.add(eng, lambda e: e.scalar_tensor_tensor(out=out, in0=in0, scalar=scalar, in1=in1,
                                                              op0=op0, op1=op1), r, w)

    def recip(self, out, in_, r, w):
        return self.add("dve", lambda e: e.reciprocal(out=out, in_=in_), r, w)

    def copy(self, out, in_, r, w, eng="dve"):
        return self.add(eng, lambda e: e.tensor_copy(out=out, in_=in_), r, w)

    def memset(self, ap, val, w, eng="dve"):
        return self.add(eng, lambda e: e.memset(ap, val), (), w)

    def load(self, out, in_, r, w):
        return self.add("sp", lambda e: e.dma_start(out=out, in_=in_), r, w, dma=True)

    def store(self, out, in_, r, w):
        return self.add("pool", lambda e: e.dma_start(out=out, in_=in_), r, w, dma=True)

    def finalize(self):
        ops = self.ops
        def needs(op, d):
            if d.dma or op.dma or d.eng != op.eng:
                return True
            return SAME_ENGINE_SYNC and op.eng != "pe"

        for op in ops:
            for d in op.deps:
                if needs(op, d):
                    d.sig = True
        cnt = {e: 0 for e in ENGS}
        csem = {e: None for e in ENGS}
        dpool = {e: [[self.new_sem(), 0] for _ in range(DMA_NS)] for e in ("sp", "pool")}
        dn = {"sp": 0, "pool": 0}
        for op in ops:
            if op.dma:
                pool = dpool[op.eng]
                j = dn[op.eng] % DMA_NS
                dn[op.eng] += 1
                ent = pool[j]
                if ent[1] > 0:
                    op.pre = (ent[0], ent[1])
                op.sem = ent[0]
                op.val = ent[1] + 16
                ent[1] = op.val
                if ent[1] > DMA_MAXV:
                    pool[j] = [self.new_sem(), 0]
            elif op.sig:
                e = op.eng
                if csem[e] is None or cnt[e] >= EPOCH:
                    csem[e] = self.new_sem()
                    cnt[e] = 0
                cnt[e] += 1
                op.sem = csem[e]
                op.val = cnt[e]
        waited = {e: {} for e in ENGS}
        nw = 0
        for op in ops:
            need = {}
            if op.pre is not None:
                need[id(op.pre[0])] = [op.pre[0], op.pre[1], None]
            for d in op.deps:
                if needs(op, d):
                    ent = need.get(id(d.sem))
                    if ent is None:
                        need[id(d.sem)] = [d.sem, d.val, d]
                    elif d.val > ent[1]:
                        ent[1] = d.val
                        ent[2] = d
            wl = []
            wd = waited[op.eng]
            for k, (s, v, dsrc) in need.items():
                if wd.get(k, 0) < v:
                    wd[k] = v
                    wl.append((s, v, dsrc))
            op.waits = wl
            nw += len(wl)
        self.n_waits = nw

    def emit(self, block):
        per = {e: [] for e in ENGS}
        for op in self.ops:
            per[op.eng].append(op)
        finals = self.finals

        def run(e, name):
            for op in per[name]:
                for (s, v, dsrc) in op.waits:
                    wi = e.wait_ge(s, v)
                    if TRACE_TAGS:
                        try:
                            self.waitinfo[wi.ins.name] = (op.tag, dsrc.tag + "@" + dsrc.eng if dsrc is not None else "dma-sem")
                        except Exception:
                            pass
                ins = op.fn(e)
                if TRACE_TAGS:
                    try:
                        self.waitinfo[ins.ins.name] = (op.tag, "op")
                    except Exception:
                        pass
                if op.sig:
                    ins.then_inc(op.sem, 16 if op.dma else 1)
            if name == "pool":
                for op in finals:
                    e.wait_ge(op.sem, op.val)

        @block.sync
        def _(e):
            run(e, "sp")

        @block.gpsimd
        def _(e):
            run(e, "pool")

        @block.vector
        def _(e):
            run(e, "dve")

        @block.scalar
        def _(e):
            run(e, "act")

        @block.tensor
        def _(e):
            run(e, "pe")


class Buf:
    __slots__ = ("t", "key")

    def __init__(self, t, key):
        self.t = t
        self.key = key


class Ring:
    def __init__(self, bufs):
        self.bufs = bufs
        self.i = 0

    def next(self):
        b = self.bufs[self.i % len(self.bufs)]
        self.i += 1
        return b


def build_program(L, NB, S, CTX):
    assert S % TT == 0 and CTX % 128 == 0 and CTX <= TT
    NC3 = NB + 1
    NLT = S // TT
    TOT = CTX + S
    NKT = TOT // 128
    NKC = CTX // 128
    NPV = L * PV_L + PV_GLOB
    UPAD = 15
    PPAD = 8

    nc = bass.Bass("TRN2", target_bir_lowering=False)
    dt_ = nc.dram_tensor
    xT_in = dt_("xT", [NB, D, S], F32, kind="ExternalInput").ap()
    cxT_in = dt_("cxT", [NB, D, CTX], F32, kind="ExternalInput").ap()
    cT_in = dt_("cT", [128, KC * NC3], F32, kind="ExternalInput").ap()
    pvec_in = dt_("pvec", [128, NPV], F32, kind="ExternalInput").ap()
    wA_in = dt_("wA", [L, 128, NA], F32, kind="ExternalInput").ap()
    wB_in = dt_("wB", [L, 128, NBW], F32, kind="ExternalInput").ap()
    wM_in = dt_("wM", [L, 128, NMW], F32, kind="ExternalInput").ap()
    rope_in = dt_("rope", [2, 128, S], F32, kind="ExternalInput").ap()
    perm_in = dt_("perm", [128, 128], F32, kind="ExternalInput").ap()
    outT = dt_("outT", [NB, D, S], F32, kind="ExternalOutput").ap()

    wA_bf = dt_("wA_bf", [L, 128, NA], BF16).ap()
    wB_bf = dt_("wB_bf", [L, 128, NBW], BF16).ap()
    wM_bf = dt_("wM_bf", [L, 128, NMW], BF16).ap()
    xs = dt_("xs", [NB, D, S], F32).ap()
    xcs = dt_("xcs", [NB, D, CTX], F32).ap()
    hT_d = dt_("hT_d", [NB, D, TOT], BF16).ap()
    ul_d = dt_("ul_d", [NB, CONV_CH, S + 2 * UPAD], F32).ap()
    uc_d = dt_("uc_d", [NB, CONV_CH, CTX + 2 * UPAD], F32).ap()
    pl_d = dt_("pl_d", [NB, CONV_CH, S + 2 * PPAD], F32).ap()
    pc_d = dt_("pc_d", [NB, CONV_CH, CTX + 2 * PPAD], F32).ap()
    QT_d = dt_("QT_d", [NB, NH, 128, TOT], BF16).ap()
    KT_d = dt_("KT_d", [NB, NH, 128, TOT], BF16).ap()
    V_d = dt_("V_d", [NB, NH, 128, NKT, 128], BF16).ap()
    OT_d = dt_("OT_d", [NB, D, TOT], BF16).ap()

    stack = ExitStack()
    cx = Cx(nc, stack)

    off = [(nc.sbuf_base + 63) // 64 * 64]
    top = nc.sbuf_top
    nbuf = [0]

    def alloc(shape, dtype, at=None):
        n = 1
        for s_ in shape[1:]:
            n *= s_
        nbytes = n * (4 if dtype == F32 else 2)
        nbytes = (nbytes + 63) // 64 * 64
        if at is None:
            o = off[0]
            off[0] += nbytes
            assert off[0] <= top, ("SBUF overflow", off[0], top)
        else:
            o = at
        nbuf[0] += 1
        t = nc.alloc_sbuf_tensor_at("b%d" % nbuf[0], list(shape), dtype, offset=o)
        return Buf(t, ("sb", nbuf[0])), o, nbytes

    def A(shape, dtype):
        return alloc(shape, dtype)[0]

    ones_f = A([128, 128], F32)
    ones_b = A([128, 128], BF16)
    perm_f = A([128, 128], F32)
    perm_b = A([128, 128], BF16)
    epsb = A([128, 1], F32)
    zerob = A([128, 1], F32)
    pvec = A([128, NPV], F32)
    cT = A([128, KC * NC3], F32)
    cs_b = A([128, KC * NC3], BF16)
    modb = A([128, 48 * NC3], F32)
    A1 = A([128, KC * NC3], F32)
    G1 = A([128, KC * NC3], F32)
    A2 = A([128, KC * NC3], F32)
    G2 = A([128, KC * NC3], F32)
    lamt = A([128, 8], F32)

    xt_ring = Ring([A([128, KC, TT], F32) for _ in range(2)])
    hT_ring = Ring([A([128, KC, TT], BF16) for _ in range(2)])
    w_ring = Ring([A([128, WBUF], BF16) for _ in range(3)])
    TFW = TT + 16
    tf_ring = Ring([A([128, TFW], F32) for _ in range(8)])
    lamtmp = tf_ring.bufs[0]
    rs_ring = Ring([A([128, TT], F32) for _ in range(2)])
    tb_ring = Ring([A([128, TT], BF16) for _ in range(4)])
    tf2_ring = Ring([A([128, TFW], F32) for _ in range(3)])
    bg_rs = A([128, TT], F32)
    wgb = A([128, 512], BF16)

    arena0 = off[0]
    o = arena0
    att = []
    for i in range(2):
        kt_, _, n1 = alloc([128, TOT], BF16, at=o); o += n1
        vt_, _, n2 = alloc([128, NKT, 128], BF16, at=o); o += n2
        qt_, _, n3 = alloc([128, TOT], BF16, at=o); o += n3
        att.append((kt_, vt_, qt_))
    pt_list = []
    for i in range(6):
        b_, _, n1 = alloc([128, 2, TT], BF16, at=o); o += n1
        pt_list.append(b_)
    pt_ring = Ring(pt_list)
    paccA, _, n1 = alloc([128, 2, TT], F32, at=o); o += n1
    paccB, _, n1 = alloc([128, 2, TT], F32, at=o); o += n1
    zsum, _, n1 = alloc([128, 2, TT], F32, at=o); o += n1
    rzb, _, n1 = alloc([128, 2, TT], F32, at=o); o += n1
    osb, _, n1 = alloc([128, 2, TT], F32, at=o); o += n1
    att_extra = [paccA, paccB, zsum, rzb, osb]
    att_extra_keys = [(paccA.key, 'd'), (paccA.key, 'p'), (paccB.key, 'd'), (paccB.key, 'p')]
    att_end = o
    o = arena0
    vtok_l = []
    for i in range(2):
        b_, _, n1 = alloc([128, D], BF16, at=o); o += n1
        vtok_l.append(b_)
    vtok_ring = Ring(vtok_l)
    rope_l = []
    for i in range(2):
        b_, _, n1 = alloc([128, 2, TT], F32, at=o); o += n1
        rope_l.append(b_)
    rope_ring = Ring(rope_l)
    pa_end = o
    o = arena0
    uwin, _, n1 = alloc([128, 4, TT + 2 * UPAD], F32, at=o); o += n1
    pwin, _, n1 = alloc([128, 4, TT + 2 * PPAD], F32, at=o); o += n1
    cvb, _, n1 = alloc([128, 4, TT], F32, at=o); o += n1
    sbuf_, _, n1 = alloc([128, 4, TT], BF16, at=o); o += n1
    mxb, _, n1 = alloc([128, 4, TT], BF16, at=o); o += n1
    pdb, _, n1 = alloc([128, 4, TT], BF16, at=o); o += n1
    mb, _, n1 = alloc([128, KC, TT], BF16, at=o); o += n1
    yb, _, n1 = alloc([128, KC, TT], F32, at=o); o += n1
    actb, o_act, n1 = alloc([128, NJ, TT], BF16, at=o); o += n1
    OTt, _, _ = alloc([128, KC, TT], BF16, at=o_act)
    OTt.key = actb.key
    lnm, _, n1 = alloc([128, TT], F32, at=o); o += n1
    if CONV_DVE_TAPS < CONV_W:
        cv2, _, n1 = alloc([128, TT], F32, at=o); o += n1
    else:
        cv2 = lnm
    pb_end = o
    arena_end = max(att_end, pa_end, pb_end)
    assert arena_end <= top, ("SBUF overflow arena", arena_end, top)
    ARENA = ("arena",)
    VIEW_ATT, VIEW_A, VIEW_B = ("view", "att"), ("view", "a"), ("view", "b")

    class HalfView:
        def __init__(self, t3, h):
            self.t3, self.h = t3, h

        def __getitem__(self, idx):
            r_, c_ = idx
            return self.t3[r_, self.h, c_]

    pp = [Buf(stack.enter_context(nc.psum_tensor("pp%d" % i, [128, 2, TT], F32)), ("pp", i)) for i in range(4)]
    ps = [Buf(HalfView(pp[i // 2].t, i % 2), ("ps", i)) for i in range(8)]
    ps_ring = Ring(ps[0:7])

    cur_view = [None]
    view_keys = {
        VIEW_ATT: [b.key for trio in att for b in trio] + [b.key for b in pt_list] + [b.key for b in att_extra] + att_extra_keys,
        VIEW_A: [b.key for b in vtok_l] + [b.key for b in rope_l],
        VIEW_B: [b.key for b in (uwin, pwin, cvb, sbuf_, mxb, pdb, mb, yb, actb, lnm)],
    }

    def switch_view(v):
        if cur_view[0] == v:
            return
        old = cur_view[0]
        cur_view[0] = v
        if old is None:
            return
        keys = view_keys[old] + view_keys[v]
        cx.memset(lamt.t[:, 7:8], 0.0, w=keys + [("fence",)])

    cx.load(pvec.t[:, :], pvec_in[:, :], r=[], w=[pvec.key])
    cx.load(cT.t[:, :], cT_in[:, :], r=[], w=[cT.key])
    cx.load(perm_f.t[:, :], perm_in[:, :], r=[], w=[perm_f.key])
    cx.memset(ones_f.t[:, :], 1.0, w=[ones_f.key])
    cx.memset(ones_b.t[:, :], 1.0, w=[ones_b.key])
    cx.memset(epsb.t[:, :], EPS, w=[epsb.key])
    cx.memset(zerob.t[:, :], 0.0, w=[zerob.key])
    cx.copy(perm_b.t[:, :], perm_f.t[:, :], r=[perm_f.key], w=[perm_b.key])
    tf = tf_ring.next()
    cx.act(tf.t[:, 0:KC * NC3], cT.t[:, :], AF.Sigmoid, r=[cT.key], w=[tf.key])
    cx.tt(cs_b.t[:, :], cT.t[:, :], tf.t[:, 0:KC * NC3], ALU.mult, r=[cT.key, tf.key], w=[cs_b.key])
    zt = tf_ring.next()
    cx.memset(zt.t[:, :], 0.0, w=[zt.key])
    for b in range(NB):
        for (dd, n_, pad, nm) in ((ul_d, S, UPAD, "ul"), (uc_d, CTX, UPAD, "uc"), (pl_d, S, PPAD, "pl"),
                                  (pc_d, CTX, PPAD, "pc")):
            v = dd[b].rearrange("(c p) t -> p c t", p=128)
            cx.store(v[:, :, 0:pad], zt.t[:, 0:4 * pad].rearrange("p (c t) -> p c t", c=4),
                     r=[zt.key], w=[(nm + "pad", b, 0)])
            cx.store(v[:, :, pad + n_:pad + n_ + pad], zt.t[:, 0:4 * pad].rearrange("p (c t) -> p c t", c=4),
                     r=[zt.key], w=[(nm + "pad", b, 1)])
    CW = 8192
    cast_q = []

    def queue_casts(l):
        for (src, dst, n_, nm) in ((wM_in, wM_bf, NMW, "wM"), (wA_in, wA_bf, NA, "wA"), (wB_in, wB_bf, NBW, "wB")):
            c0 = 0
            while c0 < n_:
                c1 = min(n_, c0 + CW)
                cast_q.append((dst[l][:, c0:c1], src[l][:, c0:c1], (nm, l, c0 // CW)))
                c0 = c1

    def drain_casts(n):
        while cast_q and n > 0:
            d_, s_, k_ = cast_q.pop(0)
            cx.store(d_, s_, r=[], w=[k_])
            n -= 1

    queue_casts(0)
    drain_casts(10 ** 9)

    def wkeys(nm, l, c0, c1):
        return [(nm, l, i) for i in range(c0 // CW, (c1 - 1) // CW + 1)]

    class WStream:
        def __init__(self, nm, dram, l):
            self.nm, self.dram, self.l, self.pos = nm, dram, l, 0

        def next(self, n):
            b = w_ring.next()
            c0, c1 = self.pos, self.pos + n
            cx.load(b.t[:, 0:n], self.dram[self.l][:, c0:c1], r=wkeys(self.nm, self.l, c0, c1), w=[b.key])
            self.pos = c1
            return b

    def pv(l, o_, n=1):
        return pvec.t[:, l * PV_L + o_: l * PV_L + o_ + n]

    def col(bufap, kc, j):
        return bufap.t[:, kc * NC3 + j: kc * NC3 + j + 1]

    def modcol(n, j):
        return modb.t[:, n * NC3 + j: n * NC3 + j + 1]

    def phase_mod(l):
        lam_init = 0.8 - 0.6 * math.exp(-0.3 * l)
        ws = WStream("wM", wM_bf, l)
        n = 0
        while n < 48:
            g = min(5, 48 - n)
            wb = ws.next(g * 1024)
            for i in range(g):
                p_ = ps_ring.next()
                for kc in range(KC):
                    cx.mm(p_.t[:, 0:NC3], wb.t[:, i * 1024 + kc * 128: i * 1024 + (kc + 1) * 128],
                          cs_b.t[:, kc * NC3:(kc + 1) * NC3], kc == 0, kc == KC - 1,
                          r=[wb.key, cs_b.key], w=[p_.key])
                cx.ts(modb.t[:, (n + i) * NC3:(n + i + 1) * NC3], p_.t[:, 0:NC3], pv(l, PV_BMOD + n + i), None,
                      ALU.add, None, r=[p_.key, pvec.key], w=[modb.key])
            n += g
        for kc in range(KC):
            sl = slice(kc * NC3, (kc + 1) * NC3)
            cx.ts(A1.t[:, sl], modb.t[:, (8 + kc) * NC3:(9 + kc) * NC3], pv(l, PV_G + 0 + kc), pv(l, PV_G + 0 + kc), ALU.mult, ALU.add,
                  r=[modb.key, pvec.key], w=[A1.key])
            cx.ts(G1.t[:, sl], modb.t[:, (16 + kc) * NC3:(17 + kc) * NC3], pv(l, PV_G + 8 + kc), None, ALU.mult, None,
                  r=[modb.key, pvec.key], w=[G1.key])
            cx.ts(A2.t[:, sl], modb.t[:, (32 + kc) * NC3:(33 + kc) * NC3], pv(l, PV_G + 16 + kc), pv(l, PV_G + 16 + kc), ALU.mult, ALU.add,
                  r=[modb.key, pvec.key], w=[A2.key])
            cx.ts(G2.t[:, sl], modb.t[:, (40 + kc) * NC3:(41 + kc) * NC3], pv(l, PV_G + 24 + kc), None, ALU.mult, None,
                  r=[modb.key, pvec.key], w=[G2.key])
        for i in range(2):
            cx.tt(lamtmp.t[:, 0:64], pv(l, PV_LAM + 128 * i, 64), pv(l, PV_LAM + 128 * i + 64, 64), ALU.mult,
                  r=[pvec.key], w=[lamtmp.key])
            cx.add("dve", (lambda i_: lambda e: e.reduce_sum(out=lamt.t[:, 5 + i_:6 + i_], in_=lamtmp.t[:, 0:64],
                                                             axis=mybir.AxisListType.X))(i),
                   r=[lamtmp.key], w=[lamt.key])
        cx.act(lamt.t[:, 0:2], lamt.t[:, 5:7], AF.Exp, r=[lamt.key], w=[lamt.key])
        cx.tt(lamt.t[:, 2:3], lamt.t[:, 0:1], lamt.t[:, 1:2], ALU.subtract, r=[lamt.key], w=[lamt.key])
        cx.ts(lamt.t[:, 3:4], lamt.t[:, 2:3], -1.0, -lam_init, ALU.mult, ALU.add, r=[lamt.key], w=[lamt.key])
        cx.ts(lamt.t[:, 4:5], pv(l, PV_SUBG), 1.0 - lam_init, None, ALU.mult, None, r=[pvec.key], w=[lamt.key])

    class Tile:
        pass

    def tiles_for(b):
        res = []
        t = Tile()
        t.kind, t.b, t.id, t.T, t.t0, t.c0, t.j = "c", b, "c", CTX, 0, 0, NB
        t.first, t.last = True, True
        res.append(t)
        for i in range(NLT):
            t = Tile()
            t.kind, t.b, t.id, t.T, t.t0, t.c0, t.j = "l", b, i, TT, i * TT, CTX + i * TT, b
            t.first, t.last = (i == 0), (i == NLT - 1)
            res.append(t)
        return res

    def xsrc(l, t):
        if t.kind == "c":
            return (cxT_in if l == 0 else xcs)[t.b]
        return (xT_in if l == 0 else xs)[t.b]

    def xdst(l, t):
        if t.kind == "c":
            return xcs[t.b]
        return (outT if l == L - 1 else xs)[t.b]

    def rstd_from_ps(p_, T, n):
        sd = tf_ring.next()
        cx.act(sd.t[:, 0:T], p_.t[:, 0:T], AF.Ln, r=[p_.key, epsb.key], w=[sd.key],
               bias=epsb.t[:, 0:1], scale=1.0 / n)
        rs = rs_ring.next()
        cx.act(rs.t[:, 0:T], sd.t[:, 0:T], AF.Exp, r=[sd.key], w=[rs.key], scale=-0.5)
        return rs

    def sumsq_stats(src, T, nchunks):
        p_ = ps_ring.next()
        for c in range(nchunks):
            sq = tf_ring.next()
            cx.act(sq.t[:, 0:T], src.t[:, c, 0:T], AF.Square, r=[src.key], w=[sq.key])
            cx.mm(p_.t[:, 0:T], ones_f.t[:, :], sq.t[:, 0:T], c == 0, c == nchunks - 1,
                  r=[ones_f.key, sq.key], w=[p_.key])
        return p_

    def adaln(xt, T, j, Asc, shift_n0, hT):
        p_ = ps_ring.next()
        for c in range(KC):
            cx.act(hT.t[:, c, 0:T], xt.t[:, c, 0:T], AF.Square, r=[xt.key], w=[hT.key])
        for c in range(KC):
            cx.mm(p_.t[:, 0:T], ones_b.t[:, :], hT.t[:, c, 0:T], c == 0, c == KC - 1,
                  r=[ones_b.key, hT.key], w=[p_.key])
        rs = rstd_from_ps(p_, T, D)
        for kc in range(KC):
            t1 = tf_ring.next()
            cx.stt(t1.t[:, 0:T], xt.t[:, kc, 0:T], col(Asc, kc, j), rs.t[:, 0:T], ALU.mult, ALU.mult,
                   r=[xt.key, Asc.key, rs.key], w=[t1.key])
            cx.act(hT.t[:, kc, 0:T], t1.t[:, 0:T], AF.Identity, r=[t1.key, modb.key], w=[hT.key],
                   bias=modcol(shift_n0 + kc, j))

    def adaln_bg(xt, T, j, Asc, shift_n0, hT):
        p_ = ps[7]
        for c in range(KC):
            cx.act(hT.t[:, c, 0:T], xt.t[:, c, 0:T], AF.Square, r=[xt.key], w=[hT.key])
            if c % 2 == 1:
                yield
        yield
        for c in range(KC):
            cx.mm(p_.t[:, 0:T], ones_b.t[:, :], hT.t[:, c, 0:T], c == 0, c == KC - 1,
                  r=[ones_b.key, hT.key], w=[p_.key])
        sd = tf2_ring.next()
        cx.act(sd.t[:, 0:T], p_.t[:, 0:T], AF.Ln, r=[p_.key, epsb.key], w=[sd.key], bias=epsb.t[:, 0:1], scale=1.0 / D)
        rs = bg_rs
        cx.act(rs.t[:, 0:T], sd.t[:, 0:T], AF.Exp, r=[sd.key], w=[rs.key], scale=-0.5)
        yield
        for kc in range(KC):
            t1 = tf2_ring.next()
            cx.stt(t1.t[:, 0:T], xt.t[:, kc, 0:T], col(Asc, kc, j), rs.t[:, 0:T], ALU.mult, ALU.mult,
                   r=[xt.key, Asc.key, rs.key], w=[t1.key])
            cx.act(hT.t[:, kc, 0:T], t1.t[:, 0:T], AF.Identity, r=[t1.key, modb.key], w=[hT.key],
                   bias=modcol(shift_n0 + kc, j))
            yield

    def drain(g):
        if g is not None:
            for _ in g:
                pass

    def make_bg(g, n):
        def bg():
            if g is None:
                return
            for _ in range(n):
                try:
                    next(g)
                except StopIteration:
                    return
        return bg

    def proj(wb, woff, nk, rhs_buf, T, extra_r=()):
        p_ = ps_ring.next()
        for kc in range(nk):
            cx.mm(p_.t[:, 0:T], wb.t[:, woff + kc * 128: woff + (kc + 1) * 128], rhs_buf.t[:, kc, 0:T],
                  kc == 0, kc == nk - 1, r=[wb.key, rhs_buf.key] + list(extra_r), w=[p_.key])
        return p_

    def a1_gen(l, t, st):
        drain_casts(1)
        cx.sec = 'A1'
        b, T, j = t.b, t.T, t.j
        xt = xt_ring.next()
        cx.load(xt.t[:, :, 0:T], xsrc(l, t).rearrange("(kc p) t -> p kc t", p=128)[:, :, t.t0:t.t0 + T],
                r=[("x", t.kind, b, t.id)], w=[xt.key])
        hT = hT_ring.next()
        st["hT"] = hT
        yield
        for _ in adaln_bg(xt, T, j, A1, 0, hT):
            yield
            cx.sec = 'A1'
        cx.store(hT_d[b].rearrange("(kc p) t -> p kc t", p=128)[:, :, t.c0:t.c0 + T], hT.t[:, :, 0:T],
                 r=[hT.key], w=[("hT", b, t.id)])
        yield

    def phase_a2(l, t, st, bg_):
        def bg():
            bg_()
            cx.sec = 'A2'
        cx.sec = 'A2'
        b, T, j = t.b, t.T, t.j
        hT = st["hT"]
        if t.kind == "l":
            rp = rope_ring.next()
            cx.load(rp.t[:, :, 0:T], rope_in.rearrange("a p t -> p a t")[:, :, t.t0:t.t0 + T], r=[], w=[rp.key, VIEW_A])
        ws = WStream("wA", wA_bf, l)
        u_d = (uc_d if t.kind == "c" else ul_d)[b]
        unm = "uc" if t.kind == "c" else "ul"
        for half in range(2):
            wb = ws.next(4096)
            for ci in range(2):
                c = half * 2 + ci
                pa = proj(wb, (2 * ci) * 1024, KC, hT, T)
                pb = proj(wb, (2 * ci + 1) * 1024, KC, hT, T)
                sg = tf_ring.next()
                cx.act(sg.t[:, 0:T], pb.t[:, 0:T], AF.Sigmoid, r=[pb.key], w=[sg.key])
                u = tf_ring.next()
                cx.tt(u.t[:, 0:T], pa.t[:, 0:T], sg.t[:, 0:T], ALU.mult, r=[pa.key, sg.key], w=[u.key])
                cx.store(u_d[c * 128:(c + 1) * 128, UPAD + t.t0: UPAD + t.t0 + T], u.t[:, 0:T],
                         r=[u.key], w=[(unm, b, t.id)])
                bg()
        rope_pend = []
        for (dst, nm) in ((QT_d, "Q"), (KT_d, "K")):
            for half in range(2):
                wb = ws.next(4096)
                for hi in range(4):
                    h = half * 4 + hi
                    pq = proj(wb, hi * 1024, KC, hT, T)
                    qb = tb_ring.next()
                    cx.act(qb.t[:, 0:T], pq.t[:, 0:T], AF.Identity, r=[pq.key, zerob.key],
                           w=[qb.key, ("lock", pq.key)], bias=zerob.t[:, 0:1])
                    if t.kind == "l":
                        t1 = tf_ring.next()
                        cx.tt(t1.t[:, 0:T], pq.t[:, 0:T], rp.t[:, 0, 0:T], ALU.mult,
                              r=[pq.key, rp.key, ("lock", pq.key)], w=[t1.key])

                        def rope_tail(qb=qb, t1=t1, dst=dst, nm=nm, h=h):
                            psw = ps_ring.next()
                            cx.mm(psw.t[:, 0:T], perm_b.t[:, :], qb.t[:, 0:T], True, True,
                                  r=[perm_b.key, qb.key], w=[psw.key])
                            t2 = tf_ring.next()
                            cx.tt(t2.t[:, 0:T], psw.t[:, 0:T], rp.t[:, 1, 0:T], ALU.mult, r=[psw.key, rp.key], w=[t2.key])
                            qr = tb_ring.next()
                            cx.tt(qr.t[:, 0:T], t1.t[:, 0:T], t2.t[:, 0:T], ALU.add, r=[t1.key, t2.key], w=[qr.key], eng="pool")
                            cx.store(dst[b, h][:, t.c0:t.c0 + T], qr.t[:, 0:T], r=[qr.key], w=[(nm, b, h, t.id)])

                        if rope_pend:
                            rope_pend.pop(0)()
                        rope_pend.append(rope_tail)
                    else:
                        cx.store(dst[b, h][:, t.c0:t.c0 + T], qb.t[:, 0:T], r=[qb.key], w=[(nm, b, h, t.id)])
                    bg()
        while rope_pend:
            rope_pend.pop(0)()
        p_d = (pc_d if t.kind == "c" else pl_d)[b]
        pnm = "pc" if t.kind == "c" else "pl"
        wb = ws.next(4096)
        for c in range(4):
            pp = proj(wb, c * 1024, KC, hT, T)
            pf = tf_ring.next()
            cx.copy(pf.t[:, 0:T], pp.t[:, 0:T], r=[pp.key], w=[pf.key])
            cx.store(p_d[c * 128:(c + 1) * 128, PPAD + t.t0: PPAD + t.t0 + T], pf.t[:, 0:T],
                     r=[pf.key], w=[(pnm, b, t.id)])
            bg()
        wv = [ws.next(4096), ws.next(4096)]
        for tsi in range(T // 128):
            vt = vtok_ring.next()
            for nh in range(2):
                p_ = ps_ring.next()
                for kc in range(KC):
                    cx.mm(p_.t[:, :], hT.t[:, kc, tsi * 128:(tsi + 1) * 128], wv[nh].t[:, kc * 512:(kc + 1) * 512],
                          kc == 0, kc == KC - 1, r=[hT.key, wv[nh].key], w=[p_.key])
                if nh == 0:
                    cx.act(vt.t[:, 0:512], p_.t[:, :], AF.Identity, r=[p_.key, zerob.key], w=[vt.key, VIEW_A], bias=zerob.t[:, 0:1])
                else:
                    cx.copy(vt.t[:, 512:1024], p_.t[:, :], r=[p_.key], w=[vt.key, VIEW_A])
            kt = t.c0 // 128 + tsi
            cx.store(V_d[b].rearrange("h p k d -> p h k d")[:, :, kt, :], vt.t[:, :].rearrange("p (h d) -> p h d", h=NH),
                     r=[vt.key], w=[("V", b, kt)])

    def attention(l, b):
        switch_view(VIEW_ATT)
        cx.sec = 'ATT'
        need_ctx = l < L - 1
        tl = tiles_for(b)
        all_ids = [t.id for t in tl]
        spairs = Ring(pp[0:3])
        O0, O1 = ps[6], ps[7]
        deferred = []

        def hk(p_):
            i_ = int(p_.key[1])
            return [("ps", 2 * i_), ("ps", 2 * i_ + 1)]

        def tick(allow=True):
            for d_ in deferred:
                d_[0] -= 1
            if allow:
                for d_ in deferred:
                    if d_[0] <= 0:
                        deferred.remove(d_)
                        d_[1]()
                        break

        def flush():
            while deferred:
                deferred.pop(0)[1]()

        def flush_p2():
            pend = [d_ for d_ in deferred if d_[2] == 2]
            for d_ in pend:
                deferred.remove(d_)
                d_[1]()

        def part2(h, t):
            T = t.T
            zp = spairs.next()
            for i in range(2):
                cx.mm(zp.t[:, i, 0:T], ones_f.t[:, :], zsum.t[:, i, 0:T], True, True,
                      r=[ones_f.key, zsum.key], w=[hk(zp)[i]])
            cx.act(rzb.t[:, :, 0:T], zp.t[:, :, 0:T], AF.Ln, r=hk(zp), w=[rzb.key])
            cx.act(rzb.t[:, :, 0:T], rzb.t[:, :, 0:T], AF.Exp, r=[rzb.key], w=[rzb.key], scale=-1.0)
            t0_ = tf_ring.next()
            cx.tt(t0_.t[:, 0:T], osb.t[:, 0, 0:T], rzb.t[:, 0, 0:T], ALU.mult, r=[osb.key, rzb.key], w=[t0_.key])
            t1_ = tf_ring.next()
            cx.tt(t1_.t[:, 0:T], osb.t[:, 1, 0:T], rzb.t[:, 1, 0:T], ALU.mult, r=[osb.key, rzb.key], w=[t1_.key])
            o_ = tf_ring.next()
            cx.stt(o_.t[:, 0:T], t1_.t[:, 0:T], lamt.t[:, 3:4], t0_.t[:, 0:T], ALU.mult, ALU.add,
                   r=[t1_.key, lamt.key, t0_.key], w=[o_.key])
            osq = tf_ring.next()
            cx.tt(osq.t[:, 0:T], o_.t[:, 0:T], o_.t[:, 0:T], ALU.mult, r=[o_.key], w=[osq.key])
            deferred.append([int(_os.environ.get('KT3', '8')), lambda: part3(h, t, o_, osq), 3])

        def part3(h, t, o_, osq):
            T = t.T
            zp = spairs.next()
            cx.mm(zp.t[:, 0, 0:T], ones_f.t[:, :], osq.t[:, 0:T], True, True, r=[ones_f.key, osq.key],
                  w=[hk(zp)[0]])
            sd = tf_ring.next()
            cx.act(sd.t[:, 0:T], zp.t[:, 0, 0:T], AF.Ln, r=[hk(zp)[0], epsb.key], w=[sd.key],
                   bias=epsb.t[:, 0:1], scale=1.0 / 128)
            rs = rs_ring.next()
            cx.act(rs.t[:, 0:T], sd.t[:, 0:T], AF.Exp, r=[sd.key], w=[rs.key], scale=-0.5)
            ob = tb_ring.next()
            cx.stt(ob.t[:, 0:T], o_.t[:, 0:T], lamt.t[:, 4:5], rs.t[:, 0:T], ALU.mult, ALU.mult,
                   r=[o_.key, lamt.key, rs.key], w=[ob.key])
            cx.store(OT_d[b][h * 128:(h + 1) * 128, t.c0:t.c0 + T], ob.t[:, 0:T], r=[ob.key], w=[("OT", b, h, t.id)])

        for h in range(NH):
            KT, VT, QT = att[h % 2]
            cx.load(KT.t[:, :], KT_d[b, h][:, :], r=[("K", b, h, i) for i in all_ids], w=[KT.key, VIEW_ATT])
            cx.load(VT.t[:, :, :], V_d[b, h][:, :, :], r=[("V", b, k) for k in range(NKT)], w=[VT.key, VIEW_ATT])
            cx.load(QT.t[:, :], QT_d[b, h][:, :], r=[("Q", b, h, i) for i in all_ids], w=[QT.key, VIEW_ATT])
            tls = [t for t in tl if not (t.kind == "c" and not need_ctx)]
            steps = []
            for t in tls:
                nk = NKC if t.kind == "c" else NKT
                for kt in range(nk):
                    steps.append((t, kt, nk))

            def qk(si):
                t_, kt_, _ = steps[si]
                T_ = t_.T
                sp = spairs.next()
                ks = slice(kt_ * 128, (kt_ + 1) * 128)
                qs_ = slice(t_.c0, t_.c0 + T_)
                cx.mm(sp.t[:, 0, 0:T_], KT.t[0:64, ks], QT.t[0:64, qs_], True, True, r=[KT.key, QT.key],
                      w=[hk(sp)[0]], tp=(0, 0))
                cx.mm(sp.t[:, 1, 0:T_], KT.t[64:128, ks], QT.t[64:128, qs_], True, True, r=[KT.key, QT.key],
                      w=[hk(sp)[1]], tp=(64, 0))
                return sp

            XL = _os.environ.get('KXL', '0') == '1'
            sq_ = []
            if XL:
                sq_ = [qk(0)]
                if len(steps) > 1:
                    sq_.append(qk(1))
            for si, (t, kt, nk) in enumerate(steps):
                T = t.T
                if not XL and kt == 0:
                    sq_ = [qk(si)]
                    if nk > 1:
                        sq_.append(qk(si + 1))
                if TRACE_TAGS == 2:
                    cx.sec = 'ATT.l%d.b%d.h%d.%s.%d' % (l, b, h, t.id, kt)
                sp = sq_.pop(0)
                pt = pt_ring.next()
                cx.act(pt.t[:, :, 0:T], sp.t[:, :, 0:T], AF.Exp, r=hk(sp), w=[pt.key], scale=0.125)
                if (si + 2 < len(steps)) if XL else (kt + 2 < nk):
                    sq_.append(qk(si + 2))
                st_, sp_ = (kt == 0), (kt == nk - 1)
                cx.mm(O0.t[:, 0:T], VT.t[:, kt, :], pt.t[:, 0, 0:T], st_, sp_,
                      r=[VT.key, pt.key], w=[O0.key])
                cx.mm(O1.t[:, 0:T], VT.t[:, kt, :], pt.t[:, 1, 0:T], st_, sp_,
                      r=[VT.key, pt.key], w=[O1.key])
                acc = paccA if kt % 2 == 0 else paccB
                TD = T if T < TT else ZSPLIT
                for (eng_, c0_, c1_, kx) in (("dve", 0, TD, "d"), ("pool", TD, T, "p")):
                    if c1_ <= c0_:
                        continue
                    if kt < 2:
                        cx.copy(acc.t[:, :, c0_:c1_], pt.t[:, :, c0_:c1_], r=[pt.key], w=[(acc.key, kx)], eng=eng_)
                    else:
                        cx.tt(acc.t[:, :, c0_:c1_], acc.t[:, :, c0_:c1_], pt.t[:, :, c0_:c1_], ALU.add,
                              r=[(acc.key, kx), pt.key], w=[(acc.key, kx)], eng=eng_)
                tick(allow=(kt < nk - 1))
                if kt < nk - 1:
                    continue
                flush_p2()
                cx.act(osb.t[:, :, 0:T], pp[3].t[:, :, 0:T], AF.Identity, r=[O0.key, O1.key, zerob.key],
                       w=[osb.key], bias=zerob.t[:, 0:1])
                ak = [(paccA.key, "d"), (paccA.key, "p"), (paccB.key, "d"), (paccB.key, "p")]
                if nk > 1:
                    cx.tt(zsum.t[:, :, 0:T], paccA.t[:, :, 0:T], paccB.t[:, :, 0:T], ALU.add,
                          r=ak, w=[zsum.key])
                else:
                    cx.copy(zsum.t[:, :, 0:T], paccA.t[:, :, 0:T], r=ak, w=[zsum.key])
                deferred.append([int(_os.environ.get('KT2', '4')), (lambda h_, t_: lambda: part2(h_, t_))(h, t), 2])
        flush()

    def b1_gen(l, t, st):
        for _ in b1_gen_(l, t, st):
            yield
            cx.sec = 'B1'

    def b1_gen_(l, t, st):
        cx.sec = 'B1'
        drain_casts(1)
        b, T, j = t.b, t.T, t.j
        VB = VIEW_B
        hT = hT_ring.next()
        cx.load(hT.t[:, :, 0:T], hT_d[b].rearrange("(kc p) t -> p kc t", p=128)[:, :, t.c0:t.c0 + T],
                r=[("hT", b, t.id)], w=[hT.key])
        xt = xt_ring.next()
        cx.load(xt.t[:, :, 0:T], xsrc(l, t).rearrange("(kc p) t -> p kc t", p=128)[:, :, t.t0:t.t0 + T],
                r=[("x", t.kind, b, t.id)], w=[xt.key])
        st["hT"], st["xt"] = hT, xt
        if t.kind == "c":
            u_d, p_d, unm, pnm = uc_d[b], pc_d[b], "uc", "pc"
            nb_ids = ["c"]
        else:
            u_d, p_d, unm, pnm = ul_d[b], pl_d[b], "ul", "pl"
            nb_ids = [i for i in (t.id - 1, t.id, t.id + 1) if 0 <= i < NLT]
        cx.load(uwin.t[:, :, 0:T + 2 * UPAD], u_d.rearrange("(c p) t -> p c t", p=128)[:, :, t.t0:t.t0 + T + 2 * UPAD],
                r=[(unm, b, i) for i in nb_ids] + [(unm + "pad", b, 0), (unm + "pad", b, 1)], w=[uwin.key, VB])
        cx.load(pwin.t[:, :, 0:T + 2 * PPAD], p_d.rearrange("(c p) t -> p c t", p=128)[:, :, t.t0:t.t0 + T + 2 * PPAD],
                r=[(pnm, b, i) for i in nb_ids] + [(pnm + "pad", b, 0), (pnm + "pad", b, 1)], w=[pwin.key, VB])
        cx.load(wgb.t[:, :], wB_bf[l][:, 0:512], r=wkeys("wB", l, 0, 512), w=[wgb.key])
        yield
        for c in range(4):
            acc = cvb.t[:, c, 0:T]
            acc2 = cv2.t[:, 0:T]
            cw0 = PV_CONVW + c * CONV_W
            ND = CONV_DVE_TAPS
            cx.ts(acc, uwin.t[:, c, 0:T], pv(l, cw0), pv(l, PV_CONVB + c), ALU.mult, ALU.add,
                  r=[uwin.key, pvec.key], w=[(cvb.key, c), cvb.key, VB])
            if ND < CONV_W:
                cx.ts(acc2, uwin.t[:, c, ND:ND + T], pv(l, cw0 + ND), None, ALU.mult, None,
                      r=[uwin.key, pvec.key], w=[cv2.key], eng="pool")
            kd, kp = 1, ND + 1
            while kd < ND or kp < CONV_W:
                for _ in range(2):
                    if kd < ND:
                        cx.stt(acc, uwin.t[:, c, kd:kd + T], pv(l, cw0 + kd), acc, ALU.mult, ALU.add,
                               r=[uwin.key, pvec.key, (cvb.key, c)], w=[(cvb.key, c)])
                        kd += 1
                if kp < CONV_W:
                    cx.stt(acc2, uwin.t[:, c, kp:kp + T], pv(l, cw0 + kp), acc2, ALU.mult, ALU.add,
                           r=[uwin.key, pvec.key, cv2.key], w=[cv2.key], eng="pool")
                    kp += 1
                yield
            if ND < CONV_W:
                cx.tt(acc, acc, acc2, ALU.add, r=[(cvb.key, c), cv2.key], w=[(cvb.key, c)])
            cx.memset(lamt.t[:, 7:8], 0.0, w=[cvb.key] + [(cvb.key, c)])
            yield
        W2 = T + 2 * PPAD
        for g in range(4):
            wwin = 2 << g
            pw = pwin.t[:, g, :]
            cur = tf2_ring.next()
            cx.tt(cur.t[:, 1:W2], pw[:, 0:W2 - 1], pw[:, 1:W2], ALU.add, r=[pwin.key], w=[cur.key])
            lo, hi = 1, W2
            sh = 1
            for step in range(g):
                nx = tf2_ring.next()
                cx.tt(nx.t[:, lo + sh:hi - sh], cur.t[:, lo:hi - 2 * sh], cur.t[:, lo + 2 * sh:hi], ALU.add,
                      r=[cur.key], w=[nx.key])
                lo, hi = lo + sh, hi - sh
                cur = nx
                sh *= 2
            ctr = cur.t[:, PPAD:PPAD + T]
            gpv = pvec.t[:, L * PV_L + PV_CF + g * 8: L * PV_L + PV_CF + g * 8 + 8]
            gpl = pvec.t[:, L * PV_L + PV_CL + g * 8: L * PV_L + PV_CL + g * 8 + 8]
            if t.first:
                cx.tt(cur.t[:, PPAD:PPAD + 8], cur.t[:, PPAD:PPAD + 8], gpv, ALU.mult, r=[cur.key, pvec.key], w=[cur.key])
            if t.last:
                cx.tt(cur.t[:, PPAD + T - 8:PPAD + T], cur.t[:, PPAD + T - 8:PPAD + T], gpl, ALU.mult,
                      r=[cur.key, pvec.key], w=[cur.key])
            cx.stt(pdb.t[:, g, 0:T], ctr, 1.0 / wwin, pw[:, PPAD:PPAD + T], ALU.mult, ALU.subtract,
                   r=[cur.key, pwin.key], w=[pdb.key, VB])
            yield

    def b1_fin(l, t, st):
        cx.sec = 'B1f'
        b, T, j = t.b, t.T, t.j
        VB = VIEW_B
        pm = ps_ring.next()
        for c in range(4):
            cx.mm(pm.t[:, 0:T], ones_f.t[:, :], cvb.t[:, c, 0:T], c == 0, c == 3, r=[ones_f.key, cvb.key], w=[pm.key])
        pq_ = ps_ring.next()
        for c in range(4):
            sq = tf_ring.next()
            cx.act(sq.t[:, 0:T], cvb.t[:, c, 0:T], AF.Square, r=[cvb.key], w=[sq.key])
            cx.mm(pq_.t[:, 0:T], ones_f.t[:, :], sq.t[:, 0:T], c == 0, c == 3, r=[ones_f.key, sq.key], w=[pq_.key])
        mean = lnm
        cx.ts(mean.t[:, 0:T], pm.t[:, 0:T], 1.0 / CONV_CH, None, ALU.mult, None, r=[pm.key], w=[mean.key])
        msq = tf_ring.next()
        cx.tt(msq.t[:, 0:T], mean.t[:, 0:T], mean.t[:, 0:T], ALU.mult, r=[mean.key], w=[msq.key])
        var = tf_ring.next()
        cx.stt(var.t[:, 0:T], pq_.t[:, 0:T], 1.0 / CONV_CH, msq.t[:, 0:T], ALU.mult, ALU.subtract,
               r=[pq_.key, msq.key], w=[var.key])
        sd = tf_ring.next()
        cx.act(sd.t[:, 0:T], var.t[:, 0:T], AF.Ln, r=[var.key, epsb.key], w=[sd.key], bias=epsb.t[:, 0:1], scale=1.0)
        rs = rs_ring.next()
        cx.act(rs.t[:, 0:T], sd.t[:, 0:T], AF.Exp, r=[sd.key], w=[rs.key], scale=-0.5)
        for c in range(4):
            z = tf_ring.next()
            cx.tt(z.t[:, 0:T], cvb.t[:, c, 0:T], mean.t[:, 0:T], ALU.subtract, r=[cvb.key, mean.key], w=[z.key])
            cx.tt(z.t[:, 0:T], z.t[:, 0:T], rs.t[:, 0:T], ALU.mult, r=[z.key, rs.key], w=[z.key])
            if USE_SILU:
                cx.act(sbuf_.t[:, c, 0:T], z.t[:, 0:T], AF.Silu, r=[z.key, pvec.key], w=[sbuf_.key, VB],
                       scale=pv(l, PV_LNG + c), bias=pv(l, PV_LNB + c))
            else:
                cx.ts(z.t[:, 0:T], z.t[:, 0:T], pv(l, PV_LNG + c), pv(l, PV_LNB + c), ALU.mult, ALU.add,
                      r=[z.key, pvec.key], w=[z.key])
                sg = tf_ring.next()
                cx.act(sg.t[:, 0:T], z.t[:, 0:T], AF.Sigmoid, r=[z.key], w=[sg.key])
                cx.tt(sbuf_.t[:, c, 0:T], z.t[:, 0:T], sg.t[:, 0:T], ALU.mult, r=[z.key, sg.key], w=[sbuf_.key, VB])
        for g in range(4):
            p_ = ps_ring.next()
            cx.mm(p_.t[:, 0:T], wgb.t[:, g * 128:(g + 1) * 128], pdb.t[:, g, 0:T], True, True,
                  r=[wgb.key, pdb.key], w=[p_.key])
            cx.act(mxb.t[:, g, 0:T], p_.t[:, 0:T], AF.Identity, r=[p_.key, pvec.key, zerob.key], w=[mxb.key, VB],
                   scale=pv(l, PV_PSC + g), bias=zerob.t[:, 0:1])

    def out_and_residual(l, t, st, ws, nk_in, src, woffs, Gs, bg):
        T, j, xt = t.T, t.j, st["xt"]
        pst = ps[7]
        pend_sq = []
        for k in range(KC):
            wb, wo = woffs(k)
            py = proj(wb, wo, nk_in, src, T)
            cx.act(yb.t[:, k, 0:T], py.t[:, 0:T], AF.Identity, r=[py.key, zerob.key], w=[yb.key, VIEW_B], bias=zerob.t[:, 0:1])
            sq = tf_ring.next()
            cx.act(sq.t[:, 0:T], yb.t[:, k, 0:T], AF.Square, r=[yb.key], w=[sq.key])
            if pend_sq:
                pend_sq.pop(0)()
            pend_sq.append((lambda sq=sq, k=k: cx.mm(pst.t[:, 0:T], ones_f.t[:, :], sq.t[:, 0:T], k == 0, k == KC - 1,
                                                    r=[ones_f.key, sq.key], w=[pst.key])))
            bg()
        while pend_sq:
            pend_sq.pop(0)()
        rs_ = rstd_from_ps(pst, T, D)
        for kc in range(KC):
            t1 = tf_ring.next()
            cx.stt(t1.t[:, 0:T], yb.t[:, kc, 0:T], col(Gs, kc, j), rs_.t[:, 0:T], ALU.mult, ALU.mult,
                   r=[yb.key, Gs.key, rs_.key], w=[t1.key])
            cx.tt(xt.t[:, kc, 0:T], xt.t[:, kc, 0:T], t1.t[:, 0:T], ALU.add, r=[xt.key, t1.key], w=[xt.key])

    def b_merge(l, t, st):
        drain_casts(1)
        b1_fin(l, t, st)
        cx.sec = 'Bmerge'
        b, T, j = t.b, t.T, t.j
        VB = VIEW_B
        hT = st["hT"]
        ws = WStream("wB", wB_bf, l)
        ws.pos = 512
        st["ws"] = ws
        for k in range(KC):
            wb = ws.next(NBK)
            if k == 0:
                cx.load(OTt.t[:, :, 0:T], OT_d[b].rearrange("(kc p) t -> p kc t", p=128)[:, :, t.c0:t.c0 + T],
                        r=[("OT", b, h, t.id) for h in range(NH)], w=[OTt.key, VB])
            pg = [proj(wb, br * 1024, KC, hT, T) for br in range(3)]
            pya = proj(wb, 3072, 4, sbuf_, T)
            pyb = proj(wb, 3072 + 512, KC, OTt, T)
            pyc = proj(wb, 3072 + 512 + 1024, 4, mxb, T)
            ms = []
            for br, py in enumerate((pya, pyb, pyc)):
                gt = tf_ring.next()
                cx.act(gt.t[:, 0:T], pg[br].t[:, 0:T], AF.Sigmoid, r=[pg[br].key, pvec.key], w=[gt.key],
                       bias=pv(l, PV_BGATE + br * 8 + k))
                m_ = tf_ring.next()
                cx.tt(m_.t[:, 0:T], py.t[:, 0:T], gt.t[:, 0:T], ALU.mult, r=[py.key, gt.key], w=[m_.key])
                ms.append(m_)
            cx.tt(ms[0].t[:, 0:T], ms[0].t[:, 0:T], ms[1].t[:, 0:T], ALU.add, r=[ms[0].key, ms[1].key], w=[ms[0].key],
                  eng="pool")
            cx.tt(mb.t[:, k, 0:T], ms[0].t[:, 0:T], ms[2].t[:, 0:T], ALU.add, r=[ms[0].key, ms[2].key], w=[mb.key, VB],
                  eng="pool")
        cx.sec = 'Bout'
        wo_cache = {}

        def wo_mix(k):
            if k % 4 == 0:
                wo_cache["b"] = ws.next(4096)
            return wo_cache["b"], (k % 4) * 1024

        out_and_residual(l, t, st, ws, KC, mb, wo_mix, G1, lambda: None)

    def b_ffn(l, t, st, bg_):
        def bg():
            bg_()
            cx.sec = 'Bffn'
        cx.sec = 'Bffn'
        b, T, j = t.b, t.T, t.j
        xt, ws = st["xt"], st["ws"]
        h2 = hT_ring.next()
        adaln(xt, T, j, A2, 24, h2)
        for jj in range(NJ):
            if jj % 2 == 0:
                wb = ws.next(4096)
            o_ = (jj % 2) * 2048
            p1 = proj(wb, o_, KC, h2, T)
            p2 = proj(wb, o_ + 1024, KC, h2, T)
            if USE_SILU:
                t1 = tf_ring.next()
                cx.act(t1.t[:, 0:T], p1.t[:, 0:T], AF.Silu, r=[p1.key], w=[t1.key])
            else:
                sg = tf_ring.next()
                cx.act(sg.t[:, 0:T], p1.t[:, 0:T], AF.Sigmoid, r=[p1.key], w=[sg.key])
                t1 = tf_ring.next()
                cx.tt(t1.t[:, 0:T], p1.t[:, 0:T], sg.t[:, 0:T], ALU.mult, r=[p1.key, sg.key], w=[t1.key])
            cx.tt(actb.t[:, jj, 0:T], t1.t[:, 0:T], p2.t[:, 0:T], ALU.mult, r=[t1.key, p2.key], w=[actb.key, VIEW_B])
            bg()

        def wo_ffn(k):
            return ws.next(D_FF), 0

        cx.sec = 'Bffo'

        out_and_residual(l, t, st, ws, NJ, actb, wo_ffn, G2, bg)
        assert ws.pos == NBW, (ws.pos, NBW)
        so = cx.store(xdst(l, t).rearrange("(kc p) t -> p kc t", p=128)[:, :, t.t0:t.t0 + T], xt.t[:, :, 0:T],
                      r=[xt.key], w=[("x", t.kind, b, t.id)])
        if l == L - 1 and t.kind == "l":
            cx.finals.append(so)

    for l in range(L):
        drain_casts(10 ** 9)
        phase_mod(l)
        if l + 1 < L:
            queue_casts(l + 1)
        for b in range(NB):
            tl = tiles_for(b)
            switch_view(VIEW_A)
            sts = [dict() for _ in tl]
            drain(a1_gen(l, tl[0], sts[0]))
            for i, t in enumerate(tl):
                g = a1_gen(l, tl[i + 1], sts[i + 1]) if i + 1 < len(tl) else None
                phase_a2(l, t, sts[i], make_bg(g, 1))
                drain(g)
            attention(l, b)
            tlb = [t for t in tl if not (t.kind == "c" and l == L - 1)]
            switch_view(VIEW_B)
            sts = [dict() for _ in tlb]
            drain(b1_gen(l, tlb[0], sts[0]))
            for i, t in enumerate(tlb):
                b_merge(l, t, sts[i])
                g = b1_gen(l, tlb[i + 1], sts[i + 1]) if i + 1 < len(tlb) else None
                b_ffn(l, t, sts[i], make_bg(g, 2))
                drain(g)

    cx.finalize()
    with nc.Block() as block:
        cx.emit(block)
    stack.close()
    return nc, cx


def _cc(W, col0):
    K = W.shape[0]
    return np.ascontiguousarray(W[:, col0:col0 + 128].reshape(K // 128, 128, 128).transpose(1, 0, 2)).reshape(128, -1)


def _wide(W, col0, n):
    K = W.shape[0]
    return np.ascontiguousarray(W[:, col0:col0 + n].reshape(K // 128, 128, n).transpose(1, 0, 2)).reshape(128, -1)


def _vec(v):
    return np.ascontiguousarray(v.reshape(-1, 128).T)


def prep_shared(inp, L, S):
    f32 = np.float32
    wA = np.empty((L, 128, NA), f32)
    wB = np.empty((L, 128, NBW), f32)
    wM = np.empty((L, 128, NMW), f32)
    pvec = np.zeros((128, L * PV_L + PV_GLOB), f32)
    for l in range(L):
        w_in = inp["w_in"][l]
        parts = []
        for c in range(4):
            parts += [_cc(w_in, COL_A + c * 128), _cc(w_in, COL_A + CONV_CH + c * 128)]
        parts += [_cc(w_in, COL_Q + h * 128) for h in range(8)]
        parts += [_cc(w_in, COL_K + h * 128) for h in range(8)]
        parts += [_cc(w_in, COL_P + c * 128) for c in range(4)]
        parts += [_wide(w_in, COL_V, 512), _wide(w_in, COL_V + 512, 512)]
        wA[l] = np.concatenate(parts, axis=1)
        parts = [np.ascontiguousarray(inp["w_pool_group"][l].transpose(1, 0, 2)).reshape(128, 512)]
        for k in range(8):
            parts += [_cc(w_in, COL_G + br * 1024 + k * 128) for br in range(3)]
            parts += [_cc(inp["w_conv_out"][l], k * 128), _cc(inp["w_attn_out"][l], k * 128),
                      _cc(inp["w_pool_out"][l], k * 128)]
        parts += [_cc(inp["w_out"][l], k * 128) for k in range(8)]
        for jj in range(NJ):
            parts += [_cc(inp["w_ffn_in"][l], jj * 128), _cc(inp["w_ffn_in"][l], D_FF + jj * 128)]
        parts += [_cc(inp["w_ffn_out"][l], k * 128) for k in range(8)]
        wB[l] = np.concatenate(parts, axis=1)
        wM[l] = np.concatenate([_cc(inp["w_mod"][l], n * 128) for n in range(48)], axis=1)
        o = l * PV_L
        for i, nm in enumerate(("g_pre_mix", "g_post_mix", "g_pre_ffn", "g_post_ffn")):
            pvec[:, o + PV_G + 8 * i: o + PV_G + 8 * i + 8] = _vec(inp[nm][l])
        pvec[:, o + PV_BMOD:o + PV_BMOD + 48] = _vec(inp["b_mod"][l])
        pvec[:, o + PV_BGATE:o + PV_BGATE + 24] = _vec(inp["b_gate"][l])
        cw = inp["conv_w"][l]
        pvec[:, o + PV_CONVW:o + PV_CONVW + 124] = np.ascontiguousarray(
            cw.reshape(CONV_W, 4, 128).transpose(2, 1, 0)).reshape(128, 124)
        pvec[:, o + PV_CONVB:o + PV_CONVB + 4] = _vec(inp["conv_b"][l])
        pvec[:, o + PV_LNG:o + PV_LNG + 4] = _vec(inp["conv_ln_g"][l])
        pvec[:, o + PV_LNB:o + PV_LNB + 4] = _vec(inp["conv_ln_b"][l])
        pvec[:, o + PV_PSC:o + PV_PSC + 4] = _vec(inp["pool_scale"][l])
        pvec[:, o + PV_SUBG] = inp["subln_g"][l]
        for i, nm in enumerate(("lam_q1", "lam_k1", "lam_q2", "lam_k2")):
            pvec[:, o + PV_LAM + 64 * i: o + PV_LAM + 64 * (i + 1)] = inp[nm][l][None, :]
    og = L * PV_L
    for g in range(4):
        w = 2 << g
        half = w // 2
        for jx in range(8):
            cnt_f = min(jx + half, w)
            pvec[:, og + PV_CF + g * 8 + jx] = w / cnt_f
            dist = 8 - jx
            cnt_l = min(dist + half, w)
            pvec[:, og + PV_CL + g * 8 + jx] = w / cnt_l
    tpos = np.arange(S)
    row = (tpos // GRID_W).astype(f32)
    colp = (tpos % GRID_W).astype(f32)
    half = HD // 2
    inv = (np.float32(10000.0) ** (-np.arange(0, half, 2, dtype=f32) / np.float32(half))).astype(f32)
    rope = np.zeros((2, 128, S), f32)
    perm = np.zeros((128, 128), f32)
    for p in range(128):
        d = p % 64
        pos = row if d < 32 else colp
        dd = d % 32
        ang = (pos * inv[dd % 16]).astype(f32)
        rope[0, p] = np.cos(ang)
        if dd < 16:
            rope[1, p] = -np.sin(ang)
            partner = p + 16
        else:
            rope[1, p] = np.sin(ang)
            partner = p - 16
        perm[partner, p] = 1.0
    return dict(wA=wA, wB=wB, wM=wM, pvec=pvec, rope=rope, perm=perm)


def prep_core(inp, bs):
    NB = len(bs)
    xT = np.ascontiguousarray(np.stack([inp["x"][b].T for b in bs]))
    cxT = np.ascontiguousarray(np.stack([inp["ctx"][b].T for b in bs]))
    cv = np.stack([inp["c"][b] for b in bs] + [inp["c_ctx"]])
    cT = np.ascontiguousarray(cv.T.reshape(KC, 128, NB + 1).transpose(1, 0, 2)).reshape(128, KC * (NB + 1))
    return dict(xT=xT, cxT=cxT, cT=cT.astype(np.float32))


_CACHE = {}


def run(inp, n_cores, NB, L=None):
    inp = {k: np.asarray(v) for k, v in inp.items()}
    B, S, _ = inp["x"].shape
    CTX = inp["ctx"].shape[1]
    if L is None:
        L = inp["w_in"].shape[0]
    key = (L, NB, S, CTX)
    if key not in _CACHE:
        _CACHE[key] = build_program(L, NB, S, CTX)
    nc, cx = _CACHE[key]
    shared = prep_shared(inp, L, S)
    in_maps = []
    for i in range(n_cores):
        m = dict(shared)
        m.update(prep_core(inp, list(range(i * NB, (i + 1) * NB))))
        in_maps.append(m)
    res = run_bass_kernel_spmd(nc, in_maps, core_ids=list(range(n_cores)))
    out = np.empty((n_cores * NB, S, D), np.float32)
    for i in range(n_cores):
        o = res.results[i]["outT"]
        for jb in range(NB):
            out[i * NB + jb] = o[jb].T
    return out


def kernel(**inputs):
    return run(inputs, 8, 2)
```

```python
import math
from contextlib import ExitStack

import numpy as np
import concourse.bass as bass
import concourse.mybir as mybir
from concourse.bass_utils import run_bass_kernel_spmd

F32 = mybir.dt.float32
BF16 = mybir.dt.bfloat16
AF = mybir.ActivationFunctionType
ALU = mybir.AluOpType

D = 1024
KC = 8
GRID_W = 64
EPS = 1e-6
CONV_CH = 512
CONV_W = 31
NH = 8
HD = 64
D_FF = 2816
NJ = D_FF // 128
COL_A = 0
COL_Q = 1024
COL_K = 2048
COL_V = 3072
COL_P = 4096
COL_G = 4608
IN_W = 7680
TT = 512

NA = 28 * 1024 + 2 * 4096
NBK = 3 * 1024 + 512 + 1024 + 512
NBW = 512 + 8 * NBK + 8 * 1024 + 44 * 1024 + 8 * D_FF
NMW = 48 * 1024
WBUF = 5120

PV_G = 0
PV_BMOD = 32
PV_BGATE = 80
PV_CONVW = 104
PV_CONVB = 228
PV_LNG = 232
PV_LNB = 236
PV_PSC = 240
PV_SUBG = 244
PV_LAM = 245
PV_L = 501
PV_CF = 0
PV_CL = 32
PV_GLOB = 64


class Op:
    __slots__ = ("eng", "fn", "dma", "deps", "sig", "sem", "val", "waits", "pre", "tag")

    def __init__(self, eng, fn, dma):
        self.eng = eng
        self.fn = fn
        self.dma = dma
        self.deps = ()
        self.sig = dma
        self.sem = None
        self.val = 0
        self.waits = ()
        self.pre = None


ENGS = ("pe", "act", "dve", "pool", "sp")
import os as _os
EPOCH = 20000
DMA_NS = 8
DMA_MAXV = 30000
TRACE_TAGS = False
ZSPLIT = int(_os.environ.get('KZSPLIT', '512'))
XLOAD_ENG = _os.environ.get('KXENG', 'sp')
USE_SILU = _os.environ.get('KSILU', '1') == '1'
CONV_DVE_TAPS = int(_os.environ.get('KCTAPS', '31'))
SAME_ENGINE_SYNC = _os.environ.get('KSES', '1') == '1'


class Cx:
    def __init__(self, nc, stack):
        self.nc = nc
        self.stack = stack
        self.ops = []
        self.sec = ""
        self.waitinfo = {}
        self.lastw = {}
        self.readers = {}
        self.nsem = 0
        self.finals = []

    def new_sem(self):
        self.nsem += 1
        return self.stack.enter_context(self.nc.semaphore("s%d" % self.nsem))

    def add(self, eng, fn, r=(), w=(), dma=False):
        op = Op(eng, fn, dma)
        op.tag = self.sec
        deps = {}
        lastw = self.lastw
        readers = self.readers
        for k in r:
            d = lastw.get(k)
            if d is not None:
                deps[id(d)] = d
            readers.setdefault(k, []).append(op)
        for k in w:
            d = lastw.get(k)
            if d is not None:
                deps[id(d)] = d
            rl = readers.get(k)
            if rl:
                for d in rl:
                    if d is not op:
                        deps[id(d)] = d
            lastw[k] = op
            readers[k] = []
        op.deps = tuple(deps.values())
        self.ops.append(op)
        return op

    def mm(self, out, lhsT, rhs, start, stop, r, w, tp=None):
        if tp is None:
            fn = lambda e: e.matmul(out, lhsT, rhs, start=start, stop=stop)
        else:
            fn = lambda e: e.matmul(out, lhsT, rhs, start=start, stop=stop, tile_position=tp)
        return self.add("pe", fn, r, w)

    def act(self, out, in_, func, r, w, bias=None, scale=None):
        kw = {}
        if bias is not None:
            kw["bias"] = bias
        if scale is not None:
            kw["scale"] = scale
        return self.add("act", lambda e: e.activation(out=out, in_=in_, func=func, **kw), r, w)

    def tt(self, out, in0, in1, op, r, w, eng="dve"):
        return self.add(eng, lambda e: e.tensor_tensor(out=out, in0=in0, in1=in1, op=op), r, w)

    def ts(self, out, in0, s1, s2, op0, op1, r, w, eng="dve"):
        if s2 is None:
            fn = lambda e: e.tensor_scalar(out=out, in0=in0, scalar1=s1, scalar2=None, op0=op0)
        else:
            fn = lambda e: e.tensor_scalar(out=out, in0=in0, scalar1=s1, scalar2=s2, op0=op0, op1=op1)
        return self.add(eng, fn, r, w)

    def stt(self, out, in0, scalar, in1, op0, op1, r, w, eng="dve"):
        return self.add(eng, lambda e: e.scalar_tensor_tensor(out=out, in0=in0, scalar=scalar, in1=in1,
                                                              op0=op0, op1=op1), r, w)

    def recip(self, out, in_, r, w):
        return self.add("dve", lambda e: e.reciprocal(out=out, in_=in_), r, w)

    def copy(self, out, in_, r, w, eng="dve"):
        return self.add(eng, lambda e: e.tensor_copy(out=out, in_=in_), r, w)

    def memset(self, ap, val, w, eng="dve"):
        return self.add(eng, lambda e: e.memset(ap, val), (), w)

    def load(self, out, in_, r, w, eng="sp"):
        return self.add(eng, lambda e: e.dma_start(out=out, in_=in_), r, w, dma=True)

    def store(self, out, in_, r, w):
        return self.add("pool", lambda e: e.dma_start(out=out, in_=in_), r, w, dma=True)

    def finalize(self):
        ops = self.ops
        def needs(op, d):
            if d.dma or op.dma or d.eng != op.eng:
                return True
            return SAME_ENGINE_SYNC and op.eng != "pe"

        for op in ops:
            for d in op.deps:
                if needs(op, d):
                    d.sig = True
        cnt = {e: 0 for e in ENGS}
        csem = {e: None for e in ENGS}
        dpool = {e: [[self.new_sem(), 0] for _ in range(DMA_NS)] for e in ("sp", "pool", "act")}
        dn = {"sp": 0, "pool": 0, "act": 0}
        for op in ops:
            if op.dma:
                pool = dpool[op.eng]
                j = dn[op.eng] % DMA_NS
                dn[op.eng] += 1
                ent = pool[j]
                if ent[1] > 0:
                    op.pre = (ent[0], ent[1])
                op.sem = ent[0]
                op.val = ent[1] + 16
                ent[1] = op.val
                if ent[1] > DMA_MAXV:
                    pool[j] = [self.new_sem(), 0]
            elif op.sig:
                e = op.eng
                if csem[e] is None or cnt[e] >= EPOCH:
                    csem[e] = self.new_sem()
                    cnt[e] = 0
                cnt[e] += 1
                op.sem = csem[e]
                op.val = cnt[e]
        waited = {e: {} for e in ENGS}
        nw = 0
        for op in ops:
            need = {}
            if op.pre is not None:
                need[id(op.pre[0])] = [op.pre[0], op.pre[1], None]
            for d in op.deps:
                if needs(op, d):
                    ent = need.get(id(d.sem))
                    if ent is None:
                        need[id(d.sem)] = [d.sem, d.val, d]
                    elif d.val > ent[1]:
                        ent[1] = d.val
                        ent[2] = d
            wl = []
            wd = waited[op.eng]
            for k, (s, v, dsrc) in need.items():
                if wd.get(k, 0) < v:
                    wd[k] = v
                    wl.append((s, v, dsrc))
            op.waits = wl
            nw += len(wl)
        self.n_waits = nw

    def emit(self, block):
        per = {e: [] for e in ENGS}
        for op in self.ops:
            per[op.eng].append(op)
        finals = self.finals

        def run(e, name):
            for op in per[name]:
                for (s, v, dsrc) in op.waits:
                    wi = e.wait_ge(s, v)
                    if TRACE_TAGS:
                        try:
                            self.waitinfo[wi.ins.name] = (op.tag, dsrc.tag + "@" + dsrc.eng if dsrc is not None else "dma-sem")
                        except Exception:
                            pass
                ins = op.fn(e)
                if TRACE_TAGS:
                    try:
                        self.waitinfo[ins.ins.name] = (op.tag, "op")
                    except Exception:
                        pass
                if op.sig:
                    ins.then_inc(op.sem, 16 if op.dma else 1)
            if name == "pool":
                for op in finals:
                    e.wait_ge(op.sem, op.val)

        @block.sync
        def _(e):
            run(e, "sp")

        @block.gpsimd
        def _(e):
            run(e, "pool")

        @block.vector
        def _(e):
            run(e, "dve")

        @block.scalar
        def _(e):
            run(e, "act")

        @block.tensor
        def _(e):
            run(e, "pe")


class Buf:
    __slots__ = ("t", "key")

    def __init__(self, t, key):
        self.t = t
        self.key = key


class Ring:
    def __init__(self, bufs):
        self.bufs = bufs
        self.i = 0

    def next(self):
        b = self.bufs[self.i % len(self.bufs)]
        self.i += 1
        return b


def build_program(L, NB, S, CTX):
    assert S % TT == 0 and CTX % 128 == 0 and CTX <= TT
    NC3 = NB + 1
    NLT = S // TT
    TOT = CTX + S
    NKT = TOT // 128
    NKC = CTX // 128
    NPV = L * PV_L + PV_GLOB
    UPAD = 15
    PPAD = 8

    nc = bass.Bass("TRN2", target_bir_lowering=False)
    dt_ = nc.dram_tensor
    xT_in = dt_("xT", [NB, D, S], F32, kind="ExternalInput").ap()
    cxT_in = dt_("cxT", [NB, D, CTX], F32, kind="ExternalInput").ap()
    cT_in = dt_("cT", [128, KC * NC3], F32, kind="ExternalInput").ap()
    pvec_in = dt_("pvec", [128, NPV], F32, kind="ExternalInput").ap()
    wA_in = dt_("wA", [L, 128, NA], F32, kind="ExternalInput").ap()
    wB_in = dt_("wB", [L, 128, NBW], F32, kind="ExternalInput").ap()
    wM_in = dt_("wM", [L, 128, NMW], F32, kind="ExternalInput").ap()
    rope_in = dt_("rope", [2, 128, S], F32, kind="ExternalInput").ap()
    perm_in = dt_("perm", [128, 128], F32, kind="ExternalInput").ap()
    outT = dt_("outT", [NB, D, S], F32, kind="ExternalOutput").ap()

    wA_bf = dt_("wA_bf", [L, 128, NA], BF16).ap()
    wB_bf = dt_("wB_bf", [L, 128, NBW], BF16).ap()
    wM_bf = dt_("wM_bf", [L, 128, NMW], BF16).ap()
    xs = dt_("xs", [NB, D, S], F32).ap()
    xcs = dt_("xcs", [NB, D, CTX], F32).ap()
    hT_d = dt_("hT_d", [NB, D, TOT], BF16).ap()
    ul_d = dt_("ul_d", [NB, CONV_CH, S + 2 * UPAD], F32).ap()
    uc_d = dt_("uc_d", [NB, CONV_CH, CTX + 2 * UPAD], F32).ap()
    pl_d = dt_("pl_d", [NB, CONV_CH, S + 2 * PPAD], F32).ap()
    pc_d = dt_("pc_d", [NB, CONV_CH, CTX + 2 * PPAD], F32).ap()
    QT_d = dt_("QT_d", [NB, NH, 128, TOT], BF16).ap()
    KT_d = dt_("KT_d", [NB, NH, 128, TOT], BF16).ap()
    V_d = dt_("V_d", [NB, NH, 128, NKT, 128], BF16).ap()
    OT_d = dt_("OT_d", [NB, D, TOT], BF16).ap()

    stack = ExitStack()
    cx = Cx(nc, stack)

    off = [(nc.sbuf_base + 63) // 64 * 64]
    top = nc.sbuf_top
    nbuf = [0]

    def alloc(shape, dtype, at=None):
        n = 1
        for s_ in shape[1:]:
            n *= s_
        nbytes = n * (4 if dtype == F32 else 2)
        nbytes = (nbytes + 63) // 64 * 64
        if at is None:
            o = off[0]
            off[0] += nbytes
            assert off[0] <= top, ("SBUF overflow", off[0], top)
        else:
            o = at
        nbuf[0] += 1
        t = nc.alloc_sbuf_tensor_at("b%d" % nbuf[0], list(shape), dtype, offset=o)
        return Buf(t, ("sb", nbuf[0])), o, nbytes

    def A(shape, dtype):
        return alloc(shape, dtype)[0]

    ones_f = A([128, 128], F32)
    ones_b = A([128, 128], BF16)
    perm_f = A([128, 128], F32)
    perm_b = A([128, 128], BF16)
    epsb = A([128, 1], F32)
    zerob = A([128, 1], F32)
    pvec = A([128, NPV], F32)
    cT = A([128, KC * NC3], F32)
    cs_b = A([128, KC * NC3], BF16)
    modb = A([128, 48 * NC3], F32)
    A1 = A([128, KC * NC3], F32)
    G1 = A([128, KC * NC3], F32)
    A2 = A([128, KC * NC3], F32)
    G2 = A([128, KC * NC3], F32)
    lamt = A([128, 8], F32)

    xt_ring = Ring([A([128, KC, TT], F32) for _ in range(2)])
    hT_ring = Ring([A([128, KC, TT], BF16) for _ in range(2)])
    w_ring = Ring([A([128, WBUF], BF16) for _ in range(3)])
    TFW = TT + 16
    tf_ring = Ring([A([128, TFW], F32) for _ in range(8)])
    lamtmp = tf_ring.bufs[0]
    rs_ring = Ring([A([128, TT], F32) for _ in range(2)])
    tb_ring = Ring([A([128, TT], BF16) for _ in range(4)])
    tf2_ring = Ring([A([128, TFW], F32) for _ in range(3)])
    bg_rs = A([128, TT], F32)
    wgb = A([128, 512], BF16)

    arena0 = off[0]
    o = arena0
    att = []
    for i in range(2):
        kt_, _, n1 = alloc([128, TOT], BF16, at=o); o += n1
        vt_, _, n2 = alloc([128, NKT, 128], BF16, at=o); o += n2
        qt_, _, n3 = alloc([128, TOT], BF16, at=o); o += n3
        att.append((kt_, vt_, qt_))
    pt_list = []
    for i in range(6):
        b_, _, n1 = alloc([128, 2, TT], BF16, at=o); o += n1
        pt_list.append(b_)
    pt_ring = Ring(pt_list)
    paccA, _, n1 = alloc([128, 2, TT], F32, at=o); o += n1
    paccB, _, n1 = alloc([128, 2, TT], F32, at=o); o += n1
    zsum, _, n1 = alloc([128, 2, TT], F32, at=o); o += n1
    rzb, _, n1 = alloc([128, 2, TT], F32, at=o); o += n1
    osb, _, n1 = alloc([128, 2, TT], F32, at=o); o += n1
    att_extra = [paccA, paccB, zsum, rzb, osb]
    att_extra_keys = [(paccA.key, 'd'), (paccA.key, 'p'), (paccB.key, 'd'), (paccB.key, 'p')]
    att_end = o
    o = arena0
    vtok_l = []
    for i in range(2):
        b_, _, n1 = alloc([128, D], BF16, at=o); o += n1
        vtok_l.append(b_)
    vtok_ring = Ring(vtok_l)
    rope_l = []
    for i in range(2):
        b_, _, n1 = alloc([128, 2, TT], F32, at=o); o += n1
        rope_l.append(b_)
    rope_ring = Ring(rope_l)
    pa_end = o
    o = arena0
    uwin, _, n1 = alloc([128, 4, TT + 2 * UPAD], F32, at=o); o += n1
    pwin, _, n1 = alloc([128, 4, TT + 2 * PPAD], F32, at=o); o += n1
    cvb, _, n1 = alloc([128, 4, TT], F32, at=o); o += n1
    sbuf_, _, n1 = alloc([128, 4, TT], BF16, at=o); o += n1
    mxb, _, n1 = alloc([128, 4, TT], BF16, at=o); o += n1
    pdb, _, n1 = alloc([128, 4, TT], BF16, at=o); o += n1
    mb, _, n1 = alloc([128, KC, TT], BF16, at=o); o += n1
    yb, _, n1 = alloc([128, KC, TT], F32, at=o); o += n1
    actb, o_act, n1 = alloc([128, NJ, TT], BF16, at=o); o += n1
    OTt, _, _ = alloc([128, KC, TT], BF16, at=o_act)
    OTt.key = actb.key
    lnm, _, n1 = alloc([128, TT], F32, at=o); o += n1
    if CONV_DVE_TAPS < CONV_W:
        cv2, _, n1 = alloc([128, TT], F32, at=o); o += n1
    else:
        cv2 = lnm
    pb_end = o
    arena_end = max(att_end, pa_end, pb_end)
    assert arena_end <= top, ("SBUF overflow arena", arena_end, top)
    ARENA = ("arena",)
    VIEW_ATT, VIEW_A, VIEW_B = ("view", "att"), ("view", "a"), ("view", "b")

    class HalfView:
        def __init__(self, t3, h):
            self.t3, self.h = t3, h

        def __getitem__(self, idx):
            r_, c_ = idx
            return self.t3[r_, self.h, c_]

    pp = [Buf(stack.enter_context(nc.psum_tensor("pp%d" % i, [128, 2, TT], F32)), ("pp", i)) for i in range(4)]
    ps = [Buf(HalfView(pp[i // 2].t, i % 2), ("ps", i)) for i in range(8)]
    ps_ring = Ring(ps[0:7])

    cur_view = [None]
    view_keys = {
        VIEW_ATT: [b.key for trio in att for b in trio] + [b.key for b in pt_list] + [b.key for b in att_extra] + att_extra_keys,
        VIEW_A: [b.key for b in vtok_l] + [b.key for b in rope_l],
        VIEW_B: [b.key for b in (uwin, pwin, cvb, sbuf_, mxb, pdb, mb, yb, actb, lnm)],
    }

    def switch_view(v):
        if cur_view[0] == v:
            return
        old = cur_view[0]
        cur_view[0] = v
        if old is None:
            return
        keys = view_keys[old] + view_keys[v]
        cx.memset(lamt.t[:, 7:8], 0.0, w=keys + [("fence",)])

    cx.load(pvec.t[:, :], pvec_in[:, :], r=[], w=[pvec.key])
    cx.load(cT.t[:, :], cT_in[:, :], r=[], w=[cT.key])
    cx.load(perm_f.t[:, :], perm_in[:, :], r=[], w=[perm_f.key])
    cx.memset(ones_f.t[:, :], 1.0, w=[ones_f.key])
    cx.memset(ones_b.t[:, :], 1.0, w=[ones_b.key])
    cx.memset(epsb.t[:, :], EPS, w=[epsb.key])
    cx.memset(zerob.t[:, :], 0.0, w=[zerob.key])
    cx.copy(perm_b.t[:, :], perm_f.t[:, :], r=[perm_f.key], w=[perm_b.key])
    tf = tf_ring.next()
    cx.act(tf.t[:, 0:KC * NC3], cT.t[:, :], AF.Sigmoid, r=[cT.key], w=[tf.key])
    cx.tt(cs_b.t[:, :], cT.t[:, :], tf.t[:, 0:KC * NC3], ALU.mult, r=[cT.key, tf.key], w=[cs_b.key])
    zt = tf_ring.next()
    cx.memset(zt.t[:, :], 0.0, w=[zt.key])
    for b in range(NB):
        for (dd, n_, pad, nm) in ((ul_d, S, UPAD, "ul"), (uc_d, CTX, UPAD, "uc"), (pl_d, S, PPAD, "pl"),
                                  (pc_d, CTX, PPAD, "pc")):
            v = dd[b].rearrange("(c p) t -> p c t", p=128)
            cx.store(v[:, :, 0:pad], zt.t[:, 0:4 * pad].rearrange("p (c t) -> p c t", c=4),
                     r=[zt.key], w=[(nm + "pad", b, 0)])
            cx.store(v[:, :, pad + n_:pad + n_ + pad], zt.t[:, 0:4 * pad].rearrange("p (c t) -> p c t", c=4),
                     r=[zt.key], w=[(nm + "pad", b, 1)])
    CW = 8192
    cast_q = []

    def queue_casts(l):
        for (src, dst, n_, nm) in ((wM_in, wM_bf, NMW, "wM"), (wA_in, wA_bf, NA, "wA"), (wB_in, wB_bf, NBW, "wB")):
            c0 = 0
            while c0 < n_:
                c1 = min(n_, c0 + CW)
                cast_q.append((dst[l][:, c0:c1], src[l][:, c0:c1], (nm, l, c0 // CW)))
                c0 = c1

    def drain_casts(n):
        while cast_q and n > 0:
            d_, s_, k_ = cast_q.pop(0)
            cx.store(d_, s_, r=[], w=[k_])
            n -= 1

    queue_casts(0)
    drain_casts(10 ** 9)

    def wkeys(nm, l, c0, c1):
        return [(nm, l, i) for i in range(c0 // CW, (c1 - 1) // CW + 1)]

    class WStream:
        def __init__(self, nm, dram, l):
            self.nm, self.dram, self.l, self.pos = nm, dram, l, 0

        def next(self, n):
            b = w_ring.next()
            c0, c1 = self.pos, self.pos + n
            cx.load(b.t[:, 0:n], self.dram[self.l][:, c0:c1], r=wkeys(self.nm, self.l, c0, c1), w=[b.key])
            self.pos = c1
            return b

    def pv(l, o_, n=1):
        return pvec.t[:, l * PV_L + o_: l * PV_L + o_ + n]

    def col(bufap, kc, j):
        return bufap.t[:, kc * NC3 + j: kc * NC3 + j + 1]

    def modcol(n, j):
        return modb.t[:, n * NC3 + j: n * NC3 + j + 1]

    def phase_mod(l):
        lam_init = 0.8 - 0.6 * math.exp(-0.3 * l)
        ws = WStream("wM", wM_bf, l)
        n = 0
        while n < 48:
            g = min(5, 48 - n)
            wb = ws.next(g * 1024)
            for i in range(g):
                p_ = ps_ring.next()
                for kc in range(KC):
                    cx.mm(p_.t[:, 0:NC3], wb.t[:, i * 1024 + kc * 128: i * 1024 + (kc + 1) * 128],
                          cs_b.t[:, kc * NC3:(kc + 1) * NC3], kc == 0, kc == KC - 1,
                          r=[wb.key, cs_b.key], w=[p_.key])
                cx.ts(modb.t[:, (n + i) * NC3:(n + i + 1) * NC3], p_.t[:, 0:NC3], pv(l, PV_BMOD + n + i), None,
                      ALU.add, None, r=[p_.key, pvec.key], w=[modb.key])
            n += g
        for kc in range(KC):
            sl = slice(kc * NC3, (kc + 1) * NC3)
            cx.ts(A1.t[:, sl], modb.t[:, (8 + kc) * NC3:(9 + kc) * NC3], pv(l, PV_G + 0 + kc), pv(l, PV_G + 0 + kc), ALU.mult, ALU.add,
                  r=[modb.key, pvec.key], w=[A1.key])
            cx.ts(G1.t[:, sl], modb.t[:, (16 + kc) * NC3:(17 + kc) * NC3], pv(l, PV_G + 8 + kc), None, ALU.mult, None,
                  r=[modb.key, pvec.key], w=[G1.key])
            cx.ts(A2.t[:, sl], modb.t[:, (32 + kc) * NC3:(33 + kc) * NC3], pv(l, PV_G + 16 + kc), pv(l, PV_G + 16 + kc), ALU.mult, ALU.add,
                  r=[modb.key, pvec.key], w=[A2.key])
            cx.ts(G2.t[:, sl], modb.t[:, (40 + kc) * NC3:(41 + kc) * NC3], pv(l, PV_G + 24 + kc), None, ALU.mult, None,
                  r=[modb.key, pvec.key], w=[G2.key])
        for i in range(2):
            cx.tt(lamtmp.t[:, 0:64], pv(l, PV_LAM + 128 * i, 64), pv(l, PV_LAM + 128 * i + 64, 64), ALU.mult,
                  r=[pvec.key], w=[lamtmp.key])
            cx.add("dve", (lambda i_: lambda e: e.reduce_sum(out=lamt.t[:, 5 + i_:6 + i_], in_=lamtmp.t[:, 0:64],
                                                             axis=mybir.AxisListType.X))(i),
                   r=[lamtmp.key], w=[lamt.key])
        cx.act(lamt.t[:, 0:2], lamt.t[:, 5:7], AF.Exp, r=[lamt.key], w=[lamt.key])
        cx.tt(lamt.t[:, 2:3], lamt.t[:, 0:1], lamt.t[:, 1:2], ALU.subtract, r=[lamt.key], w=[lamt.key])
        cx.ts(lamt.t[:, 3:4], lamt.t[:, 2:3], -1.0, -lam_init, ALU.mult, ALU.add, r=[lamt.key], w=[lamt.key])
        cx.ts(lamt.t[:, 4:5], pv(l, PV_SUBG), 1.0 - lam_init, None, ALU.mult, None, r=[pvec.key], w=[lamt.key])

    class Tile:
        pass

    def tiles_for(b):
        res = []
        t = Tile()
        t.kind, t.b, t.id, t.T, t.t0, t.c0, t.j = "c", b, "c", CTX, 0, 0, NB
        t.first, t.last = True, True
        res.append(t)
        for i in range(NLT):
            t = Tile()
            t.kind, t.b, t.id, t.T, t.t0, t.c0, t.j = "l", b, i, TT, i * TT, CTX + i * TT, b
            t.first, t.last = (i == 0), (i == NLT - 1)
            res.append(t)
        return res

    def xsrc(l, t):
        if t.kind == "c":
            return (cxT_in if l == 0 else xcs)[t.b]
        return (xT_in if l == 0 else xs)[t.b]

    def xdst(l, t):
        if t.kind == "c":
            return xcs[t.b]
        return (outT if l == L - 1 else xs)[t.b]

    def rstd_from_ps(p_, T, n):
        sd = tf_ring.next()
        cx.act(sd.t[:, 0:T], p_.t[:, 0:T], AF.Ln, r=[p_.key, epsb.key], w=[sd.key],
               bias=epsb.t[:, 0:1], scale=1.0 / n)
        rs = rs_ring.next()
        cx.act(rs.t[:, 0:T], sd.t[:, 0:T], AF.Exp, r=[sd.key], w=[rs.key], scale=-0.5)
        return rs

    def sumsq_stats(src, T, nchunks):
        p_ = ps_ring.next()
        for c in range(nchunks):
            sq = tf_ring.next()
            cx.act(sq.t[:, 0:T], src.t[:, c, 0:T], AF.Square, r=[src.key], w=[sq.key])
            cx.mm(p_.t[:, 0:T], ones_f.t[:, :], sq.t[:, 0:T], c == 0, c == nchunks - 1,
                  r=[ones_f.key, sq.key], w=[p_.key])
        return p_

    def adaln(xt, T, j, Asc, shift_n0, hT):
        p_ = ps_ring.next()
        for c in range(KC):
            cx.act(hT.t[:, c, 0:T], xt.t[:, c, 0:T], AF.Square, r=[(xt.key, c)], w=[hT.key])
        for c in range(KC):
            cx.mm(p_.t[:, 0:T], ones_b.t[:, :], hT.t[:, c, 0:T], c == 0, c == KC - 1,
                  r=[ones_b.key, hT.key], w=[p_.key])
        rs = rstd_from_ps(p_, T, D)
        for kc in range(KC):
            t1 = tf_ring.next()
            cx.stt(t1.t[:, 0:T], xt.t[:, kc, 0:T], col(Asc, kc, j), rs.t[:, 0:T], ALU.mult, ALU.mult,
                   r=[(xt.key, kc), Asc.key, rs.key], w=[t1.key])
            cx.act(hT.t[:, kc, 0:T], t1.t[:, 0:T], AF.Identity, r=[t1.key, modb.key], w=[hT.key],
                   bias=modcol(shift_n0 + kc, j))

    def adaln_bg(xt, T, j, Asc, shift_n0, hT):
        p_ = ps[7]
        for c in range(KC):
            cx.act(hT.t[:, c, 0:T], xt.t[:, c, 0:T], AF.Square, r=[xt.key], w=[hT.key])
            if c % 2 == 1:
                yield
        yield
        for c in range(KC):
            cx.mm(p_.t[:, 0:T], ones_b.t[:, :], hT.t[:, c, 0:T], c == 0, c == KC - 1,
                  r=[ones_b.key, hT.key], w=[p_.key])
        sd = tf2_ring.next()
        cx.act(sd.t[:, 0:T], p_.t[:, 0:T], AF.Ln, r=[p_.key, epsb.key], w=[sd.key], bias=epsb.t[:, 0:1], scale=1.0 / D)
        rs = bg_rs
        cx.act(rs.t[:, 0:T], sd.t[:, 0:T], AF.Exp, r=[sd.key], w=[rs.key], scale=-0.5)
        yield
        for kc in range(KC):
            t1 = tf2_ring.next()
            cx.stt(t1.t[:, 0:T], xt.t[:, kc, 0:T], col(Asc, kc, j), rs.t[:, 0:T], ALU.mult, ALU.mult,
                   r=[xt.key, Asc.key, rs.key], w=[t1.key])
            cx.act(hT.t[:, kc, 0:T], t1.t[:, 0:T], AF.Identity, r=[t1.key, modb.key], w=[hT.key],
                   bias=modcol(shift_n0 + kc, j))
            yield

    def drain(g):
        if g is not None:
            for _ in g:
                pass

    def make_bg(g, n):
        def bg(m=None):
            if g is None:
                return
            for _ in range(n if m is None else m):
                try:
                    next(g)
                except StopIteration:
                    return
        return bg

    def proj(wb, woff, nk, rhs_buf, T, extra_r=()):
        p_ = ps_ring.next()
        for kc in range(nk):
            cx.mm(p_.t[:, 0:T], wb.t[:, woff + kc * 128: woff + (kc + 1) * 128], rhs_buf.t[:, kc, 0:T],
                  kc == 0, kc == nk - 1, r=[wb.key, rhs_buf.key] + list(extra_r), w=[p_.key])
        return p_

    def a1_gen(l, t, st):
        drain_casts(1)
        cx.sec = 'A1'
        b, T, j = t.b, t.T, t.j
        xt = xt_ring.next()
        cx.load(xt.t[:, :, 0:T], xsrc(l, t).rearrange("(kc p) t -> p kc t", p=128)[:, :, t.t0:t.t0 + T],
                r=[("x", t.kind, b, t.id)], w=[xt.key] + [(xt.key, kc_) for kc_ in range(KC)], eng=XLOAD_ENG)
        hT = hT_ring.next()
        st["hT"] = hT
        yield
        for _ in adaln_bg(xt, T, j, A1, 0, hT):
            yield
            cx.sec = 'A1'
        cx.store(hT_d[b].rearrange("(kc p) t -> p kc t", p=128)[:, :, t.c0:t.c0 + T], hT.t[:, :, 0:T],
                 r=[hT.key], w=[("hT", b, t.id)])
        yield

    def phase_a2(l, t, st, bg_):
        def bg():
            bg_()
            cx.sec = 'A2'
        cx.sec = 'A2'
        b, T, j = t.b, t.T, t.j
        hT = st["hT"]
        if t.kind == "l":
            rp = rope_ring.next()
            cx.load(rp.t[:, :, 0:T], rope_in.rearrange("a p t -> p a t")[:, :, t.t0:t.t0 + T], r=[], w=[rp.key, VIEW_A],
                    eng=XLOAD_ENG)
        ws = WStream("wA", wA_bf, l)
        u_d = (uc_d if t.kind == "c" else ul_d)[b]
        unm = "uc" if t.kind == "c" else "ul"
        for half in range(2):
            wb = ws.next(4096)
            for ci in range(2):
                c = half * 2 + ci
                pa = proj(wb, (2 * ci) * 1024, KC, hT, T)
                pb = proj(wb, (2 * ci + 1) * 1024, KC, hT, T)
                sg = tf_ring.next()
                cx.act(sg.t[:, 0:T], pb.t[:, 0:T], AF.Sigmoid, r=[pb.key], w=[sg.key])
                u = tf_ring.next()
                cx.tt(u.t[:, 0:T], pa.t[:, 0:T], sg.t[:, 0:T], ALU.mult, r=[pa.key, sg.key], w=[u.key])
                cx.store(u_d[c * 128:(c + 1) * 128, UPAD + t.t0: UPAD + t.t0 + T], u.t[:, 0:T],
                         r=[u.key], w=[(unm, b, t.id)])
                bg()
        rope_pend = []
        for (dst, nm) in ((QT_d, "Q"), (KT_d, "K")):
            for half in range(2):
                wb = ws.next(4096)
                for hi in range(4):
                    h = half * 4 + hi
                    pq = proj(wb, hi * 1024, KC, hT, T)
                    qb = tb_ring.next()
                    cx.act(qb.t[:, 0:T], pq.t[:, 0:T], AF.Identity, r=[pq.key, zerob.key],
                           w=[qb.key, ("lock", pq.key)], bias=zerob.t[:, 0:1])
                    if t.kind == "l":
                        t1 = tf_ring.next()
                        cx.tt(t1.t[:, 0:T], pq.t[:, 0:T], rp.t[:, 0, 0:T], ALU.mult,
                              r=[pq.key, rp.key, ("lock", pq.key)], w=[t1.key])

                        def rope_tail(qb=qb, t1=t1, dst=dst, nm=nm, h=h):
                            psw = ps_ring.next()
                            cx.mm(psw.t[:, 0:T], perm_b.t[:, :], qb.t[:, 0:T], True, True,
                                  r=[perm_b.key, qb.key], w=[psw.key])
                            t2 = tf_ring.next()
                            cx.tt(t2.t[:, 0:T], psw.t[:, 0:T], rp.t[:, 1, 0:T], ALU.mult, r=[psw.key, rp.key], w=[t2.key])
                            qr = tb_ring.next()
                            cx.tt(qr.t[:, 0:T], t1.t[:, 0:T], t2.t[:, 0:T], ALU.add, r=[t1.key, t2.key], w=[qr.key], eng="pool")
                            cx.store(dst[b, h][:, t.c0:t.c0 + T], qr.t[:, 0:T], r=[qr.key], w=[(nm, b, h, t.id)])

                        if rope_pend:
                            rope_pend.pop(0)()
                        rope_pend.append(rope_tail)
                    else:
                        cx.store(dst[b, h][:, t.c0:t.c0 + T], qb.t[:, 0:T], r=[qb.key], w=[(nm, b, h, t.id)])
                    bg()
        while rope_pend:
            rope_pend.pop(0)()
        p_d = (pc_d if t.kind == "c" else pl_d)[b]
        pnm = "pc" if t.kind == "c" else "pl"
        wb = ws.next(4096)
        for c in range(4):
            pp = proj(wb, c * 1024, KC, hT, T)
            pf = tf_ring.next()
            cx.copy(pf.t[:, 0:T], pp.t[:, 0:T], r=[pp.key], w=[pf.key])
            cx.store(p_d[c * 128:(c + 1) * 128, PPAD + t.t0: PPAD + t.t0 + T], pf.t[:, 0:T],
                     r=[pf.key], w=[(pnm, b, t.id)])
            bg()
        wv = [ws.next(4096), ws.next(4096)]
        for tsi in range(T // 128):
            vt = vtok_ring.next()
            for nh in range(2):
                p_ = ps_ring.next()
                for kc in range(KC):
                    cx.mm(p_.t[:, :], hT.t[:, kc, tsi * 128:(tsi + 1) * 128], wv[nh].t[:, kc * 512:(kc + 1) * 512],
                          kc == 0, kc == KC - 1, r=[hT.key, wv[nh].key], w=[p_.key])
                if nh == 0:
                    cx.act(vt.t[:, 0:512], p_.t[:, :], AF.Identity, r=[p_.key, zerob.key], w=[vt.key, VIEW_A], bias=zerob.t[:, 0:1])
                else:
                    cx.copy(vt.t[:, 512:1024], p_.t[:, :], r=[p_.key], w=[vt.key, VIEW_A])
            kt = t.c0 // 128 + tsi
            cx.store(V_d[b].rearrange("h p k d -> p h k d")[:, :, kt, :], vt.t[:, :].rearrange("p (h d) -> p h d", h=NH),
                     r=[vt.key], w=[("V", b, kt)])

    def attention(l, b):
        switch_view(VIEW_ATT)
        cx.sec = 'ATT'
        need_ctx = l < L - 1
        tl = tiles_for(b)
        all_ids = [t.id for t in tl]
        sq_ = []
        lru = list(pp[0:3])

        class _Pairs:
            def next(self_):
                for p_ in lru:
                    if all(p_ is not q_ for q_ in sq_):
                        lru.remove(p_)
                        lru.append(p_)
                        return p_
                raise AssertionError("no free PSUM pair")

        spairs = _Pairs()
        O0, O1 = ps[6], ps[7]
        deferred = []

        def hk(p_):
            i_ = int(p_.key[1])
            return [("ps", 2 * i_), ("ps", 2 * i_ + 1)]

        def tick(allow=True):
            for d_ in deferred:
                d_[0] -= 1
            if allow:
                for d_ in deferred:
                    if d_[0] <= 0:
                        deferred.remove(d_)
                        d_[1]()
                        break

        def flush():
            while deferred:
                deferred.pop(0)[1]()

        def flush_p2():
            pend = [d_ for d_ in deferred if d_[2] == 2]
            for d_ in pend:
                deferred.remove(d_)
                d_[1]()

        def part2(h, t):
            T = t.T
            zp = spairs.next()
            for i in range(2):
                cx.mm(zp.t[:, i, 0:T], ones_f.t[:, :], zsum.t[:, i, 0:T], True, True,
                      r=[ones_f.key, zsum.key], w=[hk(zp)[i]])
            cx.act(rzb.t[:, :, 0:T], zp.t[:, :, 0:T], AF.Ln, r=hk(zp), w=[rzb.key])
            cx.act(rzb.t[:, :, 0:T], rzb.t[:, :, 0:T], AF.Exp, r=[rzb.key], w=[rzb.key], scale=-1.0)
            t0_ = tf_ring.next()
            cx.tt(t0_.t[:, 0:T], osb.t[:, 0, 0:T], rzb.t[:, 0, 0:T], ALU.mult, r=[osb.key, rzb.key], w=[t0_.key])
            t1_ = tf_ring.next()
            cx.tt(t1_.t[:, 0:T], osb.t[:, 1, 0:T], rzb.t[:, 1, 0:T], ALU.mult, r=[osb.key, rzb.key], w=[t1_.key])
            o_ = tf_ring.next()
            cx.stt(o_.t[:, 0:T], t1_.t[:, 0:T], lamt.t[:, 3:4], t0_.t[:, 0:T], ALU.mult, ALU.add,
                   r=[t1_.key, lamt.key, t0_.key], w=[o_.key])
            osq = tf_ring.next()
            cx.tt(osq.t[:, 0:T], o_.t[:, 0:T], o_.t[:, 0:T], ALU.mult, r=[o_.key], w=[osq.key])
            deferred.append([int(_os.environ.get('KT3', '8')), lambda: part3(h, t, o_, osq), 3])

        def part3(h, t, o_, osq):
            T = t.T
            zp = spairs.next()
            cx.mm(zp.t[:, 0, 0:T], ones_f.t[:, :], osq.t[:, 0:T], True, True, r=[ones_f.key, osq.key],
                  w=[hk(zp)[0]])
            sd = tf_ring.next()
            cx.act(sd.t[:, 0:T], zp.t[:, 0, 0:T], AF.Ln, r=[hk(zp)[0], epsb.key], w=[sd.key],
                   bias=epsb.t[:, 0:1], scale=1.0 / 128)
            rs = rs_ring.next()
            cx.act(rs.t[:, 0:T], sd.t[:, 0:T], AF.Exp, r=[sd.key], w=[rs.key], scale=-0.5)
            ob = tb_ring.next()
            cx.stt(ob.t[:, 0:T], o_.t[:, 0:T], lamt.t[:, 4:5], rs.t[:, 0:T], ALU.mult, ALU.mult,
                   r=[o_.key, lamt.key, rs.key], w=[ob.key])
            cx.store(OT_d[b][h * 128:(h + 1) * 128, t.c0:t.c0 + T], ob.t[:, 0:T], r=[ob.key], w=[("OT", b, h, t.id)])

        for h in range(NH):
            KT, VT, QT = att[h % 2]
            cx.load(KT.t[:, :], KT_d[b, h][:, :], r=[("K", b, h, i) for i in all_ids], w=[KT.key, VIEW_ATT])
            cx.load(VT.t[:, :, :], V_d[b, h][:, :, :], r=[("V", b, k) for k in range(NKT)], w=[VT.key, VIEW_ATT])
            cx.load(QT.t[:, :], QT_d[b, h][:, :], r=[("Q", b, h, i) for i in all_ids], w=[QT.key, VIEW_ATT])
            tls = [t for t in tl if not (t.kind == "c" and not need_ctx)]
            steps = []
            for t in tls:
                nk = NKC if t.kind == "c" else NKT
                for kt in range(nk):
                    steps.append((t, kt, nk))

            def qk(si):
                t_, kt_, _ = steps[si]
                T_ = t_.T
                sp = spairs.next()
                ks = slice(kt_ * 128, (kt_ + 1) * 128)
                qs_ = slice(t_.c0, t_.c0 + T_)
                cx.mm(sp.t[:, 0, 0:T_], KT.t[0:64, ks], QT.t[0:64, qs_], True, True, r=[KT.key, QT.key],
                      w=[hk(sp)[0]], tp=(0, 0))
                cx.mm(sp.t[:, 1, 0:T_], KT.t[64:128, ks], QT.t[64:128, qs_], True, True, r=[KT.key, QT.key],
                      w=[hk(sp)[1]], tp=(64, 0))
                return sp

            XL = _os.environ.get('KXL', '1') == '1'
            del sq_[:]
            if XL:
                sq_.append(qk(0))
                if len(steps) > 1:
                    sq_.append(qk(1))
            for si, (t, kt, nk) in enumerate(steps):
                T = t.T
                if not XL and kt == 0:
                    del sq_[:]
                    sq_.append(qk(si))
                    if nk > 1:
                        sq_.append(qk(si + 1))
                if TRACE_TAGS == 2:
                    cx.sec = 'ATT.l%d.b%d.h%d.%s.%d' % (l, b, h, t.id, kt)
                sp = sq_.pop(0)
                pt = pt_ring.next()
                cx.act(pt.t[:, :, 0:T], sp.t[:, :, 0:T], AF.Exp, r=hk(sp), w=[pt.key], scale=0.125)
                if (si + 2 < len(steps)) if XL else (kt + 2 < nk):
                    sq_.append(qk(si + 2))
                st_, sp_ = (kt == 0), (kt == nk - 1)
                cx.mm(O0.t[:, 0:T], VT.t[:, kt, :], pt.t[:, 0, 0:T], st_, sp_,
                      r=[VT.key, pt.key], w=[O0.key])
                cx.mm(O1.t[:, 0:T], VT.t[:, kt, :], pt.t[:, 1, 0:T], st_, sp_,
                      r=[VT.key, pt.key], w=[O1.key])
                acc = paccA if kt % 2 == 0 else paccB
                TD = T if T < TT else ZSPLIT
                for (eng_, c0_, c1_, kx) in (("dve", 0, TD, "d"), ("pool", TD, T, "p")):
                    if c1_ <= c0_:
                        continue
                    if kt < 2:
                        cx.copy(acc.t[:, :, c0_:c1_], pt.t[:, :, c0_:c1_], r=[pt.key], w=[(acc.key, kx)], eng=eng_)
                    else:
                        cx.tt(acc.t[:, :, c0_:c1_], acc.t[:, :, c0_:c1_], pt.t[:, :, c0_:c1_], ALU.add,
                              r=[(acc.key, kx), pt.key], w=[(acc.key, kx)], eng=eng_)
                tick(allow=(kt < nk - 1))
                if kt < nk - 1:
                    continue
                flush_p2()
                cx.act(osb.t[:, :, 0:T], pp[3].t[:, :, 0:T], AF.Identity, r=[O0.key, O1.key, zerob.key],
                       w=[osb.key], bias=zerob.t[:, 0:1])
                ak = [(paccA.key, "d"), (paccA.key, "p"), (paccB.key, "d"), (paccB.key, "p")]
                if nk > 1:
                    cx.tt(zsum.t[:, :, 0:T], paccA.t[:, :, 0:T], paccB.t[:, :, 0:T], ALU.add,
                          r=ak, w=[zsum.key])
                else:
                    cx.copy(zsum.t[:, :, 0:T], paccA.t[:, :, 0:T], r=ak, w=[zsum.key])
                deferred.append([int(_os.environ.get('KT2', '4')), (lambda h_, t_: lambda: part2(h_, t_))(h, t), 2])
        flush()

    def b1_gen(l, t, st):
        for _ in b1_gen_(l, t, st):
            yield
            cx.sec = 'B1'

    def b1_gen_(l, t, st):
        cx.sec = 'B1'
        drain_casts(1)
        b, T, j = t.b, t.T, t.j
        VB = VIEW_B
        hT = hT_ring.next()
        cx.load(hT.t[:, :, 0:T], hT_d[b].rearrange("(kc p) t -> p kc t", p=128)[:, :, t.c0:t.c0 + T],
                r=[("hT", b, t.id)], w=[hT.key])
        xt = xt_ring.next()
        cx.load(xt.t[:, :, 0:T], xsrc(l, t).rearrange("(kc p) t -> p kc t", p=128)[:, :, t.t0:t.t0 + T],
                r=[("x", t.kind, b, t.id)], w=[xt.key] + [(xt.key, kc_) for kc_ in range(KC)])
        st["hT"], st["xt"] = hT, xt
        if t.kind == "c":
            u_d, p_d, unm, pnm = uc_d[b], pc_d[b], "uc", "pc"
            nb_ids = ["c"]
        else:
            u_d, p_d, unm, pnm = ul_d[b], pl_d[b], "ul", "pl"
            nb_ids = [i for i in (t.id - 1, t.id, t.id + 1) if 0 <= i < NLT]
        cx.load(uwin.t[:, :, 0:T + 2 * UPAD], u_d.rearrange("(c p) t -> p c t", p=128)[:, :, t.t0:t.t0 + T + 2 * UPAD],
                r=[(unm, b, i) for i in nb_ids] + [(unm + "pad", b, 0), (unm + "pad", b, 1)], w=[uwin.key, VB])
        cx.load(pwin.t[:, :, 0:T + 2 * PPAD], p_d.rearrange("(c p) t -> p c t", p=128)[:, :, t.t0:t.t0 + T + 2 * PPAD],
                r=[(pnm, b, i) for i in nb_ids] + [(pnm + "pad", b, 0), (pnm + "pad", b, 1)], w=[pwin.key, VB])
        cx.load(wgb.t[:, :], wB_bf[l][:, 0:512], r=wkeys("wB", l, 0, 512), w=[wgb.key])
        yield
        for c in range(4):
            acc = cvb.t[:, c, 0:T]
            acc2 = cv2.t[:, 0:T]
            cw0 = PV_CONVW + c * CONV_W
            ND = CONV_DVE_TAPS
            cx.ts(acc, uwin.t[:, c, 0:T], pv(l, cw0), pv(l, PV_CONVB + c), ALU.mult, ALU.add,
                  r=[uwin.key, pvec.key], w=[(cvb.key, c), cvb.key, VB])
            if ND < CONV_W:
                cx.ts(acc2, uwin.t[:, c, ND:ND + T], pv(l, cw0 + ND), None, ALU.mult, None,
                      r=[uwin.key, pvec.key], w=[cv2.key], eng="pool")
            kd, kp = 1, ND + 1
            while kd < ND or kp < CONV_W:
                for _ in range(2):
                    if kd < ND:
                        cx.stt(acc, uwin.t[:, c, kd:kd + T], pv(l, cw0 + kd), acc, ALU.mult, ALU.add,
                               r=[uwin.key, pvec.key, (cvb.key, c)], w=[(cvb.key, c)])
                        kd += 1
                if kp < CONV_W:
                    cx.stt(acc2, uwin.t[:, c, kp:kp + T], pv(l, cw0 + kp), acc2, ALU.mult, ALU.add,
                           r=[uwin.key, pvec.key, cv2.key], w=[cv2.key], eng="pool")
                    kp += 1
                yield
            if ND < CONV_W:
                cx.tt(acc, acc, acc2, ALU.add, r=[(cvb.key, c), cv2.key], w=[(cvb.key, c)])
            cx.memset(lamt.t[:, 7:8], 0.0, w=[cvb.key] + [(cvb.key, c)])
            yield
        W2 = T + 2 * PPAD
        for g in range(4):
            wwin = 2 << g
            pw = pwin.t[:, g, :]
            cur = tf2_ring.next()
            cx.tt(cur.t[:, 1:W2], pw[:, 0:W2 - 1], pw[:, 1:W2], ALU.add, r=[pwin.key], w=[cur.key])
            lo, hi = 1, W2
            sh = 1
            for step in range(g):
                nx = tf2_ring.next()
                cx.tt(nx.t[:, lo + sh:hi - sh], cur.t[:, lo:hi - 2 * sh], cur.t[:, lo + 2 * sh:hi], ALU.add,
                      r=[cur.key], w=[nx.key])
                lo, hi = lo + sh, hi - sh
                cur = nx
                sh *= 2
            ctr = cur.t[:, PPAD:PPAD + T]
            gpv = pvec.t[:, L * PV_L + PV_CF + g * 8: L * PV_L + PV_CF + g * 8 + 8]
            gpl = pvec.t[:, L * PV_L + PV_CL + g * 8: L * PV_L + PV_CL + g * 8 + 8]
            if t.first:
                cx.tt(cur.t[:, PPAD:PPAD + 8], cur.t[:, PPAD:PPAD + 8], gpv, ALU.mult, r=[cur.key, pvec.key], w=[cur.key])
            if t.last:
                cx.tt(cur.t[:, PPAD + T - 8:PPAD + T], cur.t[:, PPAD + T - 8:PPAD + T], gpl, ALU.mult,
                      r=[cur.key, pvec.key], w=[cur.key])
            cx.stt(pdb.t[:, g, 0:T], ctr, 1.0 / wwin, pw[:, PPAD:PPAD + T], ALU.mult, ALU.subtract,
                   r=[cur.key, pwin.key], w=[pdb.key, VB])
            yield

    def b1_fin(l, t, st):
        cx.sec = 'B1f'
        b, T, j = t.b, t.T, t.j
        VB = VIEW_B
        pm = ps_ring.next()
        for c in range(4):
            cx.mm(pm.t[:, 0:T], ones_f.t[:, :], cvb.t[:, c, 0:T], c == 0, c == 3, r=[ones_f.key, cvb.key], w=[pm.key])
        pq_ = ps_ring.next()
        for c in range(4):
            sq = tf_ring.next()
            cx.act(sq.t[:, 0:T], cvb.t[:, c, 0:T], AF.Square, r=[cvb.key], w=[sq.key])
            cx.mm(pq_.t[:, 0:T], ones_f.t[:, :], sq.t[:, 0:T], c == 0, c == 3, r=[ones_f.key, sq.key], w=[pq_.key])
        mean = lnm
        cx.ts(mean.t[:, 0:T], pm.t[:, 0:T], 1.0 / CONV_CH, None, ALU.mult, None, r=[pm.key], w=[mean.key])
        msq = tf_ring.next()
        cx.tt(msq.t[:, 0:T], mean.t[:, 0:T], mean.t[:, 0:T], ALU.mult, r=[mean.key], w=[msq.key])
        var = tf_ring.next()
        cx.stt(var.t[:, 0:T], pq_.t[:, 0:T], 1.0 / CONV_CH, msq.t[:, 0:T], ALU.mult, ALU.subtract,
               r=[pq_.key, msq.key], w=[var.key])
        sd = tf_ring.next()
        cx.act(sd.t[:, 0:T], var.t[:, 0:T], AF.Ln, r=[var.key, epsb.key], w=[sd.key], bias=epsb.t[:, 0:1], scale=1.0)
        rs = rs_ring.next()
        cx.act(rs.t[:, 0:T], sd.t[:, 0:T], AF.Exp, r=[sd.key], w=[rs.key], scale=-0.5)
        for c in range(4):
            z = tf_ring.next()
            cx.tt(z.t[:, 0:T], cvb.t[:, c, 0:T], mean.t[:, 0:T], ALU.subtract, r=[cvb.key, mean.key], w=[z.key])
            cx.tt(z.t[:, 0:T], z.t[:, 0:T], rs.t[:, 0:T], ALU.mult, r=[z.key, rs.key], w=[z.key])
            if USE_SILU:
                cx.act(sbuf_.t[:, c, 0:T], z.t[:, 0:T], AF.Silu, r=[z.key, pvec.key], w=[sbuf_.key, VB],
                       scale=pv(l, PV_LNG + c), bias=pv(l, PV_LNB + c))
            else:
                cx.ts(z.t[:, 0:T], z.t[:, 0:T], pv(l, PV_LNG + c), pv(l, PV_LNB + c), ALU.mult, ALU.add,
                      r=[z.key, pvec.key], w=[z.key])
                sg = tf_ring.next()
                cx.act(sg.t[:, 0:T], z.t[:, 0:T], AF.Sigmoid, r=[z.key], w=[sg.key])
                cx.tt(sbuf_.t[:, c, 0:T], z.t[:, 0:T], sg.t[:, 0:T], ALU.mult, r=[z.key, sg.key], w=[sbuf_.key, VB])
        for g in range(4):
            p_ = ps_ring.next()
            cx.mm(p_.t[:, 0:T], wgb.t[:, g * 128:(g + 1) * 128], pdb.t[:, g, 0:T], True, True,
                  r=[wgb.key, pdb.key], w=[p_.key])
            cx.act(mxb.t[:, g, 0:T], p_.t[:, 0:T], AF.Identity, r=[p_.key, pvec.key, zerob.key], w=[mxb.key, VB],
                   scale=pv(l, PV_PSC + g), bias=zerob.t[:, 0:1])

    def out_and_residual(l, t, st, ws, nk_in, src, woffs, Gs, bg):
        T, j, xt = t.T, t.j, st["xt"]
        pst = ps[7]
        pend_sq = []
        for k in range(KC):
            wb, wo = woffs(k)
            py = proj(wb, wo, nk_in, src, T)
            cx.act(yb.t[:, k, 0:T], py.t[:, 0:T], AF.Identity, r=[py.key, zerob.key], w=[yb.key, VIEW_B], bias=zerob.t[:, 0:1])
            sq = tf_ring.next()
            cx.act(sq.t[:, 0:T], yb.t[:, k, 0:T], AF.Square, r=[yb.key], w=[sq.key])
            if pend_sq:
                pend_sq.pop(0)()
            pend_sq.append((lambda sq=sq, k=k: cx.mm(pst.t[:, 0:T], ones_f.t[:, :], sq.t[:, 0:T], k == 0, k == KC - 1,
                                                    r=[ones_f.key, sq.key], w=[pst.key])))
            bg(4) if nk_in > KC else bg()
        while pend_sq:
            pend_sq.pop(0)()
        rs_ = rstd_from_ps(pst, T, D)
        for kc in range(KC):
            t1 = tf_ring.next()
            cx.stt(t1.t[:, 0:T], yb.t[:, kc, 0:T], col(Gs, kc, j), rs_.t[:, 0:T], ALU.mult, ALU.mult,
                   r=[yb.key, Gs.key, rs_.key], w=[t1.key])
            cx.tt(xt.t[:, kc, 0:T], xt.t[:, kc, 0:T], t1.t[:, 0:T], ALU.add, r=[(xt.key, kc), t1.key], w=[(xt.key, kc)])

    def b_merge(l, t, st):
        drain_casts(1)
        b1_fin(l, t, st)
        cx.sec = 'Bmerge'
        b, T, j = t.b, t.T, t.j
        VB = VIEW_B
        hT = st["hT"]
        ws = WStream("wB", wB_bf, l)
        ws.pos = 512
        st["ws"] = ws
        for k in range(KC):
            wb = ws.next(NBK)
            if k == 0:
                cx.load(OTt.t[:, :, 0:T], OT_d[b].rearrange("(kc p) t -> p kc t", p=128)[:, :, t.c0:t.c0 + T],
                        r=[("OT", b, h, t.id) for h in range(NH)], w=[OTt.key, VB])
            pg = [proj(wb, br * 1024, KC, hT, T) for br in range(3)]
            pyc = proj(wb, 3072 + 512 + 1024, 4, mxb, T)
            pyb = proj(wb, 3072 + 512, KC, OTt, T)
            pya = proj(wb, 3072, 4, sbuf_, T)
            ms = []
            for br, py in enumerate((pya, pyb, pyc)):
                gt = tf_ring.next()
                cx.act(gt.t[:, 0:T], pg[br].t[:, 0:T], AF.Sigmoid, r=[pg[br].key, pvec.key], w=[gt.key],
                       bias=pv(l, PV_BGATE + br * 8 + k))
                m_ = tf_ring.next()
                cx.tt(m_.t[:, 0:T], py.t[:, 0:T], gt.t[:, 0:T], ALU.mult, r=[py.key, gt.key], w=[m_.key])
                ms.append(m_)
            cx.tt(ms[0].t[:, 0:T], ms[0].t[:, 0:T], ms[1].t[:, 0:T], ALU.add, r=[ms[0].key, ms[1].key], w=[ms[0].key],
                  eng="pool")
            cx.tt(mb.t[:, k, 0:T], ms[0].t[:, 0:T], ms[2].t[:, 0:T], ALU.add, r=[ms[0].key, ms[2].key], w=[mb.key, VB],
                  eng="pool")
        cx.sec = 'Bout'
        wo_cache = {}

        def wo_mix(k):
            if k % 4 == 0:
                wo_cache["b"] = ws.next(4096)
            return wo_cache["b"], (k % 4) * 1024

        out_and_residual(l, t, st, ws, KC, mb, wo_mix, G1, lambda m=None: None)

    def b_ffn(l, t, st, bg_):
        def bg(m=None):
            bg_(m)
            cx.sec = 'Bffn'
        cx.sec = 'Bffn'
        b, T, j = t.b, t.T, t.j
        xt, ws = st["xt"], st["ws"]
        h2 = hT_ring.next()
        adaln(xt, T, j, A2, 24, h2)
        for jj in range(NJ):
            if jj % 2 == 0:
                wb = ws.next(4096)
            o_ = (jj % 2) * 2048
            p1 = proj(wb, o_, KC, h2, T)
            p2 = proj(wb, o_ + 1024, KC, h2, T)
            if USE_SILU:
                t1 = tf_ring.next()
                cx.act(t1.t[:, 0:T], p1.t[:, 0:T], AF.Silu, r=[p1.key], w=[t1.key])
            else:
                sg = tf_ring.next()
                cx.act(sg.t[:, 0:T], p1.t[:, 0:T], AF.Sigmoid, r=[p1.key], w=[sg.key])
                t1 = tf_ring.next()
                cx.tt(t1.t[:, 0:T], p1.t[:, 0:T], sg.t[:, 0:T], ALU.mult, r=[p1.key, sg.key], w=[t1.key])
            cx.tt(actb.t[:, jj, 0:T], t1.t[:, 0:T], p2.t[:, 0:T], ALU.mult, r=[t1.key, p2.key], w=[actb.key, VIEW_B])
            bg()

        def wo_ffn(k):
            return ws.next(D_FF), 0

        cx.sec = 'Bffo'

        out_and_residual(l, t, st, ws, NJ, actb, wo_ffn, G2, bg)
        assert ws.pos == NBW, (ws.pos, NBW)
        so = cx.store(xdst(l, t).rearrange("(kc p) t -> p kc t", p=128)[:, :, t.t0:t.t0 + T], xt.t[:, :, 0:T],
                      r=[xt.key] + [(xt.key, kc_) for kc_ in range(KC)], w=[("x", t.kind, b, t.id)])
        if l == L - 1 and t.kind == "l":
            cx.finals.append(so)

    for l in range(L):
        drain_casts(10 ** 9)
        phase_mod(l)
        if l + 1 < L:
            queue_casts(l + 1)
        for b in range(NB):
            tl = tiles_for(b)
            switch_view(VIEW_A)
            sts = [dict() for _ in tl]
            drain(a1_gen(l, tl[0], sts[0]))
            for i, t in enumerate(tl):
                g = a1_gen(l, tl[i + 1], sts[i + 1]) if i + 1 < len(tl) else None
                phase_a2(l, t, sts[i], make_bg(g, 1))
                drain(g)
            attention(l, b)
            if _os.environ.get('KSTOPATT') == '1':
                break
            tlb = [t for t in tl if not (t.kind == "c" and l == L - 1)]
            switch_view(VIEW_B)
            sts = [dict() for _ in tlb]
            drain(b1_gen(l, tlb[0], sts[0]))
            for i, t in enumerate(tlb):
                b_merge(l, t, sts[i])
                g = b1_gen(l, tlb[i + 1], sts[i + 1]) if i + 1 < len(tlb) else None
                b_ffn(l, t, sts[i], make_bg(g, 2))
                drain(g)
        if _os.environ.get('KSTOPATT') == '1':
            break

    cx.finalize()
    with nc.Block() as block:
        cx.emit(block)
    stack.close()
    return nc, cx


def _cc(W, col0):
    K = W.shape[0]
    return np.ascontiguousarray(W[:, col0:col0 + 128].reshape(K // 128, 128, 128).transpose(1, 0, 2)).reshape(128, -1)


def _wide(W, col0, n):
    K = W.shape[0]
    return np.ascontiguousarray(W[:, col0:col0 + n].reshape(K // 128, 128, n).transpose(1, 0, 2)).reshape(128, -1)


def _vec(v):
    return np.ascontiguousarray(v.reshape(-1, 128).T)


def prep_shared(inp, L, S):
    f32 = np.float32
    wA = np.empty((L, 128, NA), f32)
    wB = np.empty((L, 128, NBW), f32)
    wM = np.empty((L, 128, NMW), f32)
    pvec = np.zeros((128, L * PV_L + PV_GLOB), f32)
    for l in range(L):
        w_in = inp["w_in"][l]
        parts = []
        for c in range(4):
            parts += [_cc(w_in, COL_A + c * 128), _cc(w_in, COL_A + CONV_CH + c * 128)]
        parts += [_cc(w_in, COL_Q + h * 128) for h in range(8)]
        parts += [_cc(w_in, COL_K + h * 128) for h in range(8)]
        parts += [_cc(w_in, COL_P + c * 128) for c in range(4)]
        parts += [_wide(w_in, COL_V, 512), _wide(w_in, COL_V + 512, 512)]
        wA[l] = np.concatenate(parts, axis=1)
        parts = [np.ascontiguousarray(inp["w_pool_group"][l].transpose(1, 0, 2)).reshape(128, 512)]
        for k in range(8):
            parts += [_cc(w_in, COL_G + br * 1024 + k * 128) for br in range(3)]
            parts += [_cc(inp["w_conv_out"][l], k * 128), _cc(inp["w_attn_out"][l], k * 128),
                      _cc(inp["w_pool_out"][l], k * 128)]
        parts += [_cc(inp["w_out"][l], k * 128) for k in range(8)]
        for jj in range(NJ):
            parts += [_cc(inp["w_ffn_in"][l], jj * 128), _cc(inp["w_ffn_in"][l], D_FF + jj * 128)]
        parts += [_cc(inp["w_ffn_out"][l], k * 128) for k in range(8)]
        wB[l] = np.concatenate(parts, axis=1)
        wM[l] = np.concatenate([_cc(inp["w_mod"][l], n * 128) for n in range(48)], axis=1)
        o = l * PV_L
        for i, nm in enumerate(("g_pre_mix", "g_post_mix", "g_pre_ffn", "g_post_ffn")):
            pvec[:, o + PV_G + 8 * i: o + PV_G + 8 * i + 8] = _vec(inp[nm][l])
        pvec[:, o + PV_BMOD:o + PV_BMOD + 48] = _vec(inp["b_mod"][l])
        pvec[:, o + PV_BGATE:o + PV_BGATE + 24] = _vec(inp["b_gate"][l])
        cw = inp["conv_w"][l]
        pvec[:, o + PV_CONVW:o + PV_CONVW + 124] = np.ascontiguousarray(
            cw.reshape(CONV_W, 4, 128).transpose(2, 1, 0)).reshape(128, 124)
        pvec[:, o + PV_CONVB:o + PV_CONVB + 4] = _vec(inp["conv_b"][l])
        pvec[:, o + PV_LNG:o + PV_LNG + 4] = _vec(inp["conv_ln_g"][l])
        pvec[:, o + PV_LNB:o + PV_LNB + 4] = _vec(inp["conv_ln_b"][l])
        pvec[:, o + PV_PSC:o + PV_PSC + 4] = _vec(inp["pool_scale"][l])
        pvec[:, o + PV_SUBG] = inp["subln_g"][l]
        for i, nm in enumerate(("lam_q1", "lam_k1", "lam_q2", "lam_k2")):
            pvec[:, o + PV_LAM + 64 * i: o + PV_LAM + 64 * (i + 1)] = inp[nm][l][None, :]
    og = L * PV_L
    for g in range(4):
        w = 2 << g
        half = w // 2
        for jx in range(8):
            cnt_f = min(jx + half, w)
            pvec[:, og + PV_CF + g * 8 + jx] = w / cnt_f
            dist = 8 - jx
            cnt_l = min(dist + half, w)
            pvec[:, og + PV_CL + g * 8 + jx] = w / cnt_l
    tpos = np.arange(S)
    row = (tpos // GRID_W).astype(f32)
    colp = (tpos % GRID_W).astype(f32)
    half = HD // 2
    inv = (np.float32(10000.0) ** (-np.arange(0, half, 2, dtype=f32) / np.float32(half))).astype(f32)
    rope = np.zeros((2, 128, S), f32)
    perm = np.zeros((128, 128), f32)
    for p in range(128):
        d = p % 64
        pos = row if d < 32 else colp
        dd = d % 32
        ang = (pos * inv[dd % 16]).astype(f32)
        rope[0, p] = np.cos(ang)
        if dd < 16:
            rope[1, p] = -np.sin(ang)
            partner = p + 16
        else:
            rope[1, p] = np.sin(ang)
            partner = p - 16
        perm[partner, p] = 1.0
    return dict(wA=wA, wB=wB, wM=wM, pvec=pvec, rope=rope, perm=perm)


def prep_core(inp, bs):
    NB = len(bs)
    xT = np.ascontiguousarray(np.stack([inp["x"][b].T for b in bs]))
    cxT = np.ascontiguousarray(np.stack([inp["ctx"][b].T for b in bs]))
    cv = np.stack([inp["c"][b] for b in bs] + [inp["c_ctx"]])
    cT = np.ascontiguousarray(cv.T.reshape(KC, 128, NB + 1).transpose(1, 0, 2)).reshape(128, KC * (NB + 1))
    return dict(xT=xT, cxT=cxT, cT=cT.astype(np.float32))


_CACHE = {}


def run(inp, n_cores, NB, L=None):
    inp = {k: np.asarray(v) for k, v in inp.items()}
    B, S, _ = inp["x"].shape
    CTX = inp["ctx"].shape[1]
    if L is None:
        L = inp["w_in"].shape[0]
    key = (L, NB, S, CTX)
    if key not in _CACHE:
        _CACHE[key] = build_program(L, NB, S, CTX)
    nc, cx = _CACHE[key]
    shared = prep_shared(inp, L, S)
    in_maps = []
    for i in range(n_cores):
        m = dict(shared)
        m.update(prep_core(inp, list(range(i * NB, (i + 1) * NB))))
        in_maps.append(m)
    res = run_bass_kernel_spmd(nc, in_maps, core_ids=list(range(n_cores)))
    out = np.empty((n_cores * NB, S, D), np.float32)
    for i in range(n_cores):
        o = res.results[i]["outT"]
        for jb in range(NB):
            out[i * NB + jb] = o[jb].T
    return out


def kernel(**inputs):
    return run(inputs, 8, 2)
```

```python
import math
from contextlib import ExitStack

import numpy as np
import concourse.bass as bass
import concourse.mybir as mybir
from concourse.bass_utils import run_bass_kernel_spmd

F32 = mybir.dt.float32
BF16 = mybir.dt.bfloat16
AF = mybir.ActivationFunctionType
ALU = mybir.AluOpType

D = 1024
KC = 8
GRID_W = 64
EPS = 1e-6
CONV_CH = 512
CONV_W = 31
NH = 8
HD = 64
D_FF = 2816
NJ = D_FF // 128
COL_A = 0
COL_Q = 1024
COL_K = 2048
COL_V = 3072
COL_P = 4096
COL_G = 4608
IN_W = 7680
TT = 512

NA = 28 * 1024 + 2 * 4096
NBK = 3 * 1024 + 512 + 1024 + 512
NBW = 512 + 8 * NBK + 8 * 1024 + 44 * 1024 + 8 * D_FF
NMW = 48 * 1024
WBUF = 5120

PV_G = 0
PV_BMOD = 32
PV_BGATE = 80
PV_CONVW = 104
PV_CONVB = 228
PV_LNG = 232
PV_LNB = 236
PV_PSC = 240
PV_SUBG = 244
PV_LAM = 245
PV_L = 501
PV_CF = 0
PV_CL = 32
PV_GLOB = 64


class Op:
    __slots__ = ("eng", "fn", "dma", "deps", "sig", "sem", "val", "waits", "pre", "tag")

    def __init__(self, eng, fn, dma):
        self.eng = eng
        self.fn = fn
        self.dma = dma
        self.deps = ()
        self.sig = dma
        self.sem = None
        self.val = 0
        self.waits = ()
        self.pre = None


ENGS = ("pe", "act", "dve", "pool", "sp")
import os as _os
EPOCH = 20000
DMA_NS = 8
DMA_MAXV = 30000
TRACE_TAGS = False
ZSPLIT = int(_os.environ.get('KZSPLIT', '512'))
XLOAD_ENG = _os.environ.get('KXENG', 'sp')
USE_SILU = _os.environ.get('KSILU', '1') == '1'
CONV_DVE_TAPS = int(_os.environ.get('KCTAPS', '31'))
SAME_ENGINE_SYNC = _os.environ.get('KSES', '1') == '1'


class Cx:
    def __init__(self, nc, stack):
        self.nc = nc
        self.stack = stack
        self.ops = []
        self.sec = ""
        self.waitinfo = {}
        self.lastw = {}
        self.readers = {}
        self.nsem = 0
        self.finals = []

    def new_sem(self):
        self.nsem += 1
        return self.stack.enter_context(self.nc.semaphore("s%d" % self.nsem))

    def add(self, eng, fn, r=(), w=(), dma=False):
        op = Op(eng, fn, dma)
        op.tag = self.sec
        deps = {}
        lastw = self.lastw
        readers = self.readers
        for k in r:
            d = lastw.get(k)
            if d is not None:
                deps[id(d)] = d
            readers.setdefault(k, []).append(op)
        for k in w:
            d = lastw.get(k)
            if d is not None:
                deps[id(d)] = d
            rl = readers.get(k)
            if rl:
                for d in rl:
                    if d is not op:
                        deps[id(d)] = d
            lastw[k] = op
            readers[k] = []
        op.deps = tuple(deps.values())
        self.ops.append(op)
        return op

    def mm(self, out, lhsT, rhs, start, stop, r, w, tp=None):
        if tp is None:
            fn = lambda e: e.matmul(out, lhsT, rhs, start=start, stop=stop)
        else:
            fn = lambda e: e.matmul(out, lhsT, rhs, start=start, stop=stop, tile_position=tp)
        return self.add("pe", fn, r, w)

    def act(self, out, in_, func, r, w, bias=None, scale=None):
        kw = {}
        if bias is not None:
            kw["bias"] = bias
        if scale is not None:
            kw["scale"] = scale
        return self.add("act", lambda e: e.activation(out=out, in_=in_, func=func, **kw), r, w)

    def tt(self, out, in0, in1, op, r, w, eng="dve"):
        return self.add(eng, lambda e: e.tensor_tensor(out=out, in0=in0, in1=in1, op=op), r, w)

    def ts(self, out, in0, s1, s2, op0, op1, r, w, eng="dve"):
        if s2 is None:
            fn = lambda e: e.tensor_scalar(out=out, in0=in0, scalar1=s1, scalar2=None, op0=op0)
        else:
            fn = lambda e: e.tensor_scalar(out=out, in0=in0, scalar1=s1, scalar2=s2, op0=op0, op1=op1)
        return self.add(eng, fn, r, w)

    def stt(self, out, in0, scalar, in1, op0, op1, r, w, eng="dve"):
        return self.add(eng, lambda e: e.scalar_tensor_tensor(out=out, in0=in0, scalar=scalar, in1=in1,
                                                              op0=op0, op1=op1), r, w)

    def recip(self, out, in_, r, w):
        return self.add("dve", lambda e: e.reciprocal(out=out, in_=in_), r, w)

    def copy(self, out, in_, r, w, eng="dve"):
        return self.add(eng, lambda e: e.tensor_copy(out=out, in_=in_), r, w)

    def memset(self, ap, val, w, eng="dve"):
        return self.add(eng, lambda e: e.memset(ap, val), (), w)

    def load(self, out, in_, r, w, eng="sp"):
        return self.add(eng, lambda e: e.dma_start(out=out, in_=in_), r, w, dma=True)

    def store(self, out, in_, r, w):
        return self.add("pool", lambda e: e.dma_start(out=out, in_=in_), r, w, dma=True)

    def finalize(self):
        ops = self.ops
        def needs(op, d):
            if d.dma or op.dma or d.eng != op.eng:
                return True
            return SAME_ENGINE_SYNC and op.eng != "pe"

        for op in ops:
            for d in op.deps:
                if needs(op, d):
                    d.sig = True
        cnt = {e: 0 for e in ENGS}
        csem = {e: None for e in ENGS}
        dpool = {e: [[self.new_sem(), 0] for _ in range(DMA_NS)] for e in ("sp", "pool", "act")}
        dn = {"sp": 0, "pool": 0, "act": 0}
        for op in ops:
            if op.dma:
                pool = dpool[op.eng]
                j = dn[op.eng] % DMA_NS
                dn[op.eng] += 1
                ent = pool[j]
                if ent[1] > 0:
                    op.pre = (ent[0], ent[1])
                op.sem = ent[0]
                op.val = ent[1] + 16
                ent[1] = op.val
                if ent[1] > DMA_MAXV:
                    pool[j] = [self.new_sem(), 0]
            elif op.sig:
                e = op.eng
                if csem[e] is None or cnt[e] >= EPOCH:
                    csem[e] = self.new_sem()
                    cnt[e] = 0
                cnt[e] += 1
                op.sem = csem[e]
                op.val = cnt[e]
        waited = {e: {} for e in ENGS}
        nw = 0
        for op in ops:
            need = {}
            if op.pre is not None:
                need[id(op.pre[0])] = [op.pre[0], op.pre[1], None]
            for d in op.deps:
                if needs(op, d):
                    ent = need.get(id(d.sem))
                    if ent is None:
                        need[id(d.sem)] = [d.sem, d.val, d]
                    elif d.val > ent[1]:
                        ent[1] = d.val
                        ent[2] = d
            wl = []
            wd = waited[op.eng]
            for k, (s, v, dsrc) in need.items():
                if wd.get(k, 0) < v:
                    wd[k] = v
                    wl.append((s, v, dsrc))
            op.waits = wl
            nw += len(wl)
        self.n_waits = nw

    def emit(self, block):
        per = {e: [] for e in ENGS}
        for op in self.ops:
            per[op.eng].append(op)
        finals = self.finals

        def run(e, name):
            for op in per[name]:
                for (s, v, dsrc) in op.waits:
                    wi = e.wait_ge(s, v)
                    if TRACE_TAGS:
                        try:
                            self.waitinfo[wi.ins.name] = (op.tag, dsrc.tag + "@" + dsrc.eng if dsrc is not None else "dma-sem")
                        except Exception:
                            pass
                ins = op.fn(e)
                if TRACE_TAGS:
                    try:
                        self.waitinfo[ins.ins.name] = (op.tag, "op")
                    except Exception:
                        pass
                if op.sig:
                    ins.then_inc(op.sem, 16 if op.dma else 1)
            if name == "pool":
                for op in finals:
                    e.wait_ge(op.sem, op.val)

        @block.sync
        def _(e):
            run(e, "sp")

        @block.gpsimd
        def _(e):
            run(e, "pool")

        @block.vector
        def _(e):
            run(e, "dve")

        @block.scalar
        def _(e):
            run(e, "act")

        @block.tensor
        def _(e):
            run(e, "pe")


class Buf:
    __slots__ = ("t", "key")

    def __init__(self, t, key):
        self.t = t
        self.key = key


class Ring:
    def __init__(self, bufs):
        self.bufs = bufs
        self.i = 0

    def next(self):
        b = self.bufs[self.i % len(self.bufs)]
        self.i += 1
        return b


def build_program(L, NB, S, CTX):
    assert S % TT == 0 and CTX % 128 == 0 and CTX <= TT
    NC3 = NB + 1
    NLT = S // TT
    TOT = CTX + S
    NKT = TOT // 128
    NKC = CTX // 128
    NPV = L * PV_L + PV_GLOB
    UPAD = 15
    PPAD = 8

    nc = bass.Bass("TRN2", target_bir_lowering=False)
    dt_ = nc.dram_tensor
    xT_in = dt_("xT", [NB, D, S], F32, kind="ExternalInput").ap()
    cxT_in = dt_("cxT", [NB, D, CTX], F32, kind="ExternalInput").ap()
    cT_in = dt_("cT", [128, KC * NC3], F32, kind="ExternalInput").ap()
    pvec_in = dt_("pvec", [128, NPV], F32, kind="ExternalInput").ap()
    wA_in = dt_("wA", [L, 128, NA], F32, kind="ExternalInput").ap()
    wB_in = dt_("wB", [L, 128, NBW], F32, kind="ExternalInput").ap()
    wM_in = dt_("wM", [L, 128, NMW], F32, kind="ExternalInput").ap()
    rope_in = dt_("rope", [2, 128, S], F32, kind="ExternalInput").ap()
    perm_in = dt_("perm", [128, 128], F32, kind="ExternalInput").ap()
    outT = dt_("outT", [NB, D, S], F32, kind="ExternalOutput").ap()

    wA_bf = dt_("wA_bf", [L, 128, NA], BF16).ap()
    wB_bf = dt_("wB_bf", [L, 128, NBW], BF16).ap()
    wM_bf = dt_("wM_bf", [L, 128, NMW], BF16).ap()
    xs = dt_("xs", [NB, D, S], F32).ap()
    xcs = dt_("xcs", [NB, D, CTX], F32).ap()
    hT_d = dt_("hT_d", [NB, D, TOT], BF16).ap()
    ul_d = dt_("ul_d", [NB, CONV_CH, S + 2 * UPAD], F32).ap()
    uc_d = dt_("uc_d", [NB, CONV_CH, CTX + 2 * UPAD], F32).ap()
    pl_d = dt_("pl_d", [NB, CONV_CH, S + 2 * PPAD], F32).ap()
    pc_d = dt_("pc_d", [NB, CONV_CH, CTX + 2 * PPAD], F32).ap()
    QT_d = dt_("QT_d", [NB, NH, 128, TOT], BF16).ap()
    KT_d = dt_("KT_d", [NB, NH, 128, TOT], BF16).ap()
    V_d = dt_("V_d", [NB, NH, 128, NKT, 128], BF16).ap()
    OT_d = dt_("OT_d", [NB, D, TOT], BF16).ap()

    stack = ExitStack()
    cx = Cx(nc, stack)

    off = [(nc.sbuf_base + 63) // 64 * 64]
    top = nc.sbuf_top
    nbuf = [0]

    def alloc(shape, dtype, at=None):
        n = 1
        for s_ in shape[1:]:
            n *= s_
        nbytes = n * (4 if dtype == F32 else 2)
        nbytes = (nbytes + 63) // 64 * 64
        if at is None:
            o = off[0]
            off[0] += nbytes
            assert off[0] <= top, ("SBUF overflow", off[0], top)
        else:
            o = at
        nbuf[0] += 1
        t = nc.alloc_sbuf_tensor_at("b%d" % nbuf[0], list(shape), dtype, offset=o)
        return Buf(t, ("sb", nbuf[0])), o, nbytes

    def A(shape, dtype):
        return alloc(shape, dtype)[0]

    ones_f = A([128, 128], F32)
    ones_b = A([128, 128], BF16)
    perm_f = A([128, 128], F32)
    perm_b = A([128, 128], BF16)
    epsb = A([128, 1], F32)
    zerob = A([128, 1], F32)
    pvec = A([128, NPV], F32)
    cT = A([128, KC * NC3], F32)
    cs_b = A([128, KC * NC3], BF16)
    modb = A([128, 48 * NC3], F32)
    A1 = A([128, KC * NC3], F32)
    G1 = A([128, KC * NC3], F32)
    A2 = A([128, KC * NC3], F32)
    G2 = A([128, KC * NC3], F32)
    lamt = A([128, 8], F32)

    xt_ring = Ring([A([128, KC, TT], F32) for _ in range(2)])
    hT_ring = Ring([A([128, KC, TT], BF16) for _ in range(2)])
    w_ring = Ring([A([128, WBUF], BF16) for _ in range(3)])
    TFW = TT + 16
    tf_ring = Ring([A([128, TFW], F32) for _ in range(8)])
    lamtmp = tf_ring.bufs[0]
    rs_ring = Ring([A([128, TT], F32) for _ in range(2)])
    tb_ring = Ring([A([128, TT], BF16) for _ in range(4)])
    tf2_ring = Ring([A([128, TFW], F32) for _ in range(3)])
    bg_rs = A([128, TT], F32)
    wgb = A([128, 512], BF16)

    arena0 = off[0]
    o = arena0
    att = []
    for i in range(2):
        kt_, _, n1 = alloc([128, TOT], BF16, at=o); o += n1
        vt_, _, n2 = alloc([128, NKT, 128], BF16, at=o); o += n2
        qt_, _, n3 = alloc([128, TOT], BF16, at=o); o += n3
        att.append((kt_, vt_, qt_))
    pt_list = []
    for i in range(6):
        b_, _, n1 = alloc([128, 2, TT], BF16, at=o); o += n1
        pt_list.append(b_)
    pt_ring = Ring(pt_list)
    paccA, _, n1 = alloc([128, 2, TT], F32, at=o); o += n1
    paccB, _, n1 = alloc([128, 2, TT], F32, at=o); o += n1
    zsum, _, n1 = alloc([128, 2, TT], F32, at=o); o += n1
    rzb, _, n1 = alloc([128, 2, TT], F32, at=o); o += n1
    osb, _, n1 = alloc([128, 2, TT], F32, at=o); o += n1
    att_extra = [paccA, paccB, zsum, rzb, osb]
    att_extra_keys = [(paccA.key, 'd'), (paccA.key, 'p'), (paccB.key, 'd'), (paccB.key, 'p')]
    att_end = o
    o = arena0
    vtok_l = []
    for i in range(4):
        b_, _, n1 = alloc([128, D], BF16, at=o); o += n1
        vtok_l.append(b_)
    vtok_ring = Ring(vtok_l)
    rope_l = []
    for i in range(2):
        b_, _, n1 = alloc([128, 2, TT], F32, at=o); o += n1
        rope_l.append(b_)
    rope_ring = Ring(rope_l)
    pa_end = o
    o = arena0
    uwin, _, n1 = alloc([128, 4, TT + 2 * UPAD], F32, at=o); o += n1
    pwin, _, n1 = alloc([128, 4, TT + 2 * PPAD], F32, at=o); o += n1
    cvb, _, n1 = alloc([128, 4, TT], F32, at=o); o += n1
    sbuf_, _, n1 = alloc([128, 4, TT], BF16, at=o); o += n1
    mxb, _, n1 = alloc([128, 4, TT], BF16, at=o); o += n1
    pdb, _, n1 = alloc([128, 4, TT], BF16, at=o); o += n1
    mb, _, n1 = alloc([128, KC, TT], BF16, at=o); o += n1
    yb, _, n1 = alloc([128, KC, TT], F32, at=o); o += n1
    actb, o_act, n1 = alloc([128, NJ, TT], BF16, at=o); o += n1
    OTt, _, _ = alloc([128, KC, TT], BF16, at=o_act)
    OTt.key = actb.key
    lnm, _, n1 = alloc([128, TT], F32, at=o); o += n1
    if CONV_DVE_TAPS < CONV_W:
        cv2, _, n1 = alloc([128, TT], F32, at=o); o += n1
    else:
        cv2 = lnm
    pb_end = o
    arena_end = max(att_end, pa_end, pb_end)
    assert arena_end <= top, ("SBUF overflow arena", arena_end, top)
    ARENA = ("arena",)
    VIEW_ATT, VIEW_A, VIEW_B = ("view", "att"), ("view", "a"), ("view", "b")

    class HalfView:
        def __init__(self, t3, h):
            self.t3, self.h = t3, h

        def __getitem__(self, idx):
            r_, c_ = idx
            return self.t3[r_, self.h, c_]

    pp = [Buf(stack.enter_context(nc.psum_tensor("pp%d" % i, [128, 2, TT], F32)), ("pp", i)) for i in range(4)]
    ps = [Buf(HalfView(pp[i // 2].t, i % 2), ("ps", i)) for i in range(8)]
    ps_ring = Ring(ps[0:7])

    cur_view = [None]
    view_keys = {
        VIEW_ATT: [b.key for trio in att for b in trio] + [b.key for b in pt_list] + [b.key for b in att_extra] + att_extra_keys,
        VIEW_A: [b.key for b in vtok_l] + [(b.key, h_) for b in vtok_l for h_ in (0, 1)] + [b.key for b in rope_l],
        VIEW_B: [b.key for b in (uwin, pwin, cvb, sbuf_, mxb, pdb, mb, yb, actb, lnm)],
    }

    def switch_view(v):
        if cur_view[0] == v:
            return
        old = cur_view[0]
        cur_view[0] = v
        if old is None:
            return
        keys = view_keys[old] + view_keys[v]
        cx.memset(lamt.t[:, 7:8], 0.0, w=keys + [("fence",)])

    cx.load(pvec.t[:, :], pvec_in[:, :], r=[], w=[pvec.key])
    cx.load(cT.t[:, :], cT_in[:, :], r=[], w=[cT.key])
    cx.load(perm_f.t[:, :], perm_in[:, :], r=[], w=[perm_f.key])
    cx.memset(ones_f.t[:, :], 1.0, w=[ones_f.key])
    cx.memset(ones_b.t[:, :], 1.0, w=[ones_b.key])
    cx.memset(epsb.t[:, :], EPS, w=[epsb.key])
    cx.memset(zerob.t[:, :], 0.0, w=[zerob.key])
    cx.copy(perm_b.t[:, :], perm_f.t[:, :], r=[perm_f.key], w=[perm_b.key])
    tf = tf_ring.next()
    cx.act(tf.t[:, 0:KC * NC3], cT.t[:, :], AF.Sigmoid, r=[cT.key], w=[tf.key])
    cx.tt(cs_b.t[:, :], cT.t[:, :], tf.t[:, 0:KC * NC3], ALU.mult, r=[cT.key, tf.key], w=[cs_b.key])
    zt = tf_ring.next()
    cx.memset(zt.t[:, :], 0.0, w=[zt.key])
    for b in range(NB):
        for (dd, n_, pad, nm) in ((ul_d, S, UPAD, "ul"), (uc_d, CTX, UPAD, "uc"), (pl_d, S, PPAD, "pl"),
                                  (pc_d, CTX, PPAD, "pc")):
            v = dd[b].rearrange("(c p) t -> p c t", p=128)
            cx.store(v[:, :, 0:pad], zt.t[:, 0:4 * pad].rearrange("p (c t) -> p c t", c=4),
                     r=[zt.key], w=[(nm + "pad", b, 0)])
            cx.store(v[:, :, pad + n_:pad + n_ + pad], zt.t[:, 0:4 * pad].rearrange("p (c t) -> p c t", c=4),
                     r=[zt.key], w=[(nm + "pad", b, 1)])
    CW = 8192
    cast_q = []

    def queue_casts(l):
        for (src, dst, n_, nm) in ((wM_in, wM_bf, NMW, "wM"), (wA_in, wA_bf, NA, "wA"), (wB_in, wB_bf, NBW, "wB")):
            c0 = 0
            while c0 < n_:
                c1 = min(n_, c0 + CW)
                cast_q.append((dst[l][:, c0:c1], src[l][:, c0:c1], (nm, l, c0 // CW)))
                c0 = c1

    def drain_casts(n):
        while cast_q and n > 0:
            d_, s_, k_ = cast_q.pop(0)
            cx.store(d_, s_, r=[], w=[k_])
            n -= 1

    queue_casts(0)
    drain_casts(10 ** 9)

    def wkeys(nm, l, c0, c1):
        return [(nm, l, i) for i in range(c0 // CW, (c1 - 1) // CW + 1)]

    class WStream:
        def __init__(self, nm, dram, l):
            self.nm, self.dram, self.l, self.pos = nm, dram, l, 0

        def next(self, n):
            b = w_ring.next()
            c0, c1 = self.pos, self.pos + n
            cx.load(b.t[:, 0:n], self.dram[self.l][:, c0:c1], r=wkeys(self.nm, self.l, c0, c1), w=[b.key])
            self.pos = c1
            return b

    def pv(l, o_, n=1):
        return pvec.t[:, l * PV_L + o_: l * PV_L + o_ + n]

    def col(bufap, kc, j):
        return bufap.t[:, kc * NC3 + j: kc * NC3 + j + 1]

    def modcol(n, j):
        return modb.t[:, n * NC3 + j: n * NC3 + j + 1]

    def phase_mod(l):
        lam_init = 0.8 - 0.6 * math.exp(-0.3 * l)
        ws = WStream("wM", wM_bf, l)
        n = 0
        while n < 48:
            g = min(5, 48 - n)
            wb = ws.next(g * 1024)
            for i in range(g):
                p_ = ps_ring.next()
                for kc in range(KC):
                    cx.mm(p_.t[:, 0:NC3], wb.t[:, i * 1024 + kc * 128: i * 1024 + (kc + 1) * 128],
                          cs_b.t[:, kc * NC3:(kc + 1) * NC3], kc == 0, kc == KC - 1,
                          r=[wb.key, cs_b.key], w=[p_.key])
                cx.ts(modb.t[:, (n + i) * NC3:(n + i + 1) * NC3], p_.t[:, 0:NC3], pv(l, PV_BMOD + n + i), None,
                      ALU.add, None, r=[p_.key, pvec.key], w=[modb.key])
            n += g
        for kc in range(KC):
            sl = slice(kc * NC3, (kc + 1) * NC3)
            cx.ts(A1.t[:, sl], modb.t[:, (8 + kc) * NC3:(9 + kc) * NC3], pv(l, PV_G + 0 + kc), pv(l, PV_G + 0 + kc), ALU.mult, ALU.add,
                  r=[modb.key, pvec.key], w=[A1.key])
            cx.ts(G1.t[:, sl], modb.t[:, (16 + kc) * NC3:(17 + kc) * NC3], pv(l, PV_G + 8 + kc), None, ALU.mult, None,
                  r=[modb.key, pvec.key], w=[G1.key])
            cx.ts(A2.t[:, sl], modb.t[:, (32 + kc) * NC3:(33 + kc) * NC3], pv(l, PV_G + 16 + kc), pv(l, PV_G + 16 + kc), ALU.mult, ALU.add,
                  r=[modb.key, pvec.key], w=[A2.key])
            cx.ts(G2.t[:, sl], modb.t[:, (40 + kc) * NC3:(41 + kc) * NC3], pv(l, PV_G + 24 + kc), None, ALU.mult, None,
                  r=[modb.key, pvec.key], w=[G2.key])
        for i in range(2):
            cx.tt(lamtmp.t[:, 0:64], pv(l, PV_LAM + 128 * i, 64), pv(l, PV_LAM + 128 * i + 64, 64), ALU.mult,
                  r=[pvec.key], w=[lamtmp.key])
            cx.add("dve", (lambda i_: lambda e: e.reduce_sum(out=lamt.t[:, 5 + i_:6 + i_], in_=lamtmp.t[:, 0:64],
                                                             axis=mybir.AxisListType.X))(i),
                   r=[lamtmp.key], w=[lamt.key])
        cx.act(lamt.t[:, 0:2], lamt.t[:, 5:7], AF.Exp, r=[lamt.key], w=[lamt.key])
        cx.tt(lamt.t[:, 2:3], lamt.t[:, 0:1], lamt.t[:, 1:2], ALU.subtract, r=[lamt.key], w=[lamt.key])
        cx.ts(lamt.t[:, 3:4], lamt.t[:, 2:3], -1.0, -lam_init, ALU.mult, ALU.add, r=[lamt.key], w=[lamt.key])
        cx.ts(lamt.t[:, 4:5], pv(l, PV_SUBG), 1.0 - lam_init, None, ALU.mult, None, r=[pvec.key], w=[lamt.key])

    class Tile:
        pass

    def tiles_for(b):
        res = []
        t = Tile()
        t.kind, t.b, t.id, t.T, t.t0, t.c0, t.j = "c", b, "c", CTX, 0, 0, NB
        t.first, t.last = True, True
        res.append(t)
        for i in range(NLT):
            t = Tile()
            t.kind, t.b, t.id, t.T, t.t0, t.c0, t.j = "l", b, i, TT, i * TT, CTX + i * TT, b
            t.first, t.last = (i == 0), (i == NLT - 1)
            res.append(t)
        return res

    def xsrc(l, t):
        if t.kind == "c":
            return (cxT_in if l == 0 else xcs)[t.b]
        return (xT_in if l == 0 else xs)[t.b]

    def xdst(l, t):
        if t.kind == "c":
            return xcs[t.b]
        return (outT if l == L - 1 else xs)[t.b]

    def rstd_from_ps(p_, T, n):
        sd = tf_ring.next()
        cx.act(sd.t[:, 0:T], p_.t[:, 0:T], AF.Ln, r=[p_.key, epsb.key], w=[sd.key],
               bias=epsb.t[:, 0:1], scale=1.0 / n)
        rs = rs_ring.next()
        cx.act(rs.t[:, 0:T], sd.t[:, 0:T], AF.Exp, r=[sd.key], w=[rs.key], scale=-0.5)
        return rs

    def sumsq_stats(src, T, nchunks):
        p_ = ps_ring.next()
        for c in range(nchunks):
            sq = tf_ring.next()
            cx.act(sq.t[:, 0:T], src.t[:, c, 0:T], AF.Square, r=[src.key], w=[sq.key])
            cx.mm(p_.t[:, 0:T], ones_f.t[:, :], sq.t[:, 0:T], c == 0, c == nchunks - 1,
                  r=[ones_f.key, sq.key], w=[p_.key])
        return p_

    def adaln(xt, T, j, Asc, shift_n0, hT):
        p_ = ps_ring.next()
        for c in range(KC):
            cx.act(hT.t[:, c, 0:T], xt.t[:, c, 0:T], AF.Square, r=[xt.key], w=[hT.key])
        for c in range(KC):
            cx.mm(p_.t[:, 0:T], ones_b.t[:, :], hT.t[:, c, 0:T], c == 0, c == KC - 1,
                  r=[ones_b.key, hT.key], w=[p_.key])
        rs = rstd_from_ps(p_, T, D)
        for kc in range(KC):
            t1 = tf_ring.next()
            cx.stt(t1.t[:, 0:T], xt.t[:, kc, 0:T], col(Asc, kc, j), rs.t[:, 0:T], ALU.mult, ALU.mult,
                   r=[xt.key, Asc.key, rs.key], w=[t1.key])
            cx.act(hT.t[:, kc, 0:T], t1.t[:, 0:T], AF.Identity, r=[t1.key, modb.key], w=[hT.key],
                   bias=modcol(shift_n0 + kc, j))

    def adaln_bg(xt, T, j, Asc, shift_n0, hT):
        p_ = ps[7]
        for c in range(KC):
            cx.act(hT.t[:, c, 0:T], xt.t[:, c, 0:T], AF.Square, r=[xt.key], w=[hT.key])
            if c % 2 == 1:
                yield
        yield
        for c in range(KC):
            cx.mm(p_.t[:, 0:T], ones_b.t[:, :], hT.t[:, c, 0:T], c == 0, c == KC - 1,
                  r=[ones_b.key, hT.key], w=[p_.key])
        sd = tf2_ring.next()
        cx.act(sd.t[:, 0:T], p_.t[:, 0:T], AF.Ln, r=[p_.key, epsb.key], w=[sd.key], bias=epsb.t[:, 0:1], scale=1.0 / D)
        rs = bg_rs
        cx.act(rs.t[:, 0:T], sd.t[:, 0:T], AF.Exp, r=[sd.key], w=[rs.key], scale=-0.5)
        yield
        for kc in range(KC):
            t1 = tf2_ring.next()
            cx.stt(t1.t[:, 0:T], xt.t[:, kc, 0:T], col(Asc, kc, j), rs.t[:, 0:T], ALU.mult, ALU.mult,
                   r=[xt.key, Asc.key, rs.key], w=[t1.key])
            cx.act(hT.t[:, kc, 0:T], t1.t[:, 0:T], AF.Identity, r=[t1.key, modb.key], w=[hT.key],
                   bias=modcol(shift_n0 + kc, j))
            yield

    def drain(g):
        if g is not None:
            for _ in g:
                pass

    def make_bg(g, n):
        def bg(m=None):
            if g is None:
                return
            for _ in range(n if m is None else m):
                try:
                    next(g)
                except StopIteration:
                    return
        return bg

    def proj(wb, woff, nk, rhs_buf, T, extra_r=()):
        p_ = ps_ring.next()
        for kc in range(nk):
            cx.mm(p_.t[:, 0:T], wb.t[:, woff + kc * 128: woff + (kc + 1) * 128], rhs_buf.t[:, kc, 0:T],
                  kc == 0, kc == nk - 1, r=[wb.key, rhs_buf.key] + list(extra_r), w=[p_.key])
        return p_

    def a1_gen(l, t, st):
        drain_casts(1)
        cx.sec = 'A1'
        b, T, j = t.b, t.T, t.j
        xt = xt_ring.next()
        cx.load(xt.t[:, :, 0:T], xsrc(l, t).rearrange("(kc p) t -> p kc t", p=128)[:, :, t.t0:t.t0 + T],
                r=[("x", t.kind, b, t.id)], w=[xt.key], eng=XLOAD_ENG)
        hT = hT_ring.next()
        st["hT"] = hT
        yield
        for _ in adaln_bg(xt, T, j, A1, 0, hT):
            yield
            cx.sec = 'A1'
        cx.store(hT_d[b].rearrange("(kc p) t -> p kc t", p=128)[:, :, t.c0:t.c0 + T], hT.t[:, :, 0:T],
                 r=[hT.key], w=[("hT", b, t.id)])
        yield

    def phase_a2(l, t, st, bg_):
        def bg():
            bg_()
            cx.sec = 'A2'
        cx.sec = 'A2'
        b, T, j = t.b, t.T, t.j
        hT = st["hT"]
        if t.kind == "l":
            rp = rope_ring.next()
            cx.load(rp.t[:, :, 0:T], rope_in.rearrange("a p t -> p a t")[:, :, t.t0:t.t0 + T], r=[], w=[rp.key, VIEW_A],
                    eng=XLOAD_ENG)
        ws = WStream("wA", wA_bf, l)
        u_d = (uc_d if t.kind == "c" else ul_d)[b]
        unm = "uc" if t.kind == "c" else "ul"
        for half in range(2):
            wb = ws.next(4096)
            for ci in range(2):
                c = half * 2 + ci
                pa = proj(wb, (2 * ci) * 1024, KC, hT, T)
                pb = proj(wb, (2 * ci + 1) * 1024, KC, hT, T)
                sg = tf_ring.next()
                cx.act(sg.t[:, 0:T], pb.t[:, 0:T], AF.Sigmoid, r=[pb.key], w=[sg.key])
                u = tf_ring.next()
                cx.tt(u.t[:, 0:T], pa.t[:, 0:T], sg.t[:, 0:T], ALU.mult, r=[pa.key, sg.key], w=[u.key])
                cx.store(u_d[c * 128:(c + 1) * 128, UPAD + t.t0: UPAD + t.t0 + T], u.t[:, 0:T],
                         r=[u.key], w=[(unm, b, t.id)])
                bg()
        rope_pend = []
        for (dst, nm) in ((QT_d, "Q"), (KT_d, "K")):
            for half in range(2):
                wb = ws.next(4096)
                for hi in range(4):
                    h = half * 4 + hi
                    pq = proj(wb, hi * 1024, KC, hT, T)
                    qb = tb_ring.next()
                    cx.act(qb.t[:, 0:T], pq.t[:, 0:T], AF.Identity, r=[pq.key, zerob.key],
                           w=[qb.key, ("lock", pq.key)], bias=zerob.t[:, 0:1])
                    if t.kind == "l":
                        t1 = tf_ring.next()
                        cx.tt(t1.t[:, 0:T], pq.t[:, 0:T], rp.t[:, 0, 0:T], ALU.mult,
                              r=[pq.key, rp.key, ("lock", pq.key)], w=[t1.key])

                        def rope_tail(qb=qb, t1=t1, dst=dst, nm=nm, h=h):
                            psw = ps_ring.next()
                            cx.mm(psw.t[:, 0:T], perm_b.t[:, :], qb.t[:, 0:T], True, True,
                                  r=[perm_b.key, qb.key], w=[psw.key])
                            t2 = tf_ring.next()
                            cx.tt(t2.t[:, 0:T], psw.t[:, 0:T], rp.t[:, 1, 0:T], ALU.mult, r=[psw.key, rp.key], w=[t2.key])
                            qr = tb_ring.next()
                            cx.tt(qr.t[:, 0:T], t1.t[:, 0:T], t2.t[:, 0:T], ALU.add, r=[t1.key, t2.key], w=[qr.key], eng="pool")
                            cx.store(dst[b, h][:, t.c0:t.c0 + T], qr.t[:, 0:T], r=[qr.key], w=[(nm, b, h, t.id)])

                        if rope_pend:
                            rope_pend.pop(0)()
                        rope_pend.append(rope_tail)
                    else:
                        cx.store(dst[b, h][:, t.c0:t.c0 + T], qb.t[:, 0:T], r=[qb.key], w=[(nm, b, h, t.id)])
                    bg()
        while rope_pend:
            rope_pend.pop(0)()
        p_d = (pc_d if t.kind == "c" else pl_d)[b]
        pnm = "pc" if t.kind == "c" else "pl"
        wb = ws.next(4096)
        for c in range(4):
            pp = proj(wb, c * 1024, KC, hT, T)
            pf = tf_ring.next()
            cx.copy(pf.t[:, 0:T], pp.t[:, 0:T], r=[pp.key], w=[pf.key])
            cx.store(p_d[c * 128:(c + 1) * 128, PPAD + t.t0: PPAD + t.t0 + T], pf.t[:, 0:T],
                     r=[pf.key], w=[(pnm, b, t.id)])
            bg()
        nts = T // 128
        vts = [vtok_ring.next() for _ in range(nts)]
        for nh in range(2):
            wvh = ws.next(4096)
            for tsi in range(nts):
                vt = vts[tsi]
                p_ = ps_ring.next()
                for kc in range(KC):
                    cx.mm(p_.t[:, :], hT.t[:, kc, tsi * 128:(tsi + 1) * 128], wvh.t[:, kc * 512:(kc + 1) * 512],
                          kc == 0, kc == KC - 1, r=[hT.key, wvh.key], w=[p_.key])
                if (tsi + nh) % 2 == 0:
                    cx.act(vt.t[:, nh * 512:(nh + 1) * 512], p_.t[:, :], AF.Identity, r=[p_.key, zerob.key],
                           w=[(vt.key, nh), VIEW_A], bias=zerob.t[:, 0:1])
                else:
                    cx.copy(vt.t[:, nh * 512:(nh + 1) * 512], p_.t[:, :], r=[p_.key], w=[(vt.key, nh), VIEW_A])
                if nh == 1:
                    kt = t.c0 // 128 + tsi
                    cx.store(V_d[b].rearrange("h p k d -> p h k d")[:, :, kt, :],
                             vt.t[:, :].rearrange("p (h d) -> p h d", h=NH),
                             r=[(vt.key, 0), (vt.key, 1)], w=[("V", b, kt), vt.key])

    def attention(l, b):
        switch_view(VIEW_ATT)
        cx.sec = 'ATT'
        need_ctx = l < L - 1
        tl = tiles_for(b)
        all_ids = [t.id for t in tl]
        sq_ = []
        lru = list(pp[0:3])

        class _Pairs:
            def next(self_):
                for p_ in lru:
                    if all(p_ is not q_ for q_ in sq_):
                        lru.remove(p_)
                        lru.append(p_)
                        return p_
                raise AssertionError("no free PSUM pair")

        spairs = _Pairs()
        O0, O1 = ps[6], ps[7]
        deferred = []

        def hk(p_):
            i_ = int(p_.key[1])
            return [("ps", 2 * i_), ("ps", 2 * i_ + 1)]

        def tick(allow=True):
            for d_ in deferred:
                d_[0] -= 1
            if allow:
                for d_ in deferred:
                    if d_[0] <= 0:
                        deferred.remove(d_)
                        d_[1]()
                        break

        def flush():
            while deferred:
                deferred.pop(0)[1]()

        def flush_p2():
            pend = [d_ for d_ in deferred if d_[2] == 2]
            for d_ in pend:
                deferred.remove(d_)
                d_[1]()

        def part2(h, t):
            T = t.T
            zp = spairs.next()
            for i in range(2):
                cx.mm(zp.t[:, i, 0:T], ones_f.t[:, :], zsum.t[:, i, 0:T], True, True,
                      r=[ones_f.key, zsum.key], w=[hk(zp)[i]])
            cx.act(rzb.t[:, :, 0:T], zp.t[:, :, 0:T], AF.Ln, r=hk(zp), w=[rzb.key])
            cx.act(rzb.t[:, :, 0:T], rzb.t[:, :, 0:T], AF.Exp, r=[rzb.key], w=[rzb.key], scale=-1.0)
            t0_ = tf_ring.next()
            cx.tt(t0_.t[:, 0:T], osb.t[:, 0, 0:T], rzb.t[:, 0, 0:T], ALU.mult, r=[osb.key, rzb.key], w=[t0_.key])
            t1_ = tf_ring.next()
            cx.tt(t1_.t[:, 0:T], osb.t[:, 1, 0:T], rzb.t[:, 1, 0:T], ALU.mult, r=[osb.key, rzb.key], w=[t1_.key])
            o_ = tf_ring.next()
            cx.stt(o_.t[:, 0:T], t1_.t[:, 0:T], lamt.t[:, 3:4], t0_.t[:, 0:T], ALU.mult, ALU.add,
                   r=[t1_.key, lamt.key, t0_.key], w=[o_.key])
            osq = tf_ring.next()
            cx.tt(osq.t[:, 0:T], o_.t[:, 0:T], o_.t[:, 0:T], ALU.mult, r=[o_.key], w=[osq.key])
            deferred.append([int(_os.environ.get('KT3', '8')), lambda: part3(h, t, o_, osq), 3])

        def part3(h, t, o_, osq):
            T = t.T
            zp = spairs.next()
            cx.mm(zp.t[:, 0, 0:T], ones_f.t[:, :], osq.t[:, 0:T], True, True, r=[ones_f.key, osq.key],
                  w=[hk(zp)[0]])
            sd = tf_ring.next()
            cx.act(sd.t[:, 0:T], zp.t[:, 0, 0:T], AF.Ln, r=[hk(zp)[0], epsb.key], w=[sd.key],
                   bias=epsb.t[:, 0:1], scale=1.0 / 128)
            rs = rs_ring.next()
            cx.act(rs.t[:, 0:T], sd.t[:, 0:T], AF.Exp, r=[sd.key], w=[rs.key], scale=-0.5)
            ob = tb_ring.next()
            cx.stt(ob.t[:, 0:T], o_.t[:, 0:T], lamt.t[:, 4:5], rs.t[:, 0:T], ALU.mult, ALU.mult,
                   r=[o_.key, lamt.key, rs.key], w=[ob.key])
            cx.store(OT_d[b][h * 128:(h + 1) * 128, t.c0:t.c0 + T], ob.t[:, 0:T], r=[ob.key], w=[("OT", b, h, t.id)])

        for h in range(NH):
            KT, VT, QT = att[h % 2]
            cx.load(KT.t[:, :], KT_d[b, h][:, :], r=[("K", b, h, i) for i in all_ids], w=[KT.key, VIEW_ATT])
            cx.load(VT.t[:, :, :], V_d[b, h][:, :, :], r=[("V", b, k) for k in range(NKT)], w=[VT.key, VIEW_ATT])
            cx.load(QT.t[:, :], QT_d[b, h][:, :], r=[("Q", b, h, i) for i in all_ids], w=[QT.key, VIEW_ATT])
            tls = [t for t in tl if not (t.kind == "c" and not need_ctx)]
            steps = []
            for t in tls:
                nk = NKC if t.kind == "c" else NKT
                for kt in range(nk):
                    steps.append((t, kt, nk))

            def qk(si):
                t_, kt_, _ = steps[si]
                T_ = t_.T
                sp = spairs.next()
                ks = slice(kt_ * 128, (kt_ + 1) * 128)
                qs_ = slice(t_.c0, t_.c0 + T_)
                cx.mm(sp.t[:, 0, 0:T_], KT.t[0:64, ks], QT.t[0:64, qs_], True, True, r=[KT.key, QT.key],
                      w=[hk(sp)[0]], tp=(0, 0))
                cx.mm(sp.t[:, 1, 0:T_], KT.t[64:128, ks], QT.t[64:128, qs_], True, True, r=[KT.key, QT.key],
                      w=[hk(sp)[1]], tp=(64, 0))
                return sp

            XL = _os.environ.get('KXL', '1') == '1'
            del sq_[:]
            if XL:
                sq_.append(qk(0))
                if len(steps) > 1:
                    sq_.append(qk(1))
            for si, (t, kt, nk) in enumerate(steps):
                T = t.T
                if not XL and kt == 0:
                    del sq_[:]
                    sq_.append(qk(si))
                    if nk > 1:
                        sq_.append(qk(si + 1))
                if TRACE_TAGS == 2:
                    cx.sec = 'ATT.l%d.b%d.h%d.%s.%d' % (l, b, h, t.id, kt)
                sp = sq_.pop(0)
                pt = pt_ring.next()
                cx.act(pt.t[:, :, 0:T], sp.t[:, :, 0:T], AF.Exp, r=hk(sp), w=[pt.key], scale=0.125)
                if (si + 2 < len(steps)) if XL else (kt + 2 < nk):
                    sq_.append(qk(si + 2))
                st_, sp_ = (kt == 0), (kt == nk - 1)
                cx.mm(O0.t[:, 0:T], VT.t[:, kt, :], pt.t[:, 0, 0:T], st_, sp_,
                      r=[VT.key, pt.key], w=[O0.key])
                cx.mm(O1.t[:, 0:T], VT.t[:, kt, :], pt.t[:, 1, 0:T], st_, sp_,
                      r=[VT.key, pt.key], w=[O1.key])
                acc = paccA if kt % 2 == 0 else paccB
                TD = T if T < TT else ZSPLIT
                for (eng_, c0_, c1_, kx) in (("dve", 0, TD, "d"), ("pool", TD, T, "p")):
                    if c1_ <= c0_:
                        continue
                    if kt < 2:
                        cx.copy(acc.t[:, :, c0_:c1_], pt.t[:, :, c0_:c1_], r=[pt.key], w=[(acc.key, kx)], eng=eng_)
                    else:
                        cx.tt(acc.t[:, :, c0_:c1_], acc.t[:, :, c0_:c1_], pt.t[:, :, c0_:c1_], ALU.add,
                              r=[(acc.key, kx), pt.key], w=[(acc.key, kx)], eng=eng_)
                tick(allow=(kt < nk - 1))
                if kt < nk - 1:
                    continue
                flush_p2()
                cx.act(osb.t[:, :, 0:T], pp[3].t[:, :, 0:T], AF.Identity, r=[O0.key, O1.key, zerob.key],
                       w=[osb.key], bias=zerob.t[:, 0:1])
                ak = [(paccA.key, "d"), (paccA.key, "p"), (paccB.key, "d"), (paccB.key, "p")]
                if nk > 1:
                    cx.tt(zsum.t[:, :, 0:T], paccA.t[:, :, 0:T], paccB.t[:, :, 0:T], ALU.add,
                          r=ak, w=[zsum.key])
                else:
                    cx.copy(zsum.t[:, :, 0:T], paccA.t[:, :, 0:T], r=ak, w=[zsum.key])
                deferred.append([int(_os.environ.get('KT2', '4')), (lambda h_, t_: lambda: part2(h_, t_))(h, t), 2])
        flush()

    def b1_gen(l, t, st):
        for _ in b1_gen_(l, t, st):
            yield
            cx.sec = 'B1'

    def b1_gen_(l, t, st):
        cx.sec = 'B1'
        drain_casts(1)
        b, T, j = t.b, t.T, t.j
        VB = VIEW_B
        hT = hT_ring.next()
        cx.load(hT.t[:, :, 0:T], hT_d[b].rearrange("(kc p) t -> p kc t", p=128)[:, :, t.c0:t.c0 + T],
                r=[("hT", b, t.id)], w=[hT.key])
        xt = xt_ring.next()
        cx.load(xt.t[:, :, 0:T], xsrc(l, t).rearrange("(kc p) t -> p kc t", p=128)[:, :, t.t0:t.t0 + T],
                r=[("x", t.kind, b, t.id)], w=[xt.key])
        st["hT"], st["xt"] = hT, xt
        if t.kind == "c":
            u_d, p_d, unm, pnm = uc_d[b], pc_d[b], "uc", "pc"
            nb_ids = ["c"]
        else:
            u_d, p_d, unm, pnm = ul_d[b], pl_d[b], "ul", "pl"
            nb_ids = [i for i in (t.id - 1, t.id, t.id + 1) if 0 <= i < NLT]
        cx.load(uwin.t[:, :, 0:T + 2 * UPAD], u_d.rearrange("(c p) t -> p c t", p=128)[:, :, t.t0:t.t0 + T + 2 * UPAD],
                r=[(unm, b, i) for i in nb_ids] + [(unm + "pad", b, 0), (unm + "pad", b, 1)], w=[uwin.key, VB])
        cx.load(pwin.t[:, :, 0:T + 2 * PPAD], p_d.rearrange("(c p) t -> p c t", p=128)[:, :, t.t0:t.t0 + T + 2 * PPAD],
                r=[(pnm, b, i) for i in nb_ids] + [(pnm + "pad", b, 0), (pnm + "pad", b, 1)], w=[pwin.key, VB])
        cx.load(wgb.t[:, :], wB_bf[l][:, 0:512], r=wkeys("wB", l, 0, 512), w=[wgb.key])
        yield
        for cp in ((0, 1), (2, 3)):
            for c in cp:
                cw0 = PV_CONVW + c * CONV_W
                cx.ts(cvb.t[:, c, 0:T], uwin.t[:, c, 0:T], pv(l, cw0), pv(l, PV_CONVB + c), ALU.mult, ALU.add,
                      r=[uwin.key, pvec.key], w=[(cvb.key, c), cvb.key, VB])
            for k in range(1, CONV_W):
                for c in cp:
                    cw0 = PV_CONVW + c * CONV_W
                    acc = cvb.t[:, c, 0:T]
                    cx.stt(acc, uwin.t[:, c, k:k + T], pv(l, cw0 + k), acc, ALU.mult, ALU.add,
                           r=[uwin.key, pvec.key, (cvb.key, c)], w=[(cvb.key, c)])
                yield
            for c in cp:
                cx.memset(lamt.t[:, 7:8], 0.0, w=[cvb.key] + [(cvb.key, c)])
            yield
        W2 = T + 2 * PPAD
        for g in range(4):
            wwin = 2 << g
            pw = pwin.t[:, g, :]
            cur = tf2_ring.next()
            cx.tt(cur.t[:, 1:W2], pw[:, 0:W2 - 1], pw[:, 1:W2], ALU.add, r=[pwin.key], w=[cur.key])
            lo, hi = 1, W2
            sh = 1
            for step in range(g):
                nx = tf2_ring.next()
                cx.tt(nx.t[:, lo + sh:hi - sh], cur.t[:, lo:hi - 2 * sh], cur.t[:, lo + 2 * sh:hi], ALU.add,
                      r=[cur.key], w=[nx.key])
                lo, hi = lo + sh, hi - sh
                cur = nx
                sh *= 2
            ctr = cur.t[:, PPAD:PPAD + T]
            gpv = pvec.t[:, L * PV_L + PV_CF + g * 8: L * PV_L + PV_CF + g * 8 + 8]
            gpl = pvec.t[:, L * PV_L + PV_CL + g * 8: L * PV_L + PV_CL + g * 8 + 8]
            if t.first:
                cx.tt(cur.t[:, PPAD:PPAD + 8], cur.t[:, PPAD:PPAD + 8], gpv, ALU.mult, r=[cur.key, pvec.key], w=[cur.key])
            if t.last:
                cx.tt(cur.t[:, PPAD + T - 8:PPAD + T], cur.t[:, PPAD + T - 8:PPAD + T], gpl, ALU.mult,
                      r=[cur.key, pvec.key], w=[cur.key])
            cx.stt(pdb.t[:, g, 0:T], ctr, 1.0 / wwin, pw[:, PPAD:PPAD + T], ALU.mult, ALU.subtract,
                   r=[cur.key, pwin.key], w=[pdb.key, VB])
            yield

    def b1_fin(l, t, st):
        cx.sec = 'B1f'
        b, T, j = t.b, t.T, t.j
        VB = VIEW_B
        pm = ps_ring.next()
        for c in range(4):
            cx.mm(pm.t[:, 0:T], ones_f.t[:, :], cvb.t[:, c, 0:T], c == 0, c == 3, r=[ones_f.key, cvb.key], w=[pm.key])
        pq_ = ps_ring.next()
        for c in range(4):
            sq = tf_ring.next()
            cx.act(sq.t[:, 0:T], cvb.t[:, c, 0:T], AF.Square, r=[cvb.key], w=[sq.key])
            cx.mm(pq_.t[:, 0:T], ones_f.t[:, :], sq.t[:, 0:T], c == 0, c == 3, r=[ones_f.key, sq.key], w=[pq_.key])
        mean = lnm
        cx.ts(mean.t[:, 0:T], pm.t[:, 0:T], 1.0 / CONV_CH, None, ALU.mult, None, r=[pm.key], w=[mean.key])
        msq = tf_ring.next()
        cx.tt(msq.t[:, 0:T], mean.t[:, 0:T], mean.t[:, 0:T], ALU.mult, r=[mean.key], w=[msq.key])
        var = tf_ring.next()
        cx.stt(var.t[:, 0:T], pq_.t[:, 0:T], 1.0 / CONV_CH, msq.t[:, 0:T], ALU.mult, ALU.subtract,
               r=[pq_.key, msq.key], w=[var.key])
        sd = tf_ring.next()
        cx.act(sd.t[:, 0:T], var.t[:, 0:T], AF.Ln, r=[var.key, epsb.key], w=[sd.key], bias=epsb.t[:, 0:1], scale=1.0)
        rs = rs_ring.next()
        cx.act(rs.t[:, 0:T], sd.t[:, 0:T], AF.Exp, r=[sd.key], w=[rs.key], scale=-0.5)
        for c in range(4):
            z = tf_ring.next()
            cx.tt(z.t[:, 0:T], cvb.t[:, c, 0:T], mean.t[:, 0:T], ALU.subtract, r=[cvb.key, mean.key], w=[z.key])
            cx.tt(z.t[:, 0:T], z.t[:, 0:T], rs.t[:, 0:T], ALU.mult, r=[z.key, rs.key], w=[z.key])
            if USE_SILU:
                cx.act(sbuf_.t[:, c, 0:T], z.t[:, 0:T], AF.Silu, r=[z.key, pvec.key], w=[sbuf_.key, VB],
                       scale=pv(l, PV_LNG + c), bias=pv(l, PV_LNB + c))
            else:
                cx.ts(z.t[:, 0:T], z.t[:, 0:T], pv(l, PV_LNG + c), pv(l, PV_LNB + c), ALU.mult, ALU.add,
                      r=[z.key, pvec.key], w=[z.key])
                sg = tf_ring.next()
                cx.act(sg.t[:, 0:T], z.t[:, 0:T], AF.Sigmoid, r=[z.key], w=[sg.key])
                cx.tt(sbuf_.t[:, c, 0:T], z.t[:, 0:T], sg.t[:, 0:T], ALU.mult, r=[z.key, sg.key], w=[sbuf_.key, VB])
        for g in range(4):
            p_ = ps_ring.next()
            cx.mm(p_.t[:, 0:T], wgb.t[:, g * 128:(g + 1) * 128], pdb.t[:, g, 0:T], True, True,
                  r=[wgb.key, pdb.key], w=[p_.key])
            cx.act(mxb.t[:, g, 0:T], p_.t[:, 0:T], AF.Identity, r=[p_.key, pvec.key, zerob.key], w=[mxb.key, VB],
                   scale=pv(l, PV_PSC + g), bias=zerob.t[:, 0:1])

    def out_and_residual(l, t, st, ws, nk_in, src, woffs, Gs, bg):
        T, j, xt = t.T, t.j, st["xt"]
        pst = ps[7]
        pend_sq = []
        for k in range(KC):
            wb, wo = woffs(k)
            py = proj(wb, wo, nk_in, src, T)
            cx.act(yb.t[:, k, 0:T], py.t[:, 0:T], AF.Identity, r=[py.key, zerob.key], w=[yb.key, VIEW_B], bias=zerob.t[:, 0:1])
            sq = tf_ring.next()
            cx.act(sq.t[:, 0:T], yb.t[:, k, 0:T], AF.Square, r=[yb.key], w=[sq.key])
            if pend_sq:
                pend_sq.pop(0)()
            pend_sq.append((lambda sq=sq, k=k: cx.mm(pst.t[:, 0:T], ones_f.t[:, :], sq.t[:, 0:T], k == 0, k == KC - 1,
                                                    r=[ones_f.key, sq.key], w=[pst.key])))
            bg(4) if nk_in > KC else bg()
        while pend_sq:
            pend_sq.pop(0)()
        rs_ = rstd_from_ps(pst, T, D)
        for kc in range(KC):
            t1 = tf_ring.next()
            cx.stt(t1.t[:, 0:T], yb.t[:, kc, 0:T], col(Gs, kc, j), rs_.t[:, 0:T], ALU.mult, ALU.mult,
                   r=[yb.key, Gs.key, rs_.key], w=[t1.key])
            cx.tt(xt.t[:, kc, 0:T], xt.t[:, kc, 0:T], t1.t[:, 0:T], ALU.add, r=[xt.key, t1.key], w=[xt.key])

    def b_merge(l, t, st):
        drain_casts(1)
        b1_fin(l, t, st)
        cx.sec = 'Bmerge'
        b, T, j = t.b, t.T, t.j
        VB = VIEW_B
        hT = st["hT"]
        ws = WStream("wB", wB_bf, l)
        ws.pos = 512
        st["ws"] = ws
        for k in range(KC):
            wb = ws.next(NBK)
            if k == 0:
                cx.load(OTt.t[:, :, 0:T], OT_d[b].rearrange("(kc p) t -> p kc t", p=128)[:, :, t.c0:t.c0 + T],
                        r=[("OT", b, h, t.id) for h in range(NH)], w=[OTt.key, VB])
            pg = [proj(wb, br * 1024, KC, hT, T) for br in range(3)]
            pya = proj(wb, 3072, 4, sbuf_, T)
            pyb = proj(wb, 3072 + 512, KC, OTt, T)
            pyc = proj(wb, 3072 + 512 + 1024, 4, mxb, T)
            ms = []
            for br, py in enumerate((pya, pyb, pyc)):
                gt = tf_ring.next()
                cx.act(gt.t[:, 0:T], pg[br].t[:, 0:T], AF.Sigmoid, r=[pg[br].key, pvec.key], w=[gt.key],
                       bias=pv(l, PV_BGATE + br * 8 + k))
                m_ = tf_ring.next()
                cx.tt(m_.t[:, 0:T], py.t[:, 0:T], gt.t[:, 0:T], ALU.mult, r=[py.key, gt.key], w=[m_.key])
                ms.append(m_)
            cx.tt(ms[0].t[:, 0:T], ms[0].t[:, 0:T], ms[1].t[:, 0:T], ALU.add, r=[ms[0].key, ms[1].key], w=[ms[0].key],
                  eng="pool")
            cx.tt(mb.t[:, k, 0:T], ms[0].t[:, 0:T], ms[2].t[:, 0:T], ALU.add, r=[ms[0].key, ms[2].key], w=[mb.key, VB],
                  eng="pool")
        cx.sec = 'Bout'
        wo_cache = {}

        def wo_mix(k):
            if k % 4 == 0:
                wo_cache["b"] = ws.next(4096)
            return wo_cache["b"], (k % 4) * 1024

        out_and_residual(l, t, st, ws, KC, mb, wo_mix, G1, lambda m=None: None)

    def b_ffn(l, t, st, bg_):
        def bg(m=None):
            bg_(m)
            cx.sec = 'Bffn'
        cx.sec = 'Bffn'
        b, T, j = t.b, t.T, t.j
        xt, ws = st["xt"], st["ws"]
        h2 = hT_ring.next()
        adaln(xt, T, j, A2, 24, h2)
        for jj in range(NJ):
            if jj % 2 == 0:
                wb = ws.next(4096)
            o_ = (jj % 2) * 2048
            p1 = proj(wb, o_, KC, h2, T)
            p2 = proj(wb, o_ + 1024, KC, h2, T)
            if USE_SILU:
                t1 = tf_ring.next()
                cx.act(t1.t[:, 0:T], p1.t[:, 0:T], AF.Silu, r=[p1.key], w=[t1.key])
            else:
                sg = tf_ring.next()
                cx.act(sg.t[:, 0:T], p1.t[:, 0:T], AF.Sigmoid, r=[p1.key], w=[sg.key])
                t1 = tf_ring.next()
                cx.tt(t1.t[:, 0:T], p1.t[:, 0:T], sg.t[:, 0:T], ALU.mult, r=[p1.key, sg.key], w=[t1.key])
            cx.tt(actb.t[:, jj, 0:T], t1.t[:, 0:T], p2.t[:, 0:T], ALU.mult, r=[t1.key, p2.key], w=[actb.key, VIEW_B])
            bg()

        def wo_ffn(k):
            return ws.next(D_FF), 0

        cx.sec = 'Bffo'

        out_and_residual(l, t, st, ws, NJ, actb, wo_ffn, G2, bg)
        assert ws.pos == NBW, (ws.pos, NBW)
        so = cx.store(xdst(l, t).rearrange("(kc p) t -> p kc t", p=128)[:, :, t.t0:t.t0 + T], xt.t[:, :, 0:T],
                      r=[xt.key], w=[("x", t.kind, b, t.id)])
        if l == L - 1 and t.kind == "l":
            cx.finals.append(so)

    for l in range(L):
        drain_casts(10 ** 9)
        phase_mod(l)
        if l + 1 < L:
            queue_casts(l + 1)
        for b in range(NB):
            tl = tiles_for(b)
            switch_view(VIEW_A)
            sts = [dict() for _ in tl]
            drain(a1_gen(l, tl[0], sts[0]))
            for i, t in enumerate(tl):
                g = a1_gen(l, tl[i + 1], sts[i + 1]) if i + 1 < len(tl) else None
                phase_a2(l, t, sts[i], make_bg(g, 1))
                drain(g)
            attention(l, b)
            if _os.environ.get('KSTOPATT') == '1':
                break
            tlb = [t for t in tl if not (t.kind == "c" and l == L - 1)]
            switch_view(VIEW_B)
            sts = [dict() for _ in tlb]
            drain(b1_gen(l, tlb[0], sts[0]))
            for i, t in enumerate(tlb):
                b_merge(l, t, sts[i])
                g = b1_gen(l, tlb[i + 1], sts[i + 1]) if i + 1 < len(tlb) else None
                b_ffn(l, t, sts[i], make_bg(g, 2))
                drain(g)
        if _os.environ.get('KSTOPATT') == '1':
            break

    cx.finalize()
    with nc.Block() as block:
        cx.emit(block)
    stack.close()
    return nc, cx


def _cc(W, col0):
    K = W.shape[0]
    return np.ascontiguousarray(W[:, col0:col0 + 128].reshape(K // 128, 128, 128).transpose(1, 0, 2)).reshape(128, -1)


def _wide(W, col0, n):
    K = W.shape[0]
    return np.ascontiguousarray(W[:, col0:col0 + n].reshape(K // 128, 128, n).transpose(1, 0, 2)).reshape(128, -1)


def _vec(v):
    return np.ascontiguousarray(v.reshape(-1, 128).T)


def prep_shared(inp, L, S):
    f32 = np.float32
    wA = np.empty((L, 128, NA), f32)
    wB = np.empty((L, 128, NBW), f32)
    wM = np.empty((L, 128, NMW), f32)
    pvec = np.zeros((128, L * PV_L + PV_GLOB), f32)
    for l in range(L):
        w_in = inp["w_in"][l]
        parts = []
        for c in range(4):
            parts += [_cc(w_in, COL_A + c * 128), _cc(w_in, COL_A + CONV_CH + c * 128)]
        parts += [_cc(w_in, COL_Q + h * 128) for h in range(8)]
        parts += [_cc(w_in, COL_K + h * 128) for h in range(8)]
        parts += [_cc(w_in, COL_P + c * 128) for c in range(4)]
        parts += [_wide(w_in, COL_V, 512), _wide(w_in, COL_V + 512, 512)]
        wA[l] = np.concatenate(parts, axis=1)
        parts = [np.ascontiguousarray(inp["w_pool_group"][l].transpose(1, 0, 2)).reshape(128, 512)]
        for k in range(8):
            parts += [_cc(w_in, COL_G + br * 1024 + k * 128) for br in range(3)]
            parts += [_cc(inp["w_conv_out"][l], k * 128), _cc(inp["w_attn_out"][l], k * 128),
                      _cc(inp["w_pool_out"][l], k * 128)]
        parts += [_cc(inp["w_out"][l], k * 128) for k in range(8)]
        for jj in range(NJ):
            parts += [_cc(inp["w_ffn_in"][l], jj * 128), _cc(inp["w_ffn_in"][l], D_FF + jj * 128)]
        parts += [_cc(inp["w_ffn_out"][l], k * 128) for k in range(8)]
        wB[l] = np.concatenate(parts, axis=1)
        wM[l] = np.concatenate([_cc(inp["w_mod"][l], n * 128) for n in range(48)], axis=1)
        o = l * PV_L
        for i, nm in enumerate(("g_pre_mix", "g_post_mix", "g_pre_ffn", "g_post_ffn")):
            pvec[:, o + PV_G + 8 * i: o + PV_G + 8 * i + 8] = _vec(inp[nm][l])
        pvec[:, o + PV_BMOD:o + PV_BMOD + 48] = _vec(inp["b_mod"][l])
        pvec[:, o + PV_BGATE:o + PV_BGATE + 24] = _vec(inp["b_gate"][l])
        cw = inp["conv_w"][l]
        pvec[:, o + PV_CONVW:o + PV_CONVW + 124] = np.ascontiguousarray(
            cw.reshape(CONV_W, 4, 128).transpose(2, 1, 0)).reshape(128, 124)
        pvec[:, o + PV_CONVB:o + PV_CONVB + 4] = _vec(inp["conv_b"][l])
        pvec[:, o + PV_LNG:o + PV_LNG + 4] = _vec(inp["conv_ln_g"][l])
        pvec[:, o + PV_LNB:o + PV_LNB + 4] = _vec(inp["conv_ln_b"][l])
        pvec[:, o + PV_PSC:o + PV_PSC + 4] = _vec(inp["pool_scale"][l])
        pvec[:, o + PV_SUBG] = inp["subln_g"][l]
        for i, nm in enumerate(("lam_q1", "lam_k1", "lam_q2", "lam_k2")):
            pvec[:, o + PV_LAM + 64 * i: o + PV_LAM + 64 * (i + 1)] = inp[nm][l][None, :]
    og = L * PV_L
    for g in range(4):
        w = 2 << g
        half = w // 2
        for jx in range(8):
            cnt_f = min(jx + half, w)
            pvec[:, og + PV_CF + g * 8 + jx] = w / cnt_f
            dist = 8 - jx
            cnt_l = min(dist + half, w)
            pvec[:, og + PV_CL + g * 8 + jx] = w / cnt_l
    tpos = np.arange(S)
    row = (tpos // GRID_W).astype(f32)
    colp = (tpos % GRID_W).astype(f32)
    half = HD // 2
    inv = (np.float32(10000.0) ** (-np.arange(0, half, 2, dtype=f32) / np.float32(half))).astype(f32)
    rope = np.zeros((2, 128, S), f32)
    perm = np.zeros((128, 128), f32)
    for p in range(128):
        d = p % 64
        pos = row if d < 32 else colp
        dd = d % 32
        ang = (pos * inv[dd % 16]).astype(f32)
        rope[0, p] = np.cos(ang)
        if dd < 16:
            rope[1, p] = -np.sin(ang)
            partner = p + 16
        else:
            rope[1, p] = np.sin(ang)
            partner = p - 16
        perm[partner, p] = 1.0
    return dict(wA=wA, wB=wB, wM=wM, pvec=pvec, rope=rope, perm=perm)


def prep_core(inp, bs):
    NB = len(bs)
    xT = np.ascontiguousarray(np.stack([inp["x"][b].T for b in bs]))
    cxT = np.ascontiguousarray(np.stack([inp["ctx"][b].T for b in bs]))
    cv = np.stack([inp["c"][b] for b in bs] + [inp["c_ctx"]])
    cT = np.ascontiguousarray(cv.T.reshape(KC, 128, NB + 1).transpose(1, 0, 2)).reshape(128, KC * (NB + 1))
    return dict(xT=xT, cxT=cxT, cT=cT.astype(np.float32))


_CACHE = {}


def run(inp, n_cores, NB, L=None):
    inp = {k: np.asarray(v) for k, v in inp.items()}
    B, S, _ = inp["x"].shape
    CTX = inp["ctx"].shape[1]
    if L is None:
        L = inp["w_in"].shape[0]
    key = (L, NB, S, CTX)
    if key not in _CACHE:
        _CACHE[key] = build_program(L, NB, S, CTX)
    nc, cx = _CACHE[key]
    shared = prep_shared(inp, L, S)
    in_maps = []
    for i in range(n_cores):
        m = dict(shared)
        m.update(prep_core(inp, list(range(i * NB, (i + 1) * NB))))
        in_maps.append(m)
    res = run_bass_kernel_spmd(nc, in_maps, core_ids=list(range(n_cores)))
    out = np.empty((n_cores * NB, S, D), np.float32)
    for i in range(n_cores):
        o = res.results[i]["outT"]
        for jb in range(NB):
            out[i * NB + jb] = o[jb].T
    return out


def kernel(**inputs):
    return run(inputs, 8, 2)
```
